# Optimizing a Trainium2 kernel written in Bass

```python
import math
import jax, jax.numpy as jnp
from jax import lax
import numpy as np

D_MODEL = 2048
BATCH = 2
SEQ = 8192
DEPTH = 2

GRID_W = 64
CTX_LEN = 256
N_BRANCH = 4
BRANCH_W = 512
BLOCK = 128
ROPE_BASE = 10000.0
EPS = 1e-6
D_FF = 4 * D_MODEL
DA_HEADS = 4
DA_DIM = 64
SSM_HEADS = 8
SSM_HEAD_DIM = 64
SSM_INNER = SSM_HEADS * SSM_HEAD_DIM
SSM_GROUPS = 2
SSM_STATE = 128
SSM_XBC = SSM_INNER + 2 * SSM_GROUPS * SSM_STATE
SSM_CONV = 5
SSM_CHUNK = 128
SW_HEADS = 8
SW_KV_HEADS = 2
SW_DIM = 64
SW_WINDOW = 128
NA_HEADS = 8
NA_DIM = 64
NA_ROWS = 8
NA_COLS = 16
IN_SPLITS = (DA_HEADS * 2 * DA_DIM, DA_HEADS * 2 * DA_DIM, DA_HEADS * 2 * DA_DIM,
             SSM_INNER, SSM_XBC, 2 * SSM_HEADS,
             SW_HEADS * SW_DIM, SW_KV_HEADS * SW_DIM, SW_KV_HEADS * SW_DIM,
             NA_HEADS * NA_DIM, NA_HEADS * NA_DIM, NA_HEADS * NA_DIM,
             N_BRANCH * D_MODEL)
IN_COLS = sum(IN_SPLITS)

kernel_name = "hybrid_diff_ssd_swa_na_dit_block"


def rmsnorm(t, g):
    tf = t.astype(jnp.float32)
    y = tf * lax.rsqrt(jnp.mean(tf * tf, -1, keepdims=True) + EPS)
    return (y * g.astype(jnp.float32)).astype(t.dtype)


def modulate(t, shift, scale):
    return t * (1 + scale) + shift


def split_cols(p):
    idx, acc = [], 0
    for w in IN_SPLITS[:-1]:
        acc += w
        idx.append(acc)
    return jnp.split(p, idx, axis=-1)


def axial_rope(n, d):
    t = jnp.arange(n)
    row = (t // GRID_W).astype(jnp.float32)
    col = (t % GRID_W).astype(jnp.float32)
    nf = d // 4
    inv = ROPE_BASE ** (-jnp.arange(nf, dtype=jnp.float32) / nf)
    ang = jnp.concatenate([row[:, None] * inv, col[:, None] * inv], -1)
    return jnp.cos(ang), jnp.sin(ang)


def apply_rope(t, cos, sin):
    shape = (t.shape[1],) + (1,) * (t.ndim - 3) + (cos.shape[-1],)
    cos = cos.reshape(shape)
    sin = sin.reshape(shape)
    t1, t2 = jnp.split(t.astype(jnp.float32), 2, -1)
    return jnp.concatenate([t1 * cos - t2 * sin, t2 * cos + t1 * sin], -1).astype(t.dtype)


def ctx_attend(q, k, v, sink=None):
    B, L, H, d = q.shape
    KV = k.shape[2]
    G = H // KV
    s = jnp.einsum('bqhgd,bkhd->bhgqk', q.reshape(B, L, KV, G, d), k).astype(jnp.float32) * d ** -0.5
    if sink is not None:
        s_sink = jnp.broadcast_to(sink.astype(jnp.float32).reshape(1, KV, G, 1, 1), s.shape[:-1] + (1,))
        s = jnp.concatenate([s, s_sink], -1)
    p = jax.nn.softmax(s, -1)[..., :k.shape[1]]
    o = jnp.einsum('bhgqk,bkhd->bqhgd', p.astype(v.dtype), v)
    return o.reshape(B, L, H * d)


def diff_core(q, k, v, lam):
    s = jnp.einsum('bqhmd,bkhmd->bhmqk', q, k).astype(jnp.float32) * q.shape[-1] ** -0.5
    p = jax.nn.softmax(s, -1)
    a = p[:, :, 0] - lam * p[:, :, 1]
    return jnp.einsum('bhqk,bkhe->bqhe', a.astype(v.dtype), v)


def diff_attn_latent(q, k_all, v_all, lam):
    B, N = q.shape[:2]
    nb = N // BLOCK
    qb = jnp.moveaxis(q.reshape((B, nb, BLOCK) + q.shape[2:]), 1, 0)
    o = lax.map(lambda qq: diff_core(qq, k_all, v_all, lam), qb)
    return jnp.moveaxis(o, 0, 1).reshape((B, N) + o.shape[3:])


def window_attn_latent(q, k, v, k_ctx, v_ctx, sink):
    B, N, H, d = q.shape
    KV = k.shape[2]
    G = H // KV
    nb = N // BLOCK
    qb = q.reshape(B, nb, BLOCK, KV, G, d)
    pad = ((0, 0), (BLOCK, BLOCK), (0, 0), (0, 0))
    kp = jnp.pad(k, pad).reshape(B, nb + 2, BLOCK, KV, d)
    vp = jnp.pad(v, pad).reshape(B, nb + 2, BLOCK, KV, d)
    k_band = jnp.concatenate([kp[:, :-2], kp[:, 1:-1], kp[:, 2:]], axis=2)
    v_band = jnp.concatenate([vp[:, :-2], vp[:, 1:-1], vp[:, 2:]], axis=2)
    scale = d ** -0.5
    s_win = jnp.einsum('bnqhgd,bnjhd->bnhgqj', qb, k_band).astype(jnp.float32) * scale
    blk = jnp.arange(nb)[:, None]
    q_pos = blk * BLOCK + jnp.arange(BLOCK)[None, :]
    k_pos = (blk - 1) * BLOCK + jnp.arange(3 * BLOCK)[None, :]
    kq = k_pos[:, None, :]
    valid = (jnp.abs(kq - q_pos[:, :, None]) <= SW_WINDOW) & (kq >= 0) & (kq < N)
    s_win = jnp.where(valid[None, :, None, None], s_win, -jnp.inf)
    s_ctx = jnp.einsum('bnqhgd,bjhd->bnhgqj', qb, k_ctx).astype(jnp.float32) * scale
    s_sink = jnp.broadcast_to(sink.astype(jnp.float32).reshape(1, 1, KV, G, 1, 1), s_ctx.shape[:-1] + (1,))
    p = jax.nn.softmax(jnp.concatenate([s_win, s_ctx, s_sink], -1), -1)
    nw = 3 * BLOCK
    L = k_ctx.shape[1]
    o = (jnp.einsum('bnhgqj,bnjhd->bnqhgd', p[..., :nw].astype(v.dtype), v_band)
         + jnp.einsum('bnhgqj,bjhd->bnqhgd', p[..., nw:nw + L].astype(v.dtype), v_ctx))
    return o.reshape(B, N, H * d)


def neighbourhood_attn_latent(q, k, v, k_ctx, v_ctx, rpb):
    B, N, H, d = q.shape
    W = GRID_W
    R = N // W
    KR = min(NA_ROWS, R)
    KC = NA_COLS
    qg = q.reshape(B, R, W, H, d)
    kg = k.reshape(B, R, W, H, d)
    vg = v.reshape(B, R, W, H, d)
    r = jnp.arange(R)
    row_idx = jnp.clip(r - KR // 2, 0, R - KR)[:, None] + jnp.arange(KR)[None, :]
    cc = jnp.arange(W)
    col_idx = jnp.clip(cc - KC // 2, 0, W - KC)[:, None] + jnp.arange(KC)[None, :]
    k_rows = kg[:, row_idx]
    v_rows = vg[:, row_idx]
    scale = d ** -0.5
    s_rows = jnp.einsum('brqhd,brjkhd->brhqjk', qg, k_rows)
    idx = jnp.broadcast_to(col_idx[:, None, :], (B, R, H, W, KR, KC))
    s_nb = jnp.take_along_axis(s_rows, idx, axis=-1).astype(jnp.float32) * scale
    ro = row_idx - r[:, None] + NA_ROWS - 1
    co = col_idx - cc[:, None] + NA_COLS - 1
    bias = rpb[:, ro[:, None, :, None], co[None, :, None, :]]
    s_nb = s_nb + jnp.moveaxis(bias, 0, 1)[None].astype(jnp.float32)
    s_ctx = jnp.einsum('brqhd,bjhd->brhqj', qg, k_ctx).astype(jnp.float32) * scale
    nk = KR * KC
    p = jax.nn.softmax(jnp.concatenate([s_nb.reshape(B, R, H, W, nk), s_ctx], -1), -1)
    p_nb = p[..., :nk].reshape(B, R, H, W, KR, KC).astype(v.dtype)
    onehot = (col_idx[:, :, None] == cc[None, None, :]).astype(v.dtype)
    p_rows = jnp.einsum('brhqjm,qmk->brhqjk', p_nb, onehot)
    o = (jnp.einsum('brhqjk,brjkhd->brqhd', p_rows, v_rows)
         + jnp.einsum('brhqj,bjhd->brqhd', p[..., nk:].astype(v.dtype), v_ctx))
    return o.reshape(B, N, H * d)


def dwconv(t, w, b):
    K = w.shape[0]
    y = lax.conv_general_dilated(t, w[:, None, :].astype(t.dtype), window_strides=(1,),
                                 padding=[(K // 2, K // 2)], dimension_numbers=('NWC', 'WIO', 'NWC'),
                                 feature_group_count=t.shape[-1])
    return y + b.astype(t.dtype)


def ssd_scan(x, dt, a, bm, cm, h0, with_y=True):
    Bn, n, H, P = x.shape
    G, S = bm.shape[2], bm.shape[3]
    R = H // G
    Lc = SSM_CHUNK
    nc = n // Lc
    xg = (x.astype(jnp.float32) * dt[..., None]).reshape(Bn, nc, Lc, G, R, P)
    bc = bm.astype(jnp.float32).reshape(Bn, nc, Lc, G, S)
    cc = cm.astype(jnp.float32).reshape(Bn, nc, Lc, G, S)
    acum = jnp.cumsum((dt * a).reshape(Bn, nc, Lc, G, R), axis=2)
    decay_states = jnp.exp(acum[:, :, -1:] - acum)
    states = jnp.einsum('bclgn,bclgr,bclgrp->bcgrpn', bc, decay_states, xg)
    chunk_decay = jnp.exp(acum[:, :, -1])

    def step(h, inp):
        dec, st = inp
        return h * dec[..., None, None] + st, h

    h_final, h_prev = lax.scan(step, h0, (jnp.moveaxis(chunk_decay, 1, 0), jnp.moveaxis(states, 1, 0)))
    if not with_y:
        return None, h_final
    seg = acum[:, :, :, None] - acum[:, :, None, :]
    tril = (jnp.arange(Lc)[:, None] >= jnp.arange(Lc)[None, :])[None, None, :, :, None, None]
    ldec = jnp.exp(jnp.where(tril, seg, -jnp.inf))
    cb = jnp.einsum('bclgn,bcsgn->bclsg', cc, bc)
    y_diag = jnp.einsum('bclsgr,bcsgrp->bclgrp', cb[..., None] * ldec, xg)
    y_off = jnp.einsum('bclgn,cbgrpn,bclgr->bclgrp', cc, h_prev, jnp.exp(acum))
    y = (y_diag + y_off).reshape(Bn, n, H, P).astype(x.dtype)
    return y, h_final


def ssm_branch(z, xbc, dtr, zc, xbcc, dtrc, conv_w, conv_b, dt_bias, a_log, d_skip, g_ssm, need_ctx):
    A = -jnp.exp(a_log.astype(jnp.float32))

    def prep(xbc_, dtr_):
        Bn, n = xbc_.shape[:2]
        u = jax.nn.silu(dwconv(xbc_, conv_w, conv_b))
        xs, bm, cm = jnp.split(u, [SSM_INNER, SSM_INNER + SSM_GROUPS * SSM_STATE], -1)
        xs = xs.reshape(Bn, n, SSM_HEADS, SSM_HEAD_DIM)
        bm = bm.reshape(Bn, n, SSM_GROUPS, SSM_STATE)
        cm = cm.reshape(Bn, n, SSM_GROUPS, SSM_STATE)
        dt = jax.nn.softplus(dtr_.reshape(Bn, n, 2, SSM_HEADS).astype(jnp.float32) + dt_bias.astype(jnp.float32))
        return xs, bm, cm, dt

    xs, bm, cm, dt = prep(xbc, dtr)
    xsc, bmc, cmc, dtc = prep(xbcc, dtrc)
    h0 = jnp.zeros((xs.shape[0], SSM_GROUPS, SSM_HEADS // SSM_GROUPS, SSM_HEAD_DIM, SSM_STATE), jnp.float32)
    fl = lambda t: jnp.flip(t, 1)
    yc_f, hc_f = ssd_scan(xsc, dtc[:, :, 0], A[0], bmc, cmc, h0, need_ctx)
    y_f, _ = ssd_scan(xs, dt[:, :, 0], A[0], bm, cm, hc_f)
    yc_b, hc_b = ssd_scan(fl(xsc), fl(dtc[:, :, 1]), A[1], fl(bmc), fl(cmc), h0, need_ctx)
    y_b, _ = ssd_scan(fl(xs), fl(dt[:, :, 1]), A[1], fl(bm), fl(cm), hc_b)

    def finish(yf, yb_rev, xs_, z_):
        y = yf + fl(yb_rev) + d_skip[:, None].astype(xs_.dtype) * xs_
        return rmsnorm(y.reshape(z_.shape) * jax.nn.silu(z_), g_ssm)

    y = finish(y_f, y_b, xs, z)
    if not need_ctx:
        return y, None
    return y, finish(yc_f, yc_b, xsc, zc)


def merge(ys, gate, w_branch, w_out):
    D = w_out.shape[0]
    g = jax.nn.sigmoid(gate).reshape(gate.shape[:-1] + (N_BRANCH, D))
    m = g[..., 0, :] * (ys[0] @ w_branch[0])
    for i in range(1, N_BRANCH):
        m = m + g[..., i, :] * (ys[i] @ w_branch[i])
    return m @ w_out


def mixer(h, hc, cos, sin, lam_init, need_ctx, w_in, lam_q1, lam_k1, lam_q2, lam_k2, g_subln,
          conv_w, conv_b, dt_bias, a_log, d_skip, g_ssm, sink, rpb, w_branch, w_out):
    B, N, _ = h.shape
    L = hc.shape[1]
    aq, ak, av, bz, bx, bdt, sq, sk, sv, nq, nk, nv, gate = split_cols(h @ w_in)
    aqc, akc, avc, bzc, bxc, bdtc, sqc, skc, svc, nqc, nkc, nvc, gatec = split_cols(hc @ w_in)
    lam = (jnp.exp(jnp.sum(lam_q1.astype(jnp.float32) * lam_k1.astype(jnp.float32)))
           - jnp.exp(jnp.sum(lam_q2.astype(jnp.float32) * lam_k2.astype(jnp.float32))) + lam_init)
    qa = apply_rope(aq.reshape(B, N, DA_HEADS, 2, DA_DIM), cos, sin)
    ka = apply_rope(ak.reshape(B, N, DA_HEADS, 2, DA_DIM), cos, sin)
    va = av.reshape(B, N, DA_HEADS, 2 * DA_DIM)
    kac = akc.reshape(B, L, DA_HEADS, 2, DA_DIM)
    vac = avc.reshape(B, L, DA_HEADS, 2 * DA_DIM)
    oa = diff_attn_latent(qa, jnp.concatenate([ka, kac], 1), jnp.concatenate([va, vac], 1), lam)
    ya = (rmsnorm(oa, g_subln) * (1.0 - lam_init)).reshape(B, N, BRANCH_W)
    yb, yb_c = ssm_branch(bz, bx, bdt, bzc, bxc, bdtc, conv_w, conv_b, dt_bias, a_log, d_skip, g_ssm, need_ctx)
    qs = apply_rope(sq.reshape(B, N, SW_HEADS, SW_DIM), cos, sin)
    ks = apply_rope(sk.reshape(B, N, SW_KV_HEADS, SW_DIM), cos, sin)
    vs = sv.reshape(B, N, SW_KV_HEADS, SW_DIM)
    ksc = skc.reshape(B, L, SW_KV_HEADS, SW_DIM)
    vsc = svc.reshape(B, L, SW_KV_HEADS, SW_DIM)
    ys = window_attn_latent(qs, ks, vs, ksc, vsc, sink)
    knc = nkc.reshape(B, L, NA_HEADS, NA_DIM)
    vnc = nvc.reshape(B, L, NA_HEADS, NA_DIM)
    yn = neighbourhood_attn_latent(nq.reshape(B, N, NA_HEADS, NA_DIM), nk.reshape(B, N, NA_HEADS, NA_DIM),
                                   nv.reshape(B, N, NA_HEADS, NA_DIM), knc, vnc, rpb)
    y = merge([ya, yb, ys, yn], gate, w_branch, w_out)
    if not need_ctx:
        return y, None
    ya_c = (rmsnorm(diff_core(aqc.reshape(B, L, DA_HEADS, 2, DA_DIM), kac, vac, lam), g_subln)
            * (1.0 - lam_init)).reshape(B, L, BRANCH_W)
    ys_c = ctx_attend(sqc.reshape(B, L, SW_HEADS, SW_DIM), ksc, vsc, sink)
    yn_c = ctx_attend(nqc.reshape(B, L, NA_HEADS, NA_DIM), knc, vnc)
    y_c = merge([ya_c, yb_c, ys_c, yn_c], gatec, w_branch, w_out)
    return y, y_c


def ffn(t, w1, w2):
    return jnp.square(jax.nn.relu(t @ w1)) @ w2


def setup_inputs(seed: int = 0) -> dict:
    key = jax.random.key(seed)
    ks = iter(jax.random.split(key, 40))
    nrm = lambda shape, s: jax.random.normal(next(ks), shape, jnp.float32) * s
    Lr, D = DEPTH, D_MODEL
    dt0 = jnp.exp(jax.random.uniform(next(ks), (Lr, 2, SSM_HEADS), jnp.float32)
                  * (math.log(0.1) - math.log(0.001)) + math.log(0.001))
    return {
        'x': nrm((BATCH, SEQ, D), 1.0),
        'c': nrm((BATCH, D), 1.0),
        'ctx': nrm((BATCH, CTX_LEN, D), 1.0),
        'c_ctx': nrm((D,), 1.0),
        'w_mod': nrm((Lr, D, 6 * D), 0.5 * D ** -0.5),
        'b_mod': nrm((Lr, 6 * D), 0.02),
        'g_pre_mix': 1.0 + nrm((Lr, D), 0.05),
        'g_post_mix': 1.0 + nrm((Lr, D), 0.05),
        'g_pre_mlp': 1.0 + nrm((Lr, D), 0.05),
        'g_post_mlp': 1.0 + nrm((Lr, D), 0.05),
        'w_in': nrm((Lr, D, IN_COLS), D ** -0.5),
        'lam_q1': nrm((Lr, DA_DIM), 0.1),
        'lam_k1': nrm((Lr, DA_DIM), 0.1),
        'lam_q2': nrm((Lr, DA_DIM), 0.1),
        'lam_k2': nrm((Lr, DA_DIM), 0.1),
        'g_subln': 1.0 + nrm((Lr, 2 * DA_DIM), 0.05),
        'conv_w': nrm((Lr, SSM_CONV, SSM_XBC), SSM_CONV ** -0.5),
        'conv_b': nrm((Lr, SSM_XBC), 0.02),
        'dt_bias': dt0 + jnp.log(-jnp.expm1(-dt0)),
        'a_log': jnp.log(jax.random.uniform(next(ks), (Lr, 2, SSM_HEADS), jnp.float32, 1.0, 16.0)),
        'd_skip': 1.0 + nrm((Lr, SSM_HEADS), 0.1),
        'g_ssm': 1.0 + nrm((Lr, SSM_INNER), 0.05),
        'sink': nrm((Lr, SW_HEADS), 0.5),
        'rpb': nrm((Lr, NA_HEADS, 2 * NA_ROWS - 1, 2 * NA_COLS - 1), 0.1),
        'w_branch': nrm((Lr, N_BRANCH, BRANCH_W, D), BRANCH_W ** -0.5),
        'w_out': nrm((Lr, D, D), D ** -0.5),
        'w_ff1': nrm((Lr, D, D_FF), D ** -0.5),
        'w_ff2': nrm((Lr, D_FF, D), D_FF ** -0.5),
    }


def reference(x, c, ctx, c_ctx, w_mod, b_mod, g_pre_mix, g_post_mix, g_pre_mlp, g_post_mlp, w_in,
              lam_q1, lam_k1, lam_q2, lam_k2, g_subln, conv_w, conv_b, dt_bias, a_log, d_skip, g_ssm,
              sink, rpb, w_branch, w_out, w_ff1, w_ff2):
    B, N, D = x.shape
    cos, sin = axial_rope(N, DA_DIM)
    cx = ctx
    for l in range(DEPTH):
        need_ctx = l < DEPTH - 1
        lam_init = 0.8 - 0.6 * math.exp(-0.3 * l)
        mod = (jax.nn.silu(c) @ w_mod[l] + b_mod[l]).reshape(B, 6, 1, D)
        modc = (jax.nn.silu(c_ctx) @ w_mod[l] + b_mod[l]).reshape(6, D)
        h = modulate(rmsnorm(x, g_pre_mix[l]), mod[:, 0], mod[:, 1])
        hc = modulate(rmsnorm(cx, g_pre_mix[l]), modc[0], modc[1])
        y, y_c = mixer(h, hc, cos, sin, lam_init, need_ctx, w_in[l], lam_q1[l], lam_k1[l], lam_q2[l], lam_k2[l],
                       g_subln[l], conv_w[l], conv_b[l], dt_bias[l], a_log[l], d_skip[l], g_ssm[l], sink[l],
                       rpb[l], w_branch[l], w_out[l])
        x = x + mod[:, 2] * rmsnorm(y, g_post_mix[l])
        h = modulate(rmsnorm(x, g_pre_mlp[l]), mod[:, 3], mod[:, 4])
        x = x + mod[:, 5] * rmsnorm(ffn(h, w_ff1[l], w_ff2[l]), g_post_mlp[l])
        if need_ctx:
            cx = cx + modc[2] * rmsnorm(y_c, g_post_mix[l])
            hc = modulate(rmsnorm(cx, g_pre_mlp[l]), modc[3], modc[4])
            cx = cx + modc[5] * rmsnorm(ffn(hc, w_ff1[l], w_ff2[l]), g_post_mlp[l])
    return x
```

```python
import contextlib
import math
import numpy as np
import ml_dtypes
import concourse.bass as bass
import concourse.mybir as mybir
from concourse.bass_utils import run_bass_kernel_spmd

F32 = mybir.dt.float32
BF16 = mybir.dt.bfloat16
AF = mybir.ActivationFunctionType
ALU = mybir.AluOpType
NPBF = ml_dtypes.bfloat16

D = 2048
NLAT = 8192
NCTX = 256
NTOK = NLAT + NCTX
DEPTH = 2
GRID_W = 64
EPS = 1e-6
D_FF = 8192
IN_SPLITS = (512, 512, 512, 512, 1024, 16, 512, 128, 128, 512, 512, 512, 8192)
IN_OFF = np.concatenate([[0], np.cumsum(IN_SPLITS)]).astype(int)
GATE_OFF = int(IN_OFF[12])

SEM_EPOCH = 30000
DMA_POOL = 12


class Res:
    __slots__ = ("name", "lw", "rd", "ps")

    def __init__(self, name, ps=False):
        self.name = name
        self.lw = None
        self.rd = []
        self.ps = ps


class Sched:
    ENGS = ("pe", "act", "dve", "pool", "sp")

    def __init__(self, nc, stack):
        self.nc = nc
        self.stack = stack
        self.ops = {e: [] for e in self.ENGS}
        self.cnt = {e: 0 for e in self.ENGS}
        self.sem = {}
        self.semcnt = {e: 0 for e in self.ENGS}
        self.waited = {e: {} for e in self.ENGS}
        self.dma_pool = {}
        self.dma_i = {e: 0 for e in self.ENGS}
        self.nsem = 0
        self.finals = []
        self.allsems = {}
        self.eng = {"pe": nc.tensor, "act": nc.scalar, "dve": nc.vector, "pool": nc.gpsimd, "sp": nc.sync}
        for e in ("pe", "act", "dve", "pool"):
            self._new_sem(e)
        for q in ("sp", "pool"):
            self.dma_pool[q] = [[self._alloc_sem(f"d{q}{i}"), 0] for i in range(DMA_POOL)]
        self.dma_pool["cc"] = [[self._alloc_sem(f"dcc{i}"), 0] for i in range(DMA_POOL)]
        self.cc_ids = {id(sl[0]) for sl in self.dma_pool["cc"]}
        self.dma_i["cc"] = 0

    def _alloc_sem(self, name):
        self.nsem += 1
        s = self.stack.enter_context(self.nc.semaphore(f"{name}_{self.nsem}"))
        self.allsems[id(s)] = [s, 0]
        return s

    def _new_sem(self, e):
        self.sem[e] = self._alloc_sem(f"s{e}")
        self.semcnt[e] = 0

    def _need(self, eng, ev, waits, idx_now):
        if ev is None:
            return
        sem, val, peng, pidx, isdma = ev
        if not isdma and peng == eng:
            if eng == "pe":
                return
            if pidx < idx_now - 2:
                return
        w = self.waited[eng]
        k = id(sem)
        if w.get(k, -1) >= val:
            return
        w[k] = val
        waits.append((sem, val))

    def op(self, eng, fn, reads=(), writes=(), dma=False, final=False, dma_inc=16, pool=None):
        waits = []
        idx_now = self.cnt[eng]
        for r in reads:
            self._need(eng, r.lw, waits, idx_now)
            if r.ps:
                for ev in r.rd:
                    if ev[2] != eng:
                        self._need(eng, ev, waits, idx_now)
        for wr in writes:
            self._need(eng, wr.lw, waits, idx_now)
            for ev in wr.rd:
                self._need(eng, ev, waits, idx_now)
        if dma:
            pname = pool or eng
            slot = self.dma_pool[pname][self.dma_i[pname] % DMA_POOL]
            self.dma_i[pname] += 1
            sem = slot[0]
            if slot[1] > 0:
                self._need(eng, (sem, slot[1], eng, -1, True), waits, idx_now)
            slot[1] += dma_inc
            ev = (sem, slot[1], eng, -1, True)
            inc = dma_inc
        else:
            if self.semcnt[eng] >= SEM_EPOCH:
                self._new_sem(eng)
            self.semcnt[eng] += 1
            ev = (self.sem[eng], self.semcnt[eng], eng, idx_now, False)
            inc = 1
        self.allsems[id(ev[0])][1] = ev[1]
        self.cnt[eng] += 1
        e = self.eng[eng]
        for (s_, v_) in waits:
            e.wait_ge(s_, v_)
        fn(e).then_inc(ev[0], inc)
        for r in reads:
            if not dma:
                r.rd = [x for x in r.rd if x[4] or x[2] != eng]
            r.rd.append(ev)
        for wr in writes:
            wr.lw = ev
            wr.rd = []
        if final:
            self.finals.append(ev)
        return ev

    def barrier(self, include_cc=True):
        snap = [(s, v) for (s, v) in self.allsems.values() if v > 0 and (include_cc or id(s) not in self.cc_ids)]
        for e in self.ENGS:
            waits = []
            w = self.waited[e]
            for (s, v) in snap:
                if w.get(id(s), -1) < v:
                    w[id(s)] = v
                    waits.append((s, v))
            for (s_, v_) in waits:
                self.eng[e].wait_ge(s_, v_)

    def emit(self):
        seen = {}
        for ev in self.finals:
            k = id(ev[0])
            if k not in seen or seen[k][1] < ev[1]:
                seen[k] = (ev[0], ev[1])
        for (s_, v_) in seen.values():
            self.nc.sync.wait_ge(s_, v_)


class KB:
    def __init__(self):
        self.nc = bass.Bass("TRN2", target_bir_lowering=False)
        self.stack = contextlib.ExitStack()
        self.S = Sched(self.nc, self.stack)
        self.uid = 0

    def din(self, name, shape, dt=F32):
        return self.nc.dram_tensor(name, list(shape), dt, kind="ExternalInput").ap()

    def dout(self, name, shape, dt=F32):
        return self.nc.dram_tensor(name, list(shape), dt, kind="ExternalOutput").ap()

    def dscratch(self, name, shape, dt=F32):
        return self.nc.dram_tensor(name, list(shape), dt, kind="Internal").ap()

    def sb(self, name, shape, dt=F32, stack=None):
        self.uid += 1
        t = (stack or self.stack).enter_context(self.nc.sbuf_tensor(f"{name}_{self.uid}", list(shape), dt))
        return t, Res(name)

    def ps(self, name, shape, dt=F32, stack=None):
        self.uid += 1
        t = (stack or self.stack).enter_context(self.nc.psum_tensor(f"{name}_{self.uid}", list(shape), dt))
        return t, Res(name, ps=True)

    def ring(self, name, n, shape, dt=F32, psum=False, stack=None):
        f = self.ps if psum else self.sb
        return Ring([f(f"{name}{i}", shape, dt, stack=stack) for i in range(n)])

    def finish(self):
        self.S.emit()
        self.stack.close()
        return self.nc


class Ring:
    def __init__(self, items):
        self.items = items
        self.i = 0

    def next(self):
        it = self.items[self.i % len(self.items)]
        self.i += 1
        return it


def run_spmd(nc, in_maps):
    res = run_bass_kernel_spmd(nc, in_maps, core_ids=list(range(len(in_maps))))
    return res.results


def dma(S, q, out, in_, reads=(), writes=(), final=False):
    return S.op(q, lambda e: e.dma_start(out=out, in_=in_), reads=reads, writes=writes, dma=True, final=final)


def emit_rstd(S, kb, ss, ss_r, n, tmp=None):
    S.op("dve", lambda e: e.tensor_scalar(out=ss, in0=ss, scalar1=1.0 / n, scalar2=EPS, op0=ALU.mult, op1=ALU.add),
         reads=[ss_r], writes=[ss_r])
    S.op("act", lambda e: e.activation(out=ss, in_=ss, func=AF.Sqrt), reads=[ss_r], writes=[ss_r])
    S.op("dve", lambda e: e.reciprocal(out=ss, in_=ss), reads=[ss_r], writes=[ss_r])


def build_mod():
    kb = KB()
    S = kb.S
    NCOL = 12288 // 8
    cT = kb.din("cT", [128, 16, 3])
    wm = kb.din("wm", [DEPTH, D, NCOL])
    bm = kb.din("bm", [DEPTH, NCOL])
    out = kb.dout("mod", [DEPTH, 3, NCOL])
    ct, ct_r = kb.sb("ct", [128, 16, 3])
    st, st_r = kb.sb("st", [128, 16, 3])
    dma(S, "sp", ct[:], cT, writes=[ct_r])
    S.op("act", lambda e: e.activation(out=st[:], in_=ct[:], func=AF.Silu), reads=[ct_r], writes=[st_r])
    wring = kb.ring("w", 2, [128, 16, 512])
    pring = kb.ring("p", 2, [3, 512], psum=True)
    bt, bt_r = kb.sb("bt", [3, DEPTH, NCOL])
    ot, ot_r = kb.sb("ot", [3, DEPTH, NCOL])
    for l in range(DEPTH):
        dma(S, "sp", bt[:, l, :], bm[l, :].partition_broadcast(3), writes=[bt_r])
    for l in range(DEPTH):
        wv = wm[l].rearrange("(kc p) c -> p kc c", p=128)
        for nt in range(NCOL // 512):
            (w, w_r) = wring.next()
            (p, p_r) = pring.next()
            dma(S, "sp", w[:], wv[:, :, nt * 512:(nt + 1) * 512], writes=[w_r])
            for kc in range(16):
                S.op("pe", lambda e, w=w, p=p, kc=kc: e.matmul(p[:], lhsT=st[:, kc, :], rhs=w[:, kc, :], start=(kc == 0), stop=(kc == 15)),
                     reads=[st_r, w_r], writes=[p_r])
            S.op("dve", lambda e, p=p, l=l, nt=nt: e.tensor_tensor(out=ot[:, l, nt * 512:(nt + 1) * 512], in0=p[:], in1=bt[:, l, nt * 512:(nt + 1) * 512], op=ALU.add),
                 reads=[p_r, bt_r], writes=[ot_r])
    dma(S, "sp", out.rearrange("l j c -> j l c"), ot[:], reads=[ot_r], final=True)
    return kb.finish()


def run_mod(inp):
    cvec = np.stack([inp["c"][0], inp["c"][1], inp["c_ctx"]], 0)
    cT = np.ascontiguousarray(cvec.T.reshape(16, 128, 3).transpose(1, 0, 2))
    NCOL = 12288 // 8
    nc = build_mod()
    maps = []
    for core in range(8):
        sl = slice(core * NCOL, (core + 1) * NCOL)
        maps.append({"cT": cT, "wm": np.ascontiguousarray(inp["w_mod"][:, :, sl]), "bm": np.ascontiguousarray(inp["b_mod"][:, sl])})
    res = run_spmd(nc, maps)
    return np.concatenate([r["mod"] for r in res], axis=2)


TROWS = 2048 + 64


def build_p1():
    kb = KB()
    S = kb.S
    x = kb.din("x", [TROWS, D])
    vec = kb.din("vec", [5, D])
    h = kb.dout("h", [TROWS, D], BF16)
    vb, vb_r = kb.sb("vb", [128, 5, D])
    for i in range(5):
        dma(S, "sp", vb[:, i, :], vec[i, :].partition_broadcast(128), writes=[vb_r])
    gs, gs_r = kb.sb("gs", [128, 2, D])
    for i in range(2):
        S.op("dve", lambda e, i=i: e.scalar_tensor_tensor(out=gs[:, i, :], in0=vb[:, 2 + 2 * i, :], scalar=1.0, in1=vb[:, 0, :], op0=ALU.add, op1=ALU.mult),
             reads=[vb_r], writes=[gs_r])
    xr = kb.ring("x", 3, [128, D])
    sq, sq_r = kb.sb("sq", [128, D])
    tr = kb.ring("t", 2, [128, D])
    hr = kb.ring("h", 2, [128, D], BF16)
    ssr = kb.ring("ss", 3, [128, 1])
    tiles = [(i * 128, 128, 0) for i in range(16)] + [(2048, 64, 1)]
    for (r0, n, isc) in tiles:
        xt, xt_r = xr.next()
        ss, ss_r = ssr.next()
        t, t_r = tr.next()
        ht, ht_r = hr.next()
        dma(S, "sp", xt[:n, :], x[r0:r0 + n, :], writes=[xt_r])
        S.op("pool", lambda e, ss=ss: e.memset(ss[:], 0.0), writes=[ss_r])
        S.op("act", lambda e, xt=xt, ss=ss, n=n: e.activation(out=sq[:n, :], in_=xt[:n, :], func=AF.Square, accum_out=ss[:n, :]),
             reads=[xt_r], writes=[sq_r, ss_r])
        emit_rstd(S, kb, ss[:n, :], ss_r, D)
        S.op("dve", lambda e, t=t, xt=xt, ss=ss, n=n, isc=isc: e.scalar_tensor_tensor(out=t[:n, :], in0=xt[:n, :], scalar=ss[:n, 0:1], in1=gs[:n, isc, :], op0=ALU.mult, op1=ALU.mult),
             reads=[xt_r, ss_r, gs_r], writes=[t_r])
        S.op("pool", lambda e, t=t, ht=ht, n=n, isc=isc: e.tensor_tensor(out=ht[:n, :], in0=t[:n, :], in1=vb[:n, 1 + 2 * isc, :], op=ALU.add),
             reads=[t_r, vb_r], writes=[ht_r])
        dma(S, "sp", h[r0:r0 + n, :], ht[:n, :], reads=[ht_r], final=True)
    return kb.finish()


def token_shard(xfull, cfull, b, q):
    return np.concatenate([xfull[b, q * 2048:(q + 1) * 2048], cfull[b, q * 64:(q + 1) * 64]], 0)


def run_p1(xcur, cxcur, mod_l, gvec, shift_i, scale_i):
    nc = build_p1()
    maps = []
    for core in range(8):
        b, q = divmod(core, 4)
        vec = np.stack([gvec, mod_l[b, shift_i], mod_l[b, scale_i], mod_l[2, shift_i], mod_l[2, scale_i]], 0)
        maps.append({"x": token_shard(xcur, cxcur, b, q), "vec": np.ascontiguousarray(vec)})
    res = run_spmd(nc, maps)
    hl = np.zeros((2, NLAT, D), NPBF)
    hc = np.zeros((2, NCTX, D), NPBF)
    for core in range(8):
        b, q = divmod(core, 4)
        hh = res[core]["h"]
        hl[b, q * 2048:(q + 1) * 2048] = hh[:2048]
        hc[b, q * 64:(q + 1) * 64] = hh[2048:]
    return hl, hc


NFM = 13
NTM = 452


def x_weight_cols(j):
    o = IN_OFF
    kv = j // 2
    g = j // 2

    def swap64(cols):
        cols = np.asarray(cols).reshape(-1, 64)
        return np.concatenate([cols[:, 32:], cols[:, :32]], 1).reshape(-1)
    r = np.arange
    aq = o[0] + j * 128 + r(128)
    ak = o[1] + j * 128 + r(128)
    cq = o[6] + 2 * j * 64 + r(128)
    ck = np.concatenate([o[7] + kv * 64 + r(64)] * 2)
    dq = o[9] + 2 * j * 64 + r(128)
    dk = o[10] + 2 * j * 64 + r(128)
    bx = o[4] + 2 * j * 64 + r(128)
    bb = o[4] + 512 + g * 128 + r(128)
    bc = o[4] + 768 + g * 128 + r(128)
    fm = np.concatenate([aq, swap64(aq), ak, swap64(ak), cq, swap64(cq), ck, swap64(ck), dq, dk, bx, bb, bc])
    av = o[2] + j * 128 + r(128)
    cv = o[8] + kv * 64 + r(64)
    dv = o[11] + 2 * j * 64 + r(128)
    bz = o[3] + 2 * j * 64 + r(128)
    bdt = o[5] + np.array([2 * j, 2 * j + 1, 8 + 2 * j, 8 + 2 * j + 1])
    tm = np.concatenate([av, cv, dv, bz, bdt])
    return fm.astype(int), tm.astype(int)


def rope_tables():
    t = np.arange(NLAT)
    row = (t // GRID_W).astype(np.float32)
    col = (t % GRID_W).astype(np.float32)
    nf = 16
    inv = (10000.0 ** (-np.arange(nf, dtype=np.float32) / nf)).astype(np.float32)
    ang = np.concatenate([row[:, None] * inv, col[:, None] * inv], -1)
    cos = np.cos(ang).T.astype(np.float32)
    sin = np.sin(ang).T.astype(np.float32)
    cosT = np.concatenate([cos, cos, cos, cos], 0)
    sinT = np.concatenate([-sin, sin, -sin, sin], 0)
    return np.ascontiguousarray(cosT), np.ascontiguousarray(sinT)


class XScratch:
    pass


P2_DEBUG = {}


def emit_p2(kb, hT, wfm, wtm, cosT, sinT, sc, after_wload=None):
    S = kb.S
    st = contextlib.ExitStack()
    W, W_r = kb.sb("W", [128, 16, NFM * 128 + NTM], BF16, stack=st)
    wv = wfm.rearrange("(kc p) c -> p kc c", p=128)
    wv2 = wtm.rearrange("(kc p) c -> p kc c", p=128)
    for kc in range(16):
        dma(S, "pool", W[:, kc, 0:NFM * 128], wv[:, kc, :], writes=[W_r])
        dma(S, "pool", W[:, kc, NFM * 128:], wv2[:, kc, :], writes=[W_r])
    if after_wload is not None:
        after_wload()
    hr = kb.ring("hT", 2, [128, 16, 512], BF16, stack=st)
    cr = kb.ring("cs", 2, [128, 2, 512], F32, stack=st)
    pr = kb.ring("pp", 6, [128, 512], F32, psum=True, stack=st)
    t1r = kb.ring("t1", 2, [128, 512], F32, stack=st)
    t2r = kb.ring("t2", 2, [128, 512], F32, stack=st)
    obr = kb.ring("ob", 3, [128, 512], BF16, stack=st)
    ofr = kb.ring("of", 3, [128, 512], F32, stack=st)
    tvr = kb.ring("tv", 2, [128, 320], BF16, stack=st)
    tzr = kb.ring("tz", 2, [128, 132], F32, stack=st)
    tiles = [(i * 512, 512, False) for i in range(16)] + [(NLAT, 256, True)]
    roped = [(0, sc.A_qT, 128), (2, sc.A_kT, 128), (4, sc.C_qT, 128), (6, sc.C_kT, 64)]
    plain = [(8, sc.D_qT), (9, sc.D_kT)]
    def load(ti):
        (t0, n, isctx) = tiles[ti]
        h, h_r = hr.next()
        if callable(hT):
            hT(h, h_r, t0, n, isctx)
        else:
            dma(S, "sp", h[:, :, :n], hT[:, :, t0:t0 + n], writes=[h_r])
        cs, cs_r = cr.next()
        if not isctx:
            dma(S, "sp", cs[:, 0, :n], cosT[:, t0:t0 + n], writes=[cs_r])
            dma(S, "sp", cs[:, 1, :n], sinT[:, t0:t0 + n], writes=[cs_r])
        return h, h_r, cs, cs_r
    nxt = load(0)
    for ti, (t0, n, isctx) in enumerate(tiles):
        h, h_r, cs, cs_r = nxt
        if ti + 1 < len(tiles):
            nxt = load(ti + 1)

        def proj(g):
            p, p_r = pr.next()
            for k in range(16):
                S.op("pe", lambda e, p=p, k=k, g=g: e.matmul(p[:, :n], lhsT=W[:, k, g * 128:(g + 1) * 128], rhs=h[:, k, :n], start=(k == 0), stop=(k == 15)),
                     reads=[W_r, h_r], writes=[p_r])
            return p, p_r
        if P2_DEBUG.get("stop") == 1:
            continue
        for (g, dst, rows) in (roped if P2_DEBUG.get("stop") != 3 else []):
            pa, pa_r = proj(g)
            ob, ob_r = obr.next()
            if isctx:
                S.op("act", lambda e, pa=pa, ob=ob: e.copy(out=ob[:, :n], in_=pa[:, :n]), reads=[pa_r], writes=[ob_r])
            else:
                pb, pb_r = proj(g + 1)
                t1, t1_r = t1r.next()
                t2, t2_r = t2r.next()
                S.op("dve", lambda e, pa=pa, t1=t1: e.tensor_tensor(out=t1[:, :n], in0=pa[:, :n], in1=cs[:, 0, :n], op=ALU.mult), reads=[pa_r, cs_r], writes=[t1_r])
                S.op("dve", lambda e, pb=pb, t2=t2: e.tensor_tensor(out=t2[:, :n], in0=pb[:, :n], in1=cs[:, 1, :n], op=ALU.mult), reads=[pb_r, cs_r], writes=[t2_r])
                S.op("pool", lambda e, t1=t1, t2=t2, ob=ob: e.tensor_tensor(out=ob[:, :n], in0=t1[:, :n], in1=t2[:, :n], op=ALU.add), reads=[t1_r, t2_r], writes=[ob_r])
            dma(S, "sp", dst[:rows, t0:t0 + n], ob[:rows, :n], reads=[ob_r])
        if P2_DEBUG.get("stop") == 2:
            continue
        for (g, dst) in (plain if P2_DEBUG.get("stop") != 3 else []):
            pa, pa_r = proj(g)
            ob, ob_r = obr.next()
            S.op("act", lambda e, pa=pa, ob=ob: e.copy(out=ob[:, :n], in_=pa[:, :n]), reads=[pa_r], writes=[ob_r])
            dma(S, "sp", dst[:, t0:t0 + n], ob[:, :n], reads=[ob_r])
        for i in range(3 if P2_DEBUG.get("stop") != 3 else 0):
            pa, pa_r = proj(10 + i)
            of, of_r = ofr.next()
            S.op("act", lambda e, pa=pa, of=of: e.copy(out=of[:, :n], in_=pa[:, :n]), reads=[pa_r], writes=[of_r])
            dma(S, "sp", sc.B_xbcT[i, :, t0:t0 + n], of[:, :n], reads=[of_r])
        if P2_DEBUG.get("stop") == 4:
            continue
        for s in range(n // 128):
            p, p_r = pr.next()
            for k in range(16):
                S.op("pe", lambda e, p=p, k=k, s=s: e.matmul(p[:, :NTM], lhsT=h[:, k, s * 128:(s + 1) * 128], rhs=W[:, k, NFM * 128:], start=(k == 0), stop=(k == 15)),
                     reads=[W_r, h_r], writes=[p_r])
            tv, tv_r = tvr.next()
            tz, tz_r = tzr.next()
            S.op("act", lambda e, p=p, tv=tv: e.copy(out=tv[:, :], in_=p[:, 0:320]), reads=[p_r], writes=[tv_r])
            S.op("act", lambda e, p=p, tz=tz: e.copy(out=tz[:, :], in_=p[:, 320:452]), reads=[p_r], writes=[tz_r])
            r0 = t0 + s * 128
            dma(S, "sp", sc.TMV[r0:r0 + 128, :], tv[:, :], reads=[tv_r])
            dma(S, "sp", sc.TMZ[r0:r0 + 128, :], tz[:, :], reads=[tz_r])
    S.barrier()
    st.close()


def make_xscratch(kb, as_output=False):
    sc = XScratch()
    mk = kb.dout if as_output else kb.dscratch
    sc.A_qT = mk("A_qT", [128, NTOK], BF16)
    sc.A_kT = mk("A_kT", [128, NTOK], BF16)
    sc.C_qT = mk("C_qT", [128, NTOK], BF16)
    sc.C_kT = mk("C_kT", [64, NTOK], BF16)
    sc.D_qT = mk("D_qT", [128, NTOK], BF16)
    sc.D_kT = mk("D_kT", [128, NTOK], BF16)
    sc.TMV = mk("TMV", [NTOK, 320], BF16)
    sc.TMZ = mk("TMZ", [NTOK, 132], F32)
    sc.A_v = sc.TMV[:, 0:128]
    sc.C_v = sc.TMV[:, 128:192]
    sc.D_v = sc.TMV[:, 192:320]
    sc.B_z = sc.TMZ[:, 0:128]
    sc.B_dt = sc.TMZ[:, 128:132]
    sc.B_xbcT = mk("B_xbcT", [3, 128, NTOK], F32)
    sc.r_fm = Res("sc_fm")
    sc.r_tm = Res("sc_tm")
    return sc


def emit_mixA(kb, sc, lamv, gsub, lam_init, need_ctx, yA):
    S = kb.S
    st = contextlib.ExitStack()
    QT, QT_r = kb.sb("QT", [64, 2, NTOK], BF16, stack=st)
    KT, KT_r = kb.sb("KT", [64, 2, NTOK], BF16, stack=st)
    V, V_r = kb.sb("V", [128, 66, 129], BF16, stack=st)
    for m in range(2):
        dma(S, "sp", QT[:, m, :], sc.A_qT[m * 64:(m + 1) * 64, :], writes=[QT_r])
        dma(S, "sp", KT[:, m, :], sc.A_kT[m * 64:(m + 1) * 64, :], writes=[KT_r])
    S.op("pool", lambda e: e.memset(V[:, :, 128:129], 1.0), writes=[V_r])
    avv = sc.A_v.rearrange("(t p) c -> p t c", p=128)
    for t in range(0, 66, 6):
        dma(S, "sp", V[:, t:t + 6, 0:128], avv[:, t:t + 6, :], writes=[V_r])
    lv, lv_r = kb.sb("lv", [128, 4, 64], stack=st)
    for i in range(4):
        dma(S, "sp", lv[:, i, :], lamv[i, :].partition_broadcast(128), writes=[lv_r])
    lp, lp_r = kb.sb("lp", [128, 2, 64], stack=st)
    ls, ls_r = kb.sb("ls", [128, 2], stack=st)
    nlam, nlam_r = kb.sb("nlam", [128, 1], stack=st)
    for i in range(2):
        S.op("dve", lambda e: e.tensor_tensor(out=lp[:, i, :], in0=lv[:, 2 * i, :], in1=lv[:, 2 * i + 1, :], op=ALU.mult), reads=[lv_r], writes=[lp_r])
    S.op("dve", lambda e: e.reduce_sum(out=ls[:], in_=lp[:], axis=mybir.AxisListType.X), reads=[lp_r], writes=[ls_r])
    S.op("act", lambda e: e.activation(out=ls[:], in_=ls[:], func=AF.Exp), reads=[ls_r], writes=[ls_r])
    S.op("dve", lambda e: e.tensor_tensor(out=nlam[:], in0=ls[:, 1:2], in1=ls[:, 0:1], op=ALU.subtract), reads=[ls_r], writes=[nlam_r])
    S.op("dve", lambda e: e.tensor_scalar(out=nlam[:], in0=nlam[:], scalar1=-float(lam_init), scalar2=None, op0=ALU.add), reads=[nlam_r], writes=[nlam_r])
    gb, gb_r = kb.sb("gb", [128, 128], stack=st)
    dma(S, "sp", gb[:], gsub.partition_broadcast(128), writes=[gb_r])
    S.op("dve", lambda e: e.tensor_scalar(out=gb[:], in0=gb[:], scalar1=float(1.0 - lam_init), scalar2=None, op0=ALU.mult), reads=[gb_r], writes=[gb_r])

    Sr = kb.ring("S", 2, [128, 2, 256], F32, psum=True, stack=st)
    Or = [[kb.ps(f"O{m}{s}", [128, 512], F32, stack=st) for s in range(2)] for m in range(2)]
    Pr = kb.ring("P", 3, [128, 2, 256], BF16, stack=st)
    rcr = kb.ring("rc", 2, [128, 2], stack=st)
    Ocr = kb.ring("Oc", 2, [128, 2, 2, 129], F32, stack=st)
    o1r = kb.ring("o1", 2, [128, 128], stack=st)
    o2r = kb.ring("o2", 2, [128, 128], stack=st)
    sqr = kb.ring("sq", 2, [128, 128], stack=st)
    ssr = kb.ring("ss", 2, [128, 1], stack=st)
    yr = kb.ring("y", 3, [128, 128], BF16, stack=st)
    jobs = [(qt * 256, list(range(66))) for qt in range(32)]
    if need_ctx:
        jobs.append((NLAT, [64, 65]))
    for (q0, kts) in jobs:
        def pv(ki, kt, P, P_r):
            for m in range(2):
                for s in range(2):
                    O, O_r = Or[m][s]
                    S.op("pe", lambda e: e.matmul(O[:, 0:129], lhsT=P[:, m, s * 128:(s + 1) * 128], rhs=V[:, kt, :], start=(ki == 0), stop=(ki == len(kts) - 1)),
                         reads=[P_r, V_r], writes=[O_r])
        pend = None
        for ki, kt in enumerate(kts):
            Sp, Sp_r = Sr.next()
            for m in range(2):
                S.op("pe", lambda e: e.matmul(Sp[:, m, :], lhsT=KT[:, m, kt * 128:(kt + 1) * 128], rhs=QT[:, m, q0:q0 + 256], start=True, stop=True, skip_group_check=True),
                     reads=[KT_r, QT_r], writes=[Sp_r])
            P, P_r = Pr.next()
            S.op("act", lambda e: e.activation(out=P[:], in_=Sp[:], func=AF.Exp, scale=0.125), reads=[Sp_r], writes=[P_r])
            if pend is not None:
                pv(*pend)
            pend = (ki, kt, P, P_r)
        pv(*pend)
        Oc, Oc_r = Ocr.next()
        for m in range(2):
            for s in range(2):
                S.op("act", lambda e: e.copy(out=Oc[:, m, s, :], in_=Or[m][s][0][:, 0:129]), reads=[Or[m][s][1]], writes=[Oc_r])
        for s in range(2):
            rc, rc_r = rcr.next()
            for m in range(2):
                S.op("dve", lambda e: e.reciprocal(out=rc[:, m:m + 1], in_=Oc[:, m, s, 128:129]), reads=[Oc_r], writes=[rc_r])
            S.op("dve", lambda e: e.tensor_tensor(out=rc[:, 1:2], in0=rc[:, 1:2], in1=nlam[:], op=ALU.mult), reads=[rc_r, nlam_r], writes=[rc_r])
            o1, o1_r = o1r.next()
            o2, o2_r = o2r.next()
            S.op("act", lambda e: e.activation(out=o1[:], in_=Oc[:, 0, s, 0:128], func=AF.Copy, scale=rc[:, 0:1]), reads=[Oc_r, rc_r], writes=[o1_r])
            S.op("dve", lambda e: e.scalar_tensor_tensor(out=o2[:], in0=Oc[:, 1, s, 0:128], scalar=rc[:, 1:2], in1=o1[:], op0=ALU.mult, op1=ALU.add),
                 reads=[Oc_r, rc_r, o1_r], writes=[o2_r])
            sq, sq_r = sqr.next()
            ss, ss_r = ssr.next()
            S.op("pool", lambda e: e.memset(ss[:], 0.0), writes=[ss_r])
            S.op("act", lambda e: e.activation(out=sq[:], in_=o2[:], func=AF.Square, accum_out=ss[:]), reads=[o2_r], writes=[sq_r, ss_r])
            emit_rstd(S, kb, ss[:], ss_r, 128)
            y, y_r = yr.next()
            S.op("dve", lambda e: e.scalar_tensor_tensor(out=y[:], in0=o2[:], scalar=ss[:, 0:1], in1=gb[:], op0=ALU.mult, op1=ALU.mult),
                 reads=[o2_r, ss_r, gb_r], writes=[y_r])
            r0 = q0 + s * 128
            if callable(yA):
                yA(y, y_r, r0, 128)
            else:
                dma(S, "sp", yA[r0:r0 + 128, :], y[:], reads=[y_r], final=True)
    S.barrier(include_cc=False)
    st.close()


def build_x(lam_init, need_ctx, parts=("A", "B", "C", "D"), debug=False):
    kb = KB()
    hT = kb.din("hT", [128, 16, NTOK], BF16)
    wfm = kb.din("wfm", [D, NFM * 128])
    wtm = kb.din("wtm", [D, NTM])
    cosT = kb.din("cosT", [128, NLAT])
    sinT = kb.din("sinT", [128, NLAT])
    sc = make_xscratch(kb, as_output=debug)
    emit_p2(kb, hT, wfm, wtm, cosT, sinT, sc)
    if "A" in parts:
        lamv = kb.din("lamv", [4, 64])
        gsub = kb.din("gsub", [128])
        yA = kb.dout("yA", [NTOK, 128], BF16)
        emit_mixA(kb, sc, lamv, gsub, lam_init, need_ctx, yA)
    if "B" in parts:
        consts = kb.din("bconsts", [128, 6, 128])
        cw = kb.din("cw", [128, 3, 5])
        cb = kb.din("cb", [128, 3])
        pv = kb.din("pv", [10])
        gssm = kb.din("gssm", [128])
        yB = kb.dout("yB", [NTOK, 128], BF16)
        ssqB = kb.dout("ssqB", [128, 66])
        emit_mixB(kb, sc, consts, cw, cb, pv, gssm, need_ctx, yB, ssqB)
    if "C" in parts:
        sinkv = kb.din("sinkv", [2])
        cmask = kb.din("cmask", [128, 2, 256], BF16)
        yC = kb.dout("yC", [NTOK, 128], BF16)
        emit_mixC(kb, sc, sinkv, cmask, need_ctx, yC)
    if "D" in parts:
        biasT = kb.din("biasT", [128, 2, 8, 4, 64])
        yD = kb.dout("yD", [NTOK, 128], BF16)
        emit_mixD(kb, sc, biasT, need_ctx, yD)
    return kb.finish()


def make_hT(hl, hc, b):
    hh = np.concatenate([hl[b], hc[b]], 0)
    return np.ascontiguousarray(hh.T.reshape(16, 128, NTOK).transpose(1, 0, 2))


def emit_mixC(kb, sc, sinkv, cmask, need_ctx, yC):
    S = kb.S
    st = contextlib.ExitStack()
    QT, QT_r = kb.sb("cQT", [64, 2, NTOK], BF16, stack=st)
    KT, KT_r = kb.sb("cKT", [64, NTOK], BF16, stack=st)
    V, V_r = kb.sb("cV", [128, 66, 65], BF16, stack=st)
    for h in range(2):
        dma(S, "sp", QT[:, h, :], sc.C_qT[h * 64:(h + 1) * 64, :], writes=[QT_r])
    dma(S, "sp", KT[:], sc.C_kT[:, :], writes=[KT_r])
    S.op("pool", lambda e: e.memset(V[:, :, 64:65], 1.0), writes=[V_r])
    cvv = sc.C_v.rearrange("(t p) c -> p t c", p=128)
    for t in range(0, 66, 6):
        dma(S, "sp", V[:, t:t + 6, 0:64], cvv[:, t:t + 6, :], writes=[V_r])
    mk, mk_r = kb.sb("cmk", [128, 2, 256], BF16, stack=st)
    dma(S, "sp", mk[:], cmask, writes=[mk_r])
    es, es_r = kb.sb("ces", [128, 2], stack=st)
    dma(S, "sp", es[:], sinkv.partition_broadcast(128), writes=[es_r])
    S.op("act", lambda e: e.activation(out=es[:], in_=es[:], func=AF.Exp), reads=[es_r], writes=[es_r])
    Sr = kb.ring("cS", 2, [128, 512], F32, psum=True, stack=st)
    Or = kb.ring("cO", 4, [128, 512], F32, psum=True, stack=st)
    Pr = kb.ring("cP", 3, [128, 256], BF16, stack=st)
    Pmr = kb.ring("cPm", 3, [128, 256], BF16, stack=st)
    lr = kb.ring("cl", 4, [128, 1], stack=st)
    yr = kb.ring("cy", 3, [128, 128], BF16, stack=st)
    blocks = list(range(64)) + ([64, 65] if need_ctx else [])
    for n in blocks:
        q0 = n * 128
        if n < 64:
            kts = ([(n - 1, 0)] if n > 0 else []) + [(n, None)] + ([(n + 1, 1)] if n < 63 else []) + [(64, None), (65, None)]
        else:
            kts = [(64, None), (65, None)]
        Os = [Or.next(), Or.next()]

        def pvc(ki, kt, P, P_r):
            for h in range(2):
                O, O_r = Os[h]
                S.op("pe", lambda e: e.matmul(O[:, 0:65], lhsT=P[:, h * 128:(h + 1) * 128], rhs=V[:, kt, :], start=(ki == 0), stop=(ki == len(kts) - 1)),
                     reads=[P_r, V_r], writes=[O_r])
        pend = None
        for ki, (kt, msk) in enumerate(kts):
            Sp, Sp_r = Sr.next()
            for h in range(2):
                S.op("pe", lambda e: e.matmul(Sp[:, h * 128:(h + 1) * 128], lhsT=KT[:, kt * 128:(kt + 1) * 128], rhs=QT[:, h, q0:q0 + 128], start=True, stop=True, skip_group_check=True),
                     reads=[KT_r, QT_r], writes=[Sp_r])
            P, P_r = Pr.next()
            S.op("act", lambda e: e.activation(out=P[:], in_=Sp[:, 0:256], func=AF.Exp, scale=0.125), reads=[Sp_r], writes=[P_r])
            if msk is not None:
                Pm, Pm_r = Pmr.next()
                S.op("pool", lambda e: e.tensor_tensor(out=Pm[:], in0=P[:], in1=mk[:, msk, :], op=ALU.mult), reads=[P_r, mk_r], writes=[Pm_r])
                P, P_r = Pm, Pm_r
            if pend is not None:
                pvc(*pend)
            pend = (ki, kt, P, P_r)
        pvc(*pend)
        y, y_r = yr.next()
        for h in range(2):
            O, O_r = Os[h]
            l_, l_r = lr.next()
            S.op("dve", lambda e: e.tensor_tensor(out=l_[:], in0=O[:, 64:65], in1=es[:, h:h + 1], op=ALU.add), reads=[O_r, es_r], writes=[l_r])
            S.op("dve", lambda e: e.reciprocal(out=l_[:], in_=l_[:]), reads=[l_r], writes=[l_r])
            S.op("act", lambda e: e.activation(out=y[:, h * 64:(h + 1) * 64], in_=O[:, 0:64], func=AF.Copy, scale=l_[:, 0:1]), reads=[O_r, l_r], writes=[y_r])
        if callable(yC):
            yC(y, y_r, q0, 128)
        else:
            dma(S, "sp", yC[q0:q0 + 128, :], y[:], reads=[y_r], final=True)
    S.barrier(include_cc=False)
    st.close()


def swa_masks():
    j = np.arange(128)[:, None]
    i = np.arange(128)[None, :]
    lo = (i <= j).astype(np.float32)
    hi = (j <= i).astype(np.float32)
    m = np.stack([np.concatenate([lo, lo], 1), np.concatenate([hi, hi], 1)], 1)
    return np.ascontiguousarray(m.astype(NPBF))


NA_PAT_ROWS = [64, 0, 1, 2, 3, 125, 126, 127]


def na_pattern(r):
    if 4 <= r <= 124:
        return 0
    return 1 + r if r < 4 else 5 + (r - 125)


def na_bias(rpb_l, j):
    out = np.full((128, 2, 8, 4, 64), -30000.0, np.float32)
    qc = np.arange(64)
    cs = np.clip(qc - 8, 0, 48)
    kc = np.arange(64)
    valid = (kc[:, None] >= cs[None, :]) & (kc[:, None] < cs[None, :] + 16)
    co = np.clip(kc[:, None] - qc[None, :] + 15, 0, 30)
    for h2 in range(2):
        head = 2 * j + h2
        for p, r in enumerate(NA_PAT_ROWS):
            rs = int(np.clip(r - 4, 0, 120))
            for t in range(4):
                for j2 in range(2):
                    ro = rs + 2 * t + j2 - r + 7
                    vals = rpb_l[head, ro][co]
                    blk = out[j2 * 64:(j2 + 1) * 64, h2, p, t, :]
                    blk[valid] = vals[valid]
    return out


def emit_mixD(kb, sc, biasT, need_ctx, yD):
    S = kb.S
    st = contextlib.ExitStack()
    QT, QT_r = kb.sb("dQT", [64, 2, NTOK], BF16, stack=st)
    KT, KT_r = kb.sb("dKT", [64, 2, NTOK], BF16, stack=st)
    Ve, Ve_r = kb.sb("dVe", [128, 66, 2, 65], BF16, stack=st)
    Vo, Vo_r = kb.sb("dVo", [128, 63, 2, 65], BF16, stack=st)
    bt, bt_r = kb.sb("dbt", [128, 2, 8, 4, 64], F32, stack=st)
    for h in range(2):
        dma(S, "sp", QT[:, h, :], sc.D_qT[h * 64:(h + 1) * 64, :], writes=[QT_r])
        dma(S, "sp", KT[:, h, :], sc.D_kT[h * 64:(h + 1) * 64, :], writes=[KT_r])
    S.op("pool", lambda e: e.memset(Ve[:], 1.0), writes=[Ve_r])
    S.op("pool", lambda e: e.memset(Vo[:], 1.0), writes=[Vo_r])
    for h in range(2):
        dve_ = sc.D_v[:, h * 64:(h + 1) * 64].rearrange("(t p) c -> p t c", p=128)
        dvo_ = sc.D_v[64:64 + 63 * 128, h * 64:(h + 1) * 64].rearrange("(t p) c -> p t c", p=128)
        for t in range(0, 66, 6):
            dma(S, "sp", Ve[:, t:t + 6, h, 0:64], dve_[:, t:t + 6, :], writes=[Ve_r])
        for t in range(0, 63, 7):
            dma(S, "sp", Vo[:, t:t + 7, h, 0:64], dvo_[:, t:t + 7, :], writes=[Vo_r])
    for h in range(2):
        for p_ in range(8):
            dma(S, "sp", bt[:, h, p_], biasT[:, h, p_], writes=[bt_r])
    Sr = kb.ring("dS", 3, [128, 512], F32, psum=True, stack=st)
    Or = kb.ring("dO", 3, [128, 512], F32, psum=True, stack=st)
    tr = kb.ring("dt", 3, [128, 4, 64], F32, stack=st)
    Pr = kb.ring("dP", 3, [128, 6, 64], BF16, stack=st)
    rr = kb.ring("dr", 4, [64, 1], stack=st)
    yr = kb.ring("dy", 3, [64, 128], BF16, stack=st)
    rows = [(r, False) for r in range(128)] + ([(c, True) for c in range(4)] if need_ctx else [])

    def stage2(P, P_r, ktl, nk, h, y, y_r, q0):
        O, O_r = Or.next()
        for ti, (_, (Vt, Vt_r, vi)) in enumerate(ktl):
            S.op("pe", lambda e: e.matmul(O[0:64, 0:65], lhsT=P[:, ti, :], rhs=Vt[:, vi, h, :], start=(ti == 0), stop=(ti == nk - 1)),
                 reads=[P_r, Vt_r], writes=[O_r])
        rc, rc_r = rr.next()
        S.op("dve", lambda e: e.reciprocal(out=rc[:], in_=O[0:64, 64:65]), reads=[O_r], writes=[rc_r])
        S.op("act", lambda e: e.activation(out=y[:, h * 64:(h + 1) * 64], in_=O[0:64, 0:64], func=AF.Copy, scale=rc[:, 0:1]), reads=[O_r, rc_r], writes=[y_r])
        if h == 1:
            if callable(yD):
                yD(y, y_r, q0, 64)
            else:
                dma(S, "sp", yD[q0:q0 + 64, :], y[:], reads=[y_r], final=True)
    pend = None
    for (r, isctx) in rows:
        y, y_r = yr.next()
        if isctx:
            q0 = NLAT + r * 64
            ktl = [(NLAT, (Ve, Ve_r, 64)), (NLAT + 128, (Ve, Ve_r, 65))]
            pat = None
        else:
            q0 = r * 64
            rs = int(np.clip(r - 4, 0, 120))
            pat = na_pattern(r)
            ktl = []
            for t in range(4):
                if rs % 2 == 0:
                    ktl.append((rs * 64 + t * 128, (Ve, Ve_r, rs // 2 + t)))
                else:
                    ktl.append((rs * 64 + t * 128, (Vo, Vo_r, (rs - 1) // 2 + t)))
            ktl += [(NLAT, (Ve, Ve_r, 64)), (NLAT + 128, (Ve, Ve_r, 65))]
        nk = len(ktl)
        for h in range(2):
            Sp, Sp_r = Sr.next()
            for ti, (k0, _) in enumerate(ktl):
                S.op("pe", lambda e: e.matmul(Sp[:, ti * 64:(ti + 1) * 64], lhsT=KT[:, h, k0:k0 + 128], rhs=QT[:, h, q0:q0 + 64], start=True, stop=True, skip_group_check=True),
                     reads=[KT_r, QT_r], writes=[Sp_r])
            P, P_r = Pr.next()
            if not isctx:
                tmp, tmp_r = tr.next()
                S.op("dve", lambda e: e.scalar_tensor_tensor(out=tmp[:], in0=Sp[:, 0:256].rearrange("p (t q) -> p t q", t=4), scalar=0.125, in1=bt[:, h, pat], op0=ALU.mult, op1=ALU.add),
                     reads=[Sp_r, bt_r], writes=[tmp_r])
                S.op("act", lambda e: e.activation(out=P[:, 0:4, :], in_=tmp[:], func=AF.Exp), reads=[tmp_r], writes=[P_r])
                S.op("act", lambda e: e.activation(out=P[:, 4:6, :], in_=Sp[:, 256:384].rearrange("p (t q) -> p t q", t=2), func=AF.Exp, scale=0.125), reads=[Sp_r], writes=[P_r])
            else:
                S.op("act", lambda e: e.activation(out=P[:, 0:2, :], in_=Sp[:, 0:128].rearrange("p (t q) -> p t q", t=2), func=AF.Exp, scale=0.125), reads=[Sp_r], writes=[P_r])
            if pend is not None:
                stage2(*pend)
            pend = (P, P_r, ktl, nk, h, y, y_r, q0)
    if pend is not None:
        stage2(*pend)
    S.barrier(include_cc=False)
    st.close()


def ssd_consts():
    s = np.arange(128)[:, None]
    l_ = np.arange(128)[None, :]
    triF = (s <= l_).astype(np.float32)
    triB = (s >= l_).astype(np.float32)
    c = np.stack([triF, triB, (triF - 1.0) * 30000.0, (triB - 1.0) * 30000.0, np.ones((128, 128), np.float32), np.eye(128, dtype=np.float32)], 1)
    return np.ascontiguousarray(c.astype(np.float32))


B_DEBUG = {}


def emit_mixB(kb, sc, consts, cw, cb, pv, gssm, need_ctx, yB, ssqB):
    S = kb.S
    AXX = mybir.AxisListType.X
    st = contextlib.ExitStack()
    NCH = 66
    cst, cst_r = kb.sb("bcst", [128, 6, 128], stack=st)
    dma(S, "sp", cst[:], consts, writes=[cst_r])
    triF, triB, mskF, mskB, ones, ident = (cst[:, i, :] for i in range(6))
    cwt, cwt_r = kb.sb("bcw", [128, 3, 5], stack=st)
    cbt, cbt_r = kb.sb("bcb", [128, 3], stack=st)
    dma(S, "sp", cwt[:], cw, writes=[cwt_r])
    dma(S, "sp", cbt[:], cb, writes=[cbt_r])
    pvt, pvt_r = kb.sb("bpv", [128, 10], stack=st)
    dma(S, "sp", pvt[:], pv.partition_broadcast(128), writes=[pvt_r])
    gsb, gsb_r = kb.sb("bgs", [128, 128], stack=st)
    dma(S, "sp", gsb[:], gssm.partition_broadcast(128), writes=[gsb_r])
    xs, xs_r = kb.sb("bxs", [128, NCH, 128], F32, stack=st)
    Btm, Btm_r = kb.sb("bBtm", [128, NCH, 128], BF16, stack=st)
    BT, BT_r = kb.sb("bBT", [128, NTOK], BF16, stack=st)
    CT, CT_r = kb.sb("bCT", [128, NTOK], BF16, stack=st)
    dt, dt_r = kb.sb("bdt", [128, NCH, 4], F32, stack=st)
    da, da_r = kb.sb("bda", [128, NCH, 4], F32, stack=st)
    zt, zt_r = kb.sb("bzt", [128, NCH, 132], F32, stack=st)
    ztv = sc.TMZ.rearrange("(c p) f -> p c f", p=128)
    for c0 in range(0, NCH, 6):
        dma(S, "sp", zt[:, c0:c0 + 6, :], ztv[:, c0:c0 + 6, :], writes=[zt_r])
    S.op("dve", lambda e: e.tensor_copy(out=dt[:], in_=zt[:, :, 128:132]), reads=[zt_r], writes=[dt_r])
    An, An_r = kb.sb("bAn", [128, 4], stack=st)
    S.op("act", lambda e: e.activation(out=An[:], in_=pvt[:, 4:8], func=AF.Exp), reads=[pvt_r], writes=[An_r])
    S.op("dve", lambda e: e.tensor_scalar(out=An[:], in0=An[:], scalar1=-1.0, scalar2=None, op0=ALU.mult), reads=[An_r], writes=[An_r])
    for f in range(4):
        S.op("dve", lambda e: e.tensor_scalar(out=dt[:, :, f], in0=dt[:, :, f], scalar1=pvt[:, f:f + 1], scalar2=None, op0=ALU.add), reads=[dt_r, pvt_r], writes=[dt_r])
    S.op("act", lambda e: e.activation(out=dt[:], in_=dt[:], func=AF.Exp), reads=[dt_r], writes=[dt_r])
    S.op("act", lambda e: e.activation(out=dt[:], in_=dt[:], func=AF.Ln, bias=1.0, scale=1.0), reads=[dt_r], writes=[dt_r])
    for f in range(4):
        S.op("dve", lambda e: e.tensor_scalar(out=da[:, :, f], in0=dt[:, :, f], scalar1=An[:, f:f + 1], scalar2=None, op0=ALU.mult), reads=[dt_r, An_r], writes=[da_r])
    if B_DEBUG.get("stop") == 1:
        S.barrier(include_cc=False)
        st.close()
        return
    st1 = contextlib.ExitStack()
    TC = 1024
    X, X_r = kb.sb("bX", [128, 3, TC + 4], F32, stack=st1)
    acc, acc_r = kb.sb("bacc", [128, 3, TC], F32, stack=st1)
    tpr = kb.ring("btp", 2, [128, 512], F32, psum=True, stack=st1)
    segs = [(i * TC, TC, 0, NLAT) for i in range(NLAT // TC)] + [(NLAT, NCTX, NLAT, NTOK)]
    for (t0, n, s0, s1) in segs:
        lo = max(t0 - 2, s0)
        hi = min(t0 + n + 2, s1)
        if lo > t0 - 2:
            S.op("pool", lambda e: e.memset(X[:, :, 0:2], 0.0), writes=[X_r])
        if hi < t0 + n + 2:
            S.op("pool", lambda e: e.memset(X[:, :, n + 2:n + 4], 0.0), writes=[X_r])
        for i in range(3):
            dma(S, "sp", X[:, i, lo - (t0 - 2):hi - (t0 - 2)], sc.B_xbcT[i, :, lo:hi], writes=[X_r])
        for i in range(3):
            S.op("dve", lambda e: e.tensor_scalar(out=acc[:, i, :n], in0=X[:, i, 0:n], scalar1=cwt[:, i, 0:1], scalar2=cbt[:, i:i + 1], op0=ALU.mult, op1=ALU.add),
                 reads=[X_r, cwt_r, cbt_r], writes=[acc_r])
            for k in range(1, 5):
                S.op("dve", lambda e: e.scalar_tensor_tensor(out=acc[:, i, :n], in0=X[:, i, k:k + n], scalar=cwt[:, i, k:k + 1], in1=acc[:, i, :n], op0=ALU.mult, op1=ALU.add),
                     reads=[X_r, cwt_r, acc_r], writes=[acc_r])
        S.op("act", lambda e: e.activation(out=acc[:, :, :n], in_=acc[:, :, :n], func=AF.Silu), reads=[acc_r], writes=[acc_r])
        S.op("pool", lambda e: e.tensor_copy(out=BT[:, t0:t0 + n], in_=acc[:, 1, :n]), reads=[acc_r], writes=[BT_r])
        S.op("pool", lambda e: e.tensor_copy(out=CT[:, t0:t0 + n], in_=acc[:, 2, :n]), reads=[acc_r], writes=[CT_r])
        for sub in range(n // 128):
            c = (t0 + sub * 128) // 128
            tp, tp_r = tpr.next()
            S.op("pe", lambda e: e.transpose(out=tp[:, 0:128], in_=acc[:, 0, sub * 128:(sub + 1) * 128], identity=ident), reads=[acc_r, cst_r], writes=[tp_r])
            S.op("pe", lambda e: e.transpose(out=tp[:, 128:256], in_=acc[:, 1, sub * 128:(sub + 1) * 128], identity=ident), reads=[acc_r, cst_r], writes=[tp_r])
            S.op("act", lambda e: e.copy(out=xs[:, c, :], in_=tp[:, 0:128]), reads=[tp_r], writes=[xs_r])
            S.op("dve", lambda e: e.tensor_copy(out=Btm[:, c, :], in_=tp[:, 128:256]), reads=[tp_r], writes=[Btm_r])
    S.barrier(include_cc=False)
    st1.close()
    if B_DEBUG.get("stop") == 2:
        st.close()
        return
    st2 = contextlib.ExitStack()
    yF, yF_r = kb.sb("byF", [128, NCH, 128], F32, stack=st2)
    S.op("pool", lambda e: e.memset(yF[:], 0.0), writes=[yF_r])

    class DirState:
        pass
    DS = []
    for d in range(2):
        o = DirState()
        o.hst, o.hst_r = kb.sb(f"bh{d}", [128, 2, 64], F32, stack=st2)
        o.hb, o.hb_r = kb.sb(f"bhb{d}", [128, 2, 64], BF16, stack=st2)
        o.pA, o.pA_r = kb.ps(f"bpA{d}", [128, 512], F32, stack=st2)
        o.pB, o.pB_r = kb.ps(f"bpB{d}", [128, 512], F32, stack=st2)
        o.ac, o.ac_r = kb.sb(f"bac{d}", [128, 2], stack=st2)
        o.wc, o.wc_r = kb.sb(f"bwc{d}", [128, 2], stack=st2)
        o.dtw, o.dtw_r = kb.sb(f"bdtw{d}", [128, 2], stack=st2)
        o.ea, o.ea_r = kb.sb(f"bea{d}", [128, 2], stack=st2)
        o.cd, o.cd_r = kb.sb(f"bcd{d}", [128, 2], stack=st2)
        o.seg, o.seg_r = kb.sb(f"bseg{d}", [128, 2, 128], F32, stack=st2)
        o.Ld, o.Ld_r = kb.sb(f"bLd{d}", [128, 2, 128], F32, stack=st2)
        o.M, o.M_r = kb.sb(f"bM{d}", [128, 2, 128], BF16, stack=st2)
        o.xg, o.xg_r = kb.sb(f"bxg{d}", [128, 2, 64], BF16, stack=st2)
        o.xgw, o.xgw_r = kb.sb(f"bxgw{d}", [128, 2, 64], BF16, stack=st2)
        o.xgf, o.xgf_r = kb.sb(f"bxgf{d}", [128, 2, 64], F32, stack=st2)
        o.yd, o.yd_r = kb.sb(f"byd{d}", [128, 2, 64], F32, stack=st2)
        S.op("pool", lambda e: e.memset(o.hst[:], 0.0), writes=[o.hst_r])
        S.op("pool", lambda e: e.memset(o.hb[:], 0.0), writes=[o.hb_r])
        DS.append(o)

    def process(d, c):
        o = DS[d]
        tri = triF if d == 0 else triB
        msk = mskF if d == 0 else mskB
        pA, pA_r, pB, pB_r = o.pA, o.pA_r, o.pB, o.pB_r
        csl = slice(c * 128, (c + 1) * 128)
        dac = da[:, c, 2 * d:2 * d + 2]
        S.op("pe", lambda e: e.matmul(pA[:, 0:2], lhsT=tri, rhs=dac, start=True, stop=True, skip_group_check=True), reads=[cst_r, da_r], writes=[pA_r])
        S.op("pe", lambda e: e.matmul(pA[:, 2:4], lhsT=ones, rhs=dac, start=True, stop=True, skip_group_check=True), reads=[cst_r, da_r], writes=[pA_r])
        for h in range(2):
            S.op("pe", lambda e: e.matmul(pA[:, 256 + h * 128:256 + (h + 1) * 128], lhsT=da[:, c, 2 * d + h:2 * d + h + 1].to_broadcast([128, 128]), rhs=tri, start=True, stop=True, skip_group_check=True),
                 reads=[cst_r, da_r], writes=[pA_r])
        S.op("pe", lambda e: e.matmul(pA[:, 128:256], lhsT=BT[:, csl], rhs=CT[:, csl], start=True, stop=True, skip_group_check=True), reads=[BT_r, CT_r], writes=[pA_r])
        yield
        S.op("act", lambda e: e.copy(out=o.ac[:], in_=pA[:, 0:2]), reads=[pA_r], writes=[o.ac_r])
        S.op("act", lambda e: e.activation(out=o.ea[:], in_=pA[:, 0:2], func=AF.Exp), reads=[pA_r], writes=[o.ea_r])
        S.op("act", lambda e: e.activation(out=o.cd[:], in_=pA[:, 2:4], func=AF.Exp), reads=[pA_r], writes=[o.cd_r])
        yield
        S.op("dve", lambda e: e.tensor_tensor(out=o.wc[:], in0=pA[:, 2:4], in1=o.ac[:], op=ALU.subtract), reads=[pA_r, o.ac_r], writes=[o.wc_r])
        for h in range(2):
            S.op("dve", lambda e: e.scalar_tensor_tensor(out=o.seg[:, h, :], in0=pA[:, 256 + h * 128:256 + (h + 1) * 128], scalar=o.ac[:, h:h + 1], in1=msk, op0=ALU.subtract, op1=ALU.add),
                 reads=[pA_r, o.ac_r, cst_r], writes=[o.seg_r])
        yield
        S.op("act", lambda e: e.activation(out=o.wc[:], in_=o.wc[:], func=AF.Exp), reads=[o.wc_r], writes=[o.wc_r])
        S.op("act", lambda e: e.activation(out=o.Ld[:], in_=o.seg[:], func=AF.Exp), reads=[o.seg_r], writes=[o.Ld_r])
        yield
        S.op("dve", lambda e: e.tensor_tensor(out=o.dtw[:], in0=dt[:, c, 2 * d:2 * d + 2], in1=o.wc[:], op=ALU.mult), reads=[dt_r, o.wc_r], writes=[o.dtw_r])
        for h in range(2):
            S.op("dve", lambda e: e.tensor_tensor(out=o.M[:, h, :], in0=pA[:, 128:256], in1=o.Ld[:, h, :], op=ALU.mult), reads=[pA_r, o.Ld_r], writes=[o.M_r])
            S.op("act", lambda e: e.activation(out=o.xg[:, h, :], in_=xs[:, c, h * 64:(h + 1) * 64], func=AF.Copy, scale=dt[:, c, 2 * d + h:2 * d + h + 1]),
                 reads=[xs_r, dt_r], writes=[o.xg_r])
            S.op("act", lambda e: e.activation(out=o.xgw[:, h, :], in_=xs[:, c, h * 64:(h + 1) * 64], func=AF.Copy, scale=o.dtw[:, h:h + 1]),
                 reads=[xs_r, o.dtw_r], writes=[o.xgw_r])
        yield
        for h in range(2):
            S.op("pe", lambda e: e.matmul(pB[:, (2 * h) * 64:(2 * h + 1) * 64], lhsT=o.M[:, h, :], rhs=o.xg[:, h, :], start=True, stop=True, skip_group_check=True),
                 reads=[o.M_r, o.xg_r], writes=[pB_r])
            S.op("pe", lambda e: e.matmul(pB[:, (2 * h + 1) * 64:(2 * h + 2) * 64], lhsT=CT[:, csl], rhs=o.hb[:, h, :], start=True, stop=True, skip_group_check=True),
                 reads=[CT_r, o.hb_r], writes=[pB_r])
            S.op("pe", lambda e: e.matmul(pB[:, 256 + h * 64:256 + (h + 1) * 64], lhsT=Btm[:, c, :], rhs=o.xgw[:, h, :], start=True, stop=True, skip_group_check=True),
                 reads=[Btm_r, o.xgw_r], writes=[pB_r])
        yield
        S.op("act", lambda e: e.copy(out=o.yd[:], in_=pB[:, 0:256].rearrange("p (h t q) -> p h t q", h=2, t=2)[:, :, 0, :]), reads=[pB_r], writes=[o.yd_r])
        yield
        for h in range(2):
            S.op("dve", lambda e: e.scalar_tensor_tensor(out=o.yd[:, h, :], in0=pB[:, (2 * h + 1) * 64:(2 * h + 2) * 64], scalar=o.ea[:, h:h + 1], in1=o.yd[:, h, :], op0=ALU.mult, op1=ALU.add),
                 reads=[pB_r, o.ea_r, o.yd_r], writes=[o.yd_r])
            S.op("dve", lambda e: e.scalar_tensor_tensor(out=o.hst[:, h, :], in0=o.hst[:, h, :], scalar=o.cd[:, h:h + 1], in1=pB[:, 256 + h * 64:256 + (h + 1) * 64], op0=ALU.mult, op1=ALU.add),
                 reads=[o.hst_r, o.cd_r, pB_r], writes=[o.hst_r])
        yield
        S.op("act", lambda e: e.copy(out=o.hb[:], in_=o.hst[:]), reads=[o.hst_r], writes=[o.hb_r])
        S.op("pool", lambda e: e.tensor_tensor(out=yF[:, c, :], in0=yF[:, c, :], in1=o.yd[:].rearrange("p h q -> p (h q)"), op=ALU.add),
             reads=[yF_r, o.yd_r], writes=[yF_r])

    orders = [[64, 65] + list(range(64)), [65, 64] + list(range(63, -1, -1))]
    for step in range(NCH):
        gens = [process(0, orders[0][step]), process(1, orders[1][step])]
        alive = list(gens)
        while alive:
            for g in list(alive):
                try:
                    next(g)
                except StopIteration:
                    alive.remove(g)
    zr = kb.ring("bz", 2, [128, 128], F32, stack=st2)
    yr = kb.ring("byo", 2, [128, 128], F32, stack=st2)
    sqr = kb.ring("bsq", 2, [128, 128], F32, stack=st2)
    yor = kb.ring("byb", 2, [128, 128], BF16, stack=st2)
    sso, sso_r = kb.sb("bsso", [128, NCH], F32, stack=st2)
    S.op("pool", lambda e: e.memset(sso[:], 0.0), writes=[sso_r])
    nch_out = NCH if need_ctx else 64
    for c in range(nch_out):
        z, z_r = zr.next()
        y, y_r = yr.next()
        sq, sq_r = sqr.next()
        S.op("act", lambda e: e.activation(out=z[:], in_=zt[:, c, 0:128], func=AF.Silu), reads=[zt_r], writes=[z_r])
        for h in range(2):
            S.op("dve", lambda e: e.scalar_tensor_tensor(out=y[:, h * 64:(h + 1) * 64], in0=xs[:, c, h * 64:(h + 1) * 64], scalar=pvt[:, 8 + h:9 + h], in1=yF[:, c, h * 64:(h + 1) * 64], op0=ALU.mult, op1=ALU.add),
                 reads=[xs_r, pvt_r, yF_r], writes=[y_r])
        S.op("dve", lambda e: e.tensor_tensor(out=y[:], in0=y[:], in1=z[:], op=ALU.mult), reads=[y_r, z_r], writes=[y_r])
        S.op("act", lambda e: e.activation(out=sq[:], in_=y[:], func=AF.Square, accum_out=sso[:, c:c + 1]), reads=[y_r], writes=[sq_r, sso_r])
        yo, yo_r = yor.next()
        S.op("pool", lambda e: e.tensor_tensor(out=yo[:], in0=y[:], in1=gsb[:], op=ALU.mult), reads=[y_r, gsb_r], writes=[yo_r])
        if callable(yB):
            yB(yo, yo_r, c * 128, 128)
        else:
            dma(S, "sp", yB[c * 128:(c + 1) * 128, :], yo[:], reads=[yo_r], final=True)
    if callable(ssqB):
        ssqB(sso, sso_r)
    else:
        dma(S, "sp", ssqB[:, 0:nch_out], sso[:, 0:nch_out], reads=[sso_r], final=True)
    S.barrier(include_cc=False)
    st2.close()
    st.close()


def ssd_params(inp, l, j):
    g = j // 2
    chans = [np.arange(2 * j * 64, (2 * j + 2) * 64), 512 + g * 128 + np.arange(128), 768 + g * 128 + np.arange(128)]
    cw = np.stack([inp["conv_w"][l][:, ch].T for ch in chans], 1)
    cb = np.stack([inp["conv_b"][l][ch] for ch in chans], 1)
    hh = [2 * j, 2 * j + 1]
    pv = np.concatenate([inp["dt_bias"][l][0, hh], inp["dt_bias"][l][1, hh], inp["a_log"][l][0, hh], inp["a_log"][l][1, hh], inp["d_skip"][l][hh]])
    gssm = inp["g_ssm"][l][2 * j * 64:(2 * j + 2) * 64]
    return (np.ascontiguousarray(cw, np.float32), np.ascontiguousarray(cb, np.float32), np.ascontiguousarray(pv, np.float32), np.ascontiguousarray(gssm, np.float32))


def g_chunks(T):
    chunks = [[(i * 512, 512, False)] for i in range(4)]
    if T > 2048:
        chunks[3].append((2048, T - 2048, True))
    return chunks


def wload(S, dst, src, k0, nk, c0, ncols, writes):
    if src.dtype == BF16:
        dma(S, "sp", dst[:, 0:nk, :], src[k0:k0 + nk * 128, c0:c0 + ncols].rearrange("(kc p) c -> p kc c", p=128), writes=writes)
    else:
        for k in range(nk):
            dma(S, "pool", dst[:, k, :], src[k0 + k * 128:k0 + (k + 1) * 128, c0:c0 + ncols], writes=writes)


def emit_wconv(kb, src, dst):
    rows = src.shape[0]
    for r0 in range(0, rows, 512):
        dma(kb.S, "pool", dst[r0:r0 + 512, :], src[r0:r0 + 512, :])


def emit_rstd_bc(S, kb, dst, dst_r, ps, ps_r, n, width):
    S.op("dve", lambda e: e.tensor_scalar(out=dst[:, :width], in0=ps[:, :width], scalar1=1.0 / n, scalar2=EPS, op0=ALU.mult, op1=ALU.add),
         reads=[ps_r], writes=[dst_r])
    S.op("act", lambda e: e.activation(out=dst[:, :width], in_=dst[:, :width], func=AF.Sqrt), reads=[dst_r], writes=[dst_r])
    S.op("dve", lambda e: e.reciprocal(out=dst[:, :width], in_=dst[:, :width]), reads=[dst_r], writes=[dst_r])


def build_g(T):
    kb = KB()
    hT = kb.din("hT", [128, 16, T], BF16)
    ysT = kb.din("ysT", [128, 16, T], BF16)
    ssq4 = kb.din("ssq4", [4, T])
    xT = kb.din("xT", [128, 16, T])
    vecs = kb.din("vecs", [128, 16, 11])
    wg = kb.din("wg", [D, 4 * D])
    wb = kb.din("wb", [4 * 512, D])
    wo = kb.din("wo", [D, D])
    w1 = kb.din("w1", [D, D_FF])
    w2 = kb.din("w2", [D_FF, D])
    x2T = kb.dout("x2T", [128, 16, T])
    emit_g(kb, T, hT, ysT, ssq4, xT, vecs, wg, wb, wo, w1, w2, x2T, True)
    return kb.finish()


def emit_g(kb, T, hT, ysT, ssq4, xT, vecs, wg, wb, wo, w1, w2, x2T, final):
    S = kb.S
    top = contextlib.ExitStack()
    _sb, _ring = kb.sb, kb.ring
    kb_sb = lambda name, shape, dt=F32, stack=None: _sb(name, shape, dt, stack=stack or top)
    kb_ring = lambda name, n, shape, dt=F32, psum=False, stack=None: _ring(name, n, shape, dt, psum=psum, stack=stack or top)
    TCM = 576
    vt, vt_r = kb_sb("vt", [128, 16, 11])
    if callable(vecs):
        vecs(vt, vt_r)
    else:
        dma(S, "sp", vt[:], vecs, writes=[vt_r])
    cv, cv_r = kb_sb("cv", [128, 16, 8])
    for z in range(2):
        o = 3 + 4 * z
        S.op("dve", lambda e: e.tensor_tensor(out=cv[:, :, 4 * z + 0], in0=vt[:, :, o + 0], in1=vt[:, :, 0], op=ALU.mult), reads=[vt_r], writes=[cv_r])
        S.op("dve", lambda e: e.tensor_copy(out=cv[:, :, 4 * z + 1], in_=vt[:, :, o + 1]), reads=[vt_r], writes=[cv_r])
        S.op("dve", lambda e: e.scalar_tensor_tensor(out=cv[:, :, 4 * z + 2], in0=vt[:, :, o + 2], scalar=1.0, in1=vt[:, :, 1], op0=ALU.add, op1=ALU.mult), reads=[vt_r], writes=[cv_r])
        S.op("dve", lambda e: e.tensor_tensor(out=cv[:, :, 4 * z + 3], in0=vt[:, :, o + 3], in1=vt[:, :, 2], op=ALU.mult), reads=[vt_r], writes=[cv_r])
    onesb, onesb_r = kb_sb("onesb", [128, 128], BF16)
    S.op("pool", lambda e: e.memset(onesb[:], 1.0), writes=[onesb_r])
    pr = kb_ring("gp", 4, [128, 512], F32, psum=True)
    sr = kb_ring("gs", 2, [128, 512], F32, psum=True)
    yT, yT_r = kb_sb("yT", [128, 16, TCM], F32)
    sgr = kb_ring("sg", 2, [128, 512], F32)
    tmr = kb_ring("tm", 2, [128, 512], F32)
    sqr = kb_ring("gsq", 2, [128, 512], BF16)
    rs, rs_r = kb_sb("rs", [128, TCM], F32)
    for chunk in g_chunks(T):
        c0 = chunk[0][0]
        tc_ = sum(n for (_, n, _) in chunk)
        nts = [(t0 - c0, n, isc) for (t0, n, isc) in chunk]
        sa = contextlib.ExitStack()
        h, h_r = kb_sb("gh", [128, 16, TCM], BF16, stack=sa)
        ys, ys_r = kb_sb("gys", [128, 16, TCM], BF16, stack=sa)
        mT, mT_r = kb_sb("gm", [128, 16, TCM], BF16, stack=sa)
        macc, macc_r = kb_sb("gmacc", [128, 4, TCM], F32, stack=sa)
        sq4, sq4_r = kb_sb("gsq4", [128, 4, TCM], F32, stack=sa)
        Wr = kb_ring("gW", 2, [128, 16, 512], BF16, stack=sa)
        Wbr = kb_ring("gWb", 2, [128, 4, 512], BF16, stack=sa)
        def load_g1(cg, i):
            W, W_r = Wr.next()
            Wb, Wb_r = Wbr.next()
            wload(S, W, wg, 0, 16, i * D + cg * 512, 512, [W_r])
            wload(S, Wb, wb, i * 512, 4, cg * 512, 512, [Wb_r])
            return W, W_r, Wb, Wb_r

        def load_g2(cg):
            W, W_r = Wr.next()
            wload(S, W, wo, 0, 16, cg * 512, 512, [W_r])
            return W, W_r
        g1blocks = [(cg, i) for cg in range(4) for i in range(4)]
        nxt = load_g1(*g1blocks[0])
        for k in range(16):
            if isinstance(hT, list):
                dma(S, "sp", h[:, k, :tc_], hT[k][:, c0:c0 + tc_], writes=[h_r])
            else:
                dma(S, "sp", h[:, k, :tc_], hT[:, k, c0:c0 + tc_], writes=[h_r])
            dma(S, "sp", ys[:, k, :tc_], ysT[:, k, c0:c0 + tc_], writes=[ys_r])
        for j in range(4):
            dma(S, "sp", sq4[:, j, :tc_], ssq4[j, c0:c0 + tc_].partition_broadcast(128), writes=[sq4_r])
        for j in range(1, 4):
            S.op("dve", lambda e: e.tensor_tensor(out=sq4[:, 0, :tc_], in0=sq4[:, 0, :tc_], in1=sq4[:, j, :tc_], op=ALU.add), reads=[sq4_r], writes=[sq4_r])
        S.op("dve", lambda e: e.tensor_scalar(out=sq4[:, 0, :tc_], in0=sq4[:, 0, :tc_], scalar1=1.0 / 512, scalar2=EPS, op0=ALU.mult, op1=ALU.add), reads=[sq4_r], writes=[sq4_r])
        S.op("act", lambda e: e.activation(out=sq4[:, 0, :tc_], in_=sq4[:, 0, :tc_], func=AF.Sqrt), reads=[sq4_r], writes=[sq4_r])
        S.op("dve", lambda e: e.reciprocal(out=sq4[:, 0, :tc_], in_=sq4[:, 0, :tc_]), reads=[sq4_r], writes=[sq4_r])
        for k in range(4, 8):
            S.op("dve", lambda e: e.tensor_tensor(out=ys[:, k, :tc_], in0=ys[:, k, :tc_], in1=sq4[:, 0, :tc_], op=ALU.mult), reads=[ys_r, sq4_r], writes=[ys_r])
        nxt2 = None
        for bi, (cg, i) in enumerate(g1blocks):
            if True:
                W, W_r, Wb, Wb_r = nxt
                if bi + 1 < len(g1blocks):
                    nxt = load_g1(*g1blocks[bi + 1])
                else:
                    nxt2 = load_g2(0)
                for c4 in range(4):
                    for (o, n, isc) in nts:
                        pg, pg_r = pr.next()
                        pb, pb_r = pr.next()
                        for k in range(16):
                            S.op("pe", lambda e: e.matmul(pg[:, :n], lhsT=W[:, k, c4 * 128:(c4 + 1) * 128], rhs=h[:, k, o:o + n], start=(k == 0), stop=(k == 15)),
                                 reads=[W_r, h_r], writes=[pg_r])
                        for k in range(4):
                            S.op("pe", lambda e: e.matmul(pb[:, :n], lhsT=Wb[:, k, c4 * 128:(c4 + 1) * 128], rhs=ys[:, 4 * i + k, o:o + n], start=(k == 0), stop=(k == 3)),
                                 reads=[Wb_r, ys_r], writes=[pb_r])
                        sg, sg_r = sgr.next()
                        S.op("act", lambda e: e.activation(out=sg[:, :n], in_=pg[:, :n], func=AF.Sigmoid), reads=[pg_r], writes=[sg_r])
                        if i == 0:
                            S.op("dve", lambda e: e.tensor_tensor(out=macc[:, c4, o:o + n], in0=pb[:, :n], in1=sg[:, :n], op=ALU.mult), reads=[pb_r, sg_r], writes=[macc_r])
                        else:
                            tm, tm_r = tmr.next()
                            S.op("dve", lambda e: e.tensor_tensor(out=tm[:, :n], in0=pb[:, :n], in1=sg[:, :n], op=ALU.mult), reads=[pb_r, sg_r], writes=[tm_r])
                            if i < 3:
                                S.op("pool", lambda e: e.tensor_tensor(out=macc[:, c4, o:o + n], in0=macc[:, c4, o:o + n], in1=tm[:, :n], op=ALU.add), reads=[macc_r, tm_r], writes=[macc_r])
                            else:
                                S.op("pool", lambda e: e.tensor_tensor(out=mT[:, cg * 4 + c4, o:o + n], in0=macc[:, c4, o:o + n], in1=tm[:, :n], op=ALU.add), reads=[macc_r, tm_r], writes=[mT_r])
        ssp = [sr.next() for _ in nts]
        for cg in range(4):
            W, W_r = nxt2
            if cg + 1 < 4:
                nxt2 = load_g2(cg + 1)
            for c4 in range(4):
                ct = cg * 4 + c4
                for ni, (o, n, isc) in enumerate(nts):
                    py, py_r = pr.next()
                    for k in range(16):
                        S.op("pe", lambda e: e.matmul(py[:, :n], lhsT=W[:, k, c4 * 128:(c4 + 1) * 128], rhs=mT[:, k, o:o + n], start=(k == 0), stop=(k == 15)),
                             reads=[W_r, mT_r], writes=[py_r])
                    S.op("act", lambda e: e.copy(out=yT[:, ct, o:o + n], in_=py[:, :n]), reads=[py_r], writes=[yT_r])
                    sq, sq_r = sqr.next()
                    S.op("dve", lambda e: e.tensor_tensor(out=sq[:, :n], in0=yT[:, ct, o:o + n], in1=yT[:, ct, o:o + n], op=ALU.mult), reads=[yT_r], writes=[sq_r])
                    S.op("pe", lambda e: e.matmul(ssp[ni][0][:, :n], lhsT=onesb[:], rhs=sq[:, :n], start=(ct == 0), stop=(ct == 15)),
                         reads=[onesb_r, sq_r], writes=[ssp[ni][1]])
        for ni, (o, n, isc) in enumerate(nts):
            emit_rstd_bc(S, kb, rs[:, o:o + n], rs_r, ssp[ni][0], ssp[ni][1], D, n)
        S.barrier()
        sa.close()
        sb_ = contextlib.ExitStack()
        h2, h2_r = kb_sb("gh2", [128, 16, TCM], BF16, stack=sb_)
        facc, facc_r = kb_sb("gfacc", [128, 16, TCM], F32, stack=sb_)
        xin = kb_ring("gxin", 2, [128, TCM], F32, stack=sb_)
        W1r = kb_ring("gW1", 2, [128, 16, 256], BF16, stack=sb_)
        W2r = kb_ring("gW2", 3, [128, 2, D], BF16, stack=sb_)
        aTr = kb_ring("gaT", 3, [128, 2, TCM], BF16, stack=sb_)
        rlr = kb_ring("grl", 2, [128, 512], F32, stack=sb_)

        def load_ffn(hc):
            W1, W1_r = W1r.next()
            W2, W2_r = W2r.next()
            wload(S, W1, w1, 0, 16, hc * 256, 256, [W1_r])
            wload(S, W2, w2, hc * 256, 2, 0, D, [W2_r])
            return W1, W1_r, W2, W2_r
        nxtf = load_ffn(0)
        ssp = [sr.next() for _ in nts]
        for ft in range(16):
            xt, xt_r = xin.next()
            dma(S, "sp", xt[:, :tc_], xT[:, ft, c0:c0 + tc_], writes=[xt_r])
            S.op("dve", lambda e: e.tensor_tensor(out=yT[:, ft, :tc_], in0=yT[:, ft, :tc_], in1=rs[:, :tc_], op=ALU.mult), reads=[yT_r, rs_r], writes=[yT_r])
            for ni, (o, n, isc) in enumerate(nts):
                z = 4 * int(isc)
                S.op("dve", lambda e: e.scalar_tensor_tensor(out=yT[:, ft, o:o + n], in0=yT[:, ft, o:o + n], scalar=cv[:, ft, z:z + 1], in1=xt[:, o:o + n], op0=ALU.mult, op1=ALU.add),
                     reads=[yT_r, cv_r, xt_r], writes=[yT_r])
                sq, sq_r = sqr.next()
                S.op("pool", lambda e: e.tensor_tensor(out=sq[:, :n], in0=yT[:, ft, o:o + n], in1=yT[:, ft, o:o + n], op=ALU.mult), reads=[yT_r], writes=[sq_r])
                S.op("pe", lambda e: e.matmul(ssp[ni][0][:, :n], lhsT=onesb[:], rhs=sq[:, :n], start=(ft == 0), stop=(ft == 15)),
                     reads=[onesb_r, sq_r], writes=[ssp[ni][1]])
        for ni, (o, n, isc) in enumerate(nts):
            emit_rstd_bc(S, kb, rs[:, o:o + n], rs_r, ssp[ni][0], ssp[ni][1], D, n)
        for ft in range(16):
            for ni, (o, n, isc) in enumerate(nts):
                z = 4 * int(isc)
                tm, tm_r = tmr.next()
                S.op("dve", lambda e: e.tensor_tensor(out=tm[:, :n], in0=yT[:, ft, o:o + n], in1=rs[:, o:o + n], op=ALU.mult), reads=[yT_r, rs_r], writes=[tm_r])
                S.op("act", lambda e: e.activation(out=h2[:, ft, o:o + n], in_=tm[:, :n], func=AF.Identity, scale=cv[:, ft, z + 2:z + 3], bias=cv[:, ft, z + 1:z + 2]),
                     reads=[tm_r, cv_r], writes=[h2_r])
        NHC = D_FF // 256

        def ffn_a(hc):
            nonlocal nxtf
            W1, W1_r, W2, W2_r = nxtf
            if hc + 1 < NHC:
                nxtf = load_ffn(hc + 1)
            aT, aT_r = aTr.next()
            for kt in range(2):
                for (o, n, isc) in nts:
                    pa, pa_r = pr.next()
                    for k in range(16):
                        S.op("pe", lambda e: e.matmul(pa[:, :n], lhsT=W1[:, k, kt * 128:(kt + 1) * 128], rhs=h2[:, k, o:o + n], start=(k == 0), stop=(k == 15)),
                             reads=[W1_r, h2_r], writes=[pa_r])
                    rl, rl_r = rlr.next()
                    S.op("act", lambda e: e.activation(out=rl[:, :n], in_=pa[:, :n], func=AF.Relu), reads=[pa_r], writes=[rl_r])
                    S.op("pool", lambda e: e.tensor_tensor(out=aT[:, kt, o:o + n], in0=rl[:, :n], in1=rl[:, :n], op=ALU.mult), reads=[rl_r], writes=[aT_r])
            return hc, W2, W2_r, aT, aT_r

        def ffn_f(hc, W2, W2_r, aT, aT_r):
            for ct in range(16):
                for (o, n, isc) in nts:
                    pf, pf_r = pr.next()
                    for kt in range(2):
                        S.op("pe", lambda e: e.matmul(pf[:, :n], lhsT=W2[:, kt, ct * 128:(ct + 1) * 128], rhs=aT[:, kt, o:o + n], start=(kt == 0), stop=(kt == 1)),
                             reads=[W2_r, aT_r], writes=[pf_r])
                    if hc == 0:
                        S.op("act", lambda e: e.copy(out=facc[:, ct, o:o + n], in_=pf[:, :n]), reads=[pf_r], writes=[facc_r])
                    else:
                        S.op("dve", lambda e: e.tensor_tensor(out=facc[:, ct, o:o + n], in0=pf[:, :n], in1=facc[:, ct, o:o + n], op=ALU.add), reads=[pf_r, facc_r], writes=[facc_r])
        pendf = ffn_a(0)
        for hc in range(1, NHC):
            cur = ffn_a(hc)
            ffn_f(*pendf)
            pendf = cur
        ffn_f(*pendf)
        ssp = [sr.next() for _ in nts]
        for ft in range(16):
            for ni, (o, n, isc) in enumerate(nts):
                sq, sq_r = sqr.next()
                S.op("pool", lambda e: e.tensor_tensor(out=sq[:, :n], in0=facc[:, ft, o:o + n], in1=facc[:, ft, o:o + n], op=ALU.mult), reads=[facc_r], writes=[sq_r])
                S.op("pe", lambda e: e.matmul(ssp[ni][0][:, :n], lhsT=onesb[:], rhs=sq[:, :n], start=(ft == 0), stop=(ft == 15)),
                     reads=[onesb_r, sq_r], writes=[ssp[ni][1]])
        for ni, (o, n, isc) in enumerate(nts):
            emit_rstd_bc(S, kb, rs[:, o:o + n], rs_r, ssp[ni][0], ssp[ni][1], D, n)
        for ft in range(16):
            S.op("dve", lambda e: e.tensor_tensor(out=facc[:, ft, :tc_], in0=facc[:, ft, :tc_], in1=rs[:, :tc_], op=ALU.mult), reads=[facc_r, rs_r], writes=[facc_r])
            for ni, (o, n, isc) in enumerate(nts):
                z = 4 * int(isc)
                S.op("dve", lambda e: e.scalar_tensor_tensor(out=facc[:, ft, o:o + n], in0=facc[:, ft, o:o + n], scalar=cv[:, ft, z + 3:z + 4], in1=yT[:, ft, o:o + n], op0=ALU.mult, op1=ALU.add),
                     reads=[facc_r, cv_r, yT_r], writes=[facc_r])
            dma(S, "sp", x2T[:, ft, c0:c0 + tc_], facc[:, ft, :tc_], reads=[facc_r], final=final)
        S.barrier()
        sb_.close()
    S.barrier()
    top.close()


def fm_layout(a):
    T, Dd = a.shape
    return np.ascontiguousarray(a.T.reshape(Dd // 128, 128, T).transpose(1, 0, 2))


def run_x(inp, l, hl, hc, need_ctx):
    lam_init = 0.8 - 0.6 * math.exp(-0.3 * l)
    nc = build_x(lam_init, need_ctx)
    cosT, sinT = rope_tables()
    w_in = inp["w_in"][l]
    hTs = [make_hT(hl, hc, b) for b in range(2)]
    lamv = np.ascontiguousarray(np.stack([inp["lam_q1"][l], inp["lam_k1"][l], inp["lam_q2"][l], inp["lam_k2"][l]], 0))
    consts = ssd_consts()
    cmask = swa_masks()
    maps = []
    for core in range(8):
        b, j = divmod(core, 4)
        fm, tm = x_weight_cols(j)
        cw, cb, pv, gssm = ssd_params(inp, l, j)
        maps.append({
            "hT": hTs[b], "wfm": np.ascontiguousarray(w_in[:, fm]), "wtm": np.ascontiguousarray(w_in[:, tm]),
            "cosT": cosT, "sinT": sinT, "lamv": lamv, "gsub": np.ascontiguousarray(inp["g_subln"][l]),
            "bconsts": consts, "cw": cw, "cb": cb, "pv": pv, "gssm": gssm,
            "sinkv": np.ascontiguousarray(inp["sink"][l][2 * j:2 * j + 2]), "cmask": cmask,
            "biasT": na_bias(inp["rpb"][l], j),
        })
    return run_spmd(nc, maps)


def run_g(inp, l, xcur, cxcur, hl, hc, xres, mod_l, need_ctx):
    T = TROWS if need_ctx else 2048
    nc = build_g(T)
    wg = np.ascontiguousarray(inp["w_in"][l][:, GATE_OFF:])
    wb = np.ascontiguousarray(inp["w_branch"][l].reshape(4 * 512, D))
    maps = []
    for core in range(8):
        b, q = divmod(core, 4)
        rows = np.arange(q * 2048, (q + 1) * 2048)
        if need_ctx:
            rows = np.concatenate([rows, NLAT + np.arange(q * 64, (q + 1) * 64)])
        hh = np.concatenate([hl[b], hc[b]], 0)[rows]
        xx = np.concatenate([xcur[b], cxcur[b]], 0)[rows]
        ysT = np.zeros((128, 16, T), NPBF)
        ssq4 = np.zeros((4, T), np.float32)
        for j in range(4):
            r = xres[b * 4 + j]
            for i, nm in enumerate(("yA", "yB", "yC", "yD")):
                ysT[:, 4 * i + j, :] = r[nm][rows].T
            ssq4[j] = r["ssqB"].T.reshape(-1)[rows]
        vec = np.stack([inp["g_post_mix"][l], inp["g_pre_mlp"][l], inp["g_post_mlp"][l],
                        mod_l[b, 2], mod_l[b, 3], mod_l[b, 4], mod_l[b, 5],
                        mod_l[2, 2], mod_l[2, 3], mod_l[2, 4], mod_l[2, 5]], 1)
        vecs = np.ascontiguousarray(vec.reshape(16, 128, 11).transpose(1, 0, 2))
        maps.append({"hT": fm_layout(hh), "ysT": ysT, "ssq4": ssq4, "xT": fm_layout(xx), "vecs": vecs,
                     "wg": wg, "wb": wb, "wo": inp["w_out"][l], "w1": inp["w_ff1"][l], "w2": inp["w_ff2"][l]})
    res = run_spmd(nc, maps)
    xn = np.zeros_like(xcur)
    cxn = np.array(cxcur, copy=True)
    for core in range(8):
        b, q = divmod(core, 4)
        o = res[core]["x2T"].transpose(1, 0, 2).reshape(D, T).T
        xn[b, q * 2048:(q + 1) * 2048] = o[:2048]
        if need_ctx:
            cxn[b, q * 64:(q + 1) * 64] = o[2048:]
    return xn, cxn


RG = [[0, 1, 2, 3], [4, 5, 6, 7]]


def allgather(S, src, dst, bg=False):
    return S.op("pool", lambda e: e.collective_compute("AllGather", ALU.bypass, replica_groups=RG, ins=[src], outs=[dst]),
                dma=True, dma_inc=1, pool="cc" if bg else None)


def emit_mod(kb, cT2, wmod, bmT, cvm, cvm_r, identf, identf_r, msrc, mdst):
    S = kb.S
    st = contextlib.ExitStack()
    ct, ct_r = kb.sb("mct", [128, 16, 2], stack=st)
    sT, sT_r = kb.sb("msT", [128, 16, 2], stack=st)
    bt, bt_r = kb.sb("mbt", [128, DEPTH, 24], stack=st)
    mrow, mrow_r = kb.sb("mrow", [2, 3072], F32, stack=st)
    part, part_r = kb.sb("mpart", [128, DEPTH, 24, 2], F32, stack=st)
    dma(S, "sp", ct[:], cT2, writes=[ct_r])
    for l in range(DEPTH):
        dma(S, "sp", bt[:, l, :], bmT[l], writes=[bt_r])
    S.op("act", lambda e: e.activation(out=sT[:], in_=ct[:], func=AF.Silu), reads=[ct_r], writes=[sT_r])
    wr = kb.ring("mw", 2, [128, 16, 512], F32, stack=st)
    pr = kb.ring("mp", 2, [128, 512], F32, psum=True, stack=st)
    pt, pt_r = kb.ps("mpt", [128, 512], F32, stack=st)
    for l in range(DEPTH):
        wv = wmod[l].rearrange("(kc p) c -> p kc c", p=128)
        for cg in range(6):
            w, w_r = wr.next()
            dma(S, "sp", w[:], wv[:, :, cg * 512:(cg + 1) * 512], writes=[w_r])
            p, p_r = pr.next()
            for k in range(16):
                S.op("pe", lambda e: e.matmul(p[0:2, :], lhsT=sT[:, k, :], rhs=w[:, k, :], start=(k == 0), stop=(k == 15)), reads=[w_r, sT_r], writes=[p_r])
            S.op("act", lambda e: e.copy(out=mrow[:, cg * 512:(cg + 1) * 512], in_=p[0:2, :]), reads=[p_r], writes=[mrow_r])
        for c_ in range(24):
            S.op("pe", lambda e: e.transpose(out=pt[:, c_ * 2:c_ * 2 + 2], in_=mrow[0:2, c_ * 128:(c_ + 1) * 128], identity=identf[0:2, 0:2]),
                 reads=[mrow_r, identf_r], writes=[pt_r])
        ptv = pt[:, 0:48].rearrange("p (c z) -> p c z", z=2)
        for z in range(2):
            S.op("dve", lambda e: e.tensor_tensor(out=part[:, l, :, z], in0=ptv[:, :, z], in1=bt[:, l, :], op=ALU.add), reads=[pt_r, bt_r], writes=[part_r])
    dma(S, "sp", msrc, part[:].rearrange("p l c z -> p (l c z)"), reads=[part_r])
    S.barrier()
    allgather(S, msrc, mdst)
    S.barrier()
    for l in range(DEPTH):
        cvl = cvm[:, l].rearrange("p s k z -> p (s k z)")
        for r in range(4):
            dma(S, "sp", cvl[:, r * 48:(r + 1) * 48], mdst[r * 128:(r + 1) * 128, l * 48:(l + 1) * 48], writes=[cvm_r])
    S.barrier()
    st.close()


def emit_x2xT(kb, xrows, xT, identf, identf_r):
    S = kb.S
    st = contextlib.ExitStack()
    xr = kb.ring("tx", 2, [128, D], F32, stack=st)
    pr = kb.ring("tp", 2, [128, 512], F32, psum=True, stack=st)
    orr = kb.ring("to", 2, [128, 16, 128], F32, stack=st)
    tiles = [(i * 128, 128) for i in range(16)] + [(2048, 64)]
    for (r0, n) in tiles:
        xt, xt_r = xr.next()
        ot, ot_r = orr.next()
        dma(S, "sp", xt[:n, :], xrows[r0:r0 + n, :], writes=[xt_r])
        for g in range(4):
            p, p_r = pr.next()
            for k4 in range(4):
                k = g * 4 + k4
                S.op("pe", lambda e: e.transpose(out=p[:, k4 * 128:k4 * 128 + n], in_=xt[:n, k * 128:(k + 1) * 128], identity=identf[:n, :n]),
                     reads=[xt_r, identf_r], writes=[p_r])
            for k4 in range(4):
                eng = "act" if k4 % 2 == 0 else "dve"
                if eng == "act":
                    S.op("act", lambda e: e.copy(out=ot[:, g * 4 + k4, :n], in_=p[:, k4 * 128:k4 * 128 + n]), reads=[p_r], writes=[ot_r])
                else:
                    S.op("dve", lambda e: e.tensor_copy(out=ot[:, g * 4 + k4, :n], in_=p[:, k4 * 128:k4 * 128 + n]), reads=[p_r], writes=[ot_r])
        dma(S, "sp", xT[:, :, r0:r0 + n], ot[:, :, :n], reads=[ot_r])
    S.barrier()
    st.close()


def emit_xT2out(kb, xT, out, identf, identf_r):
    S = kb.S
    st = contextlib.ExitStack()
    xr = kb.ring("ux", 2, [128, 16, 128], F32, stack=st)
    pr = kb.ring("up", 2, [128, 512], F32, psum=True, stack=st)
    orr = kb.ring("uo", 2, [128, D], F32, stack=st)
    for t in range(16):
        xt, xt_r = xr.next()
        ot, ot_r = orr.next()
        dma(S, "sp", xt[:], xT[:, :, t * 128:(t + 1) * 128], writes=[xt_r])
        for g in range(4):
            p, p_r = pr.next()
            for k4 in range(4):
                k = g * 4 + k4
                S.op("pe", lambda e: e.transpose(out=p[:, k4 * 128:(k4 + 1) * 128], in_=xt[:, k, :], identity=identf[:]), reads=[xt_r, identf_r], writes=[p_r])
            S.op("act", lambda e: e.copy(out=ot[:, g * 512:(g + 1) * 512], in_=p[:]), reads=[p_r], writes=[ot_r])
        dma(S, "sp", out[t * 128:(t + 1) * 128, :], ot[:], reads=[ot_r], final=True)
    S.barrier()
    st.close()


def emit_p1f(kb, xT, c0, c1, c_r, hT_own, onesb, onesb_r):
    S = kb.S
    st = contextlib.ExitStack()
    xr = kb.ring("px", 2, [128, 16, 512], F32, stack=st)
    hr = kb.ring("ph", 2, [128, 16, 512], BF16, stack=st)
    sqr = kb.ring("psq", 2, [128, 512], BF16, stack=st)
    tmr = kb.ring("ptm", 2, [128, 512], F32, stack=st)
    sr = kb.ring("pss", 2, [128, 512], F32, psum=True, stack=st)
    rsr = kb.ring("prs", 2, [128, 512], F32, stack=st)
    tiles = [(i * 512, 512, 0) for i in range(4)] + [(2048, 64, 1)]
    for (t0, n, z) in tiles:
        xt, xt_r = xr.next()
        ht, ht_r = hr.next()
        for k in range(0, 16, 4):
            dma(S, "sp", xt[:, k:k + 4, :n], xT[:, k:k + 4, t0:t0 + n], writes=[xt_r])
        ss, ss_r = sr.next()
        for k in range(16):
            sq, sq_r = sqr.next()
            S.op("pool" if k % 2 else "dve", lambda e: e.tensor_tensor(out=sq[:, :n], in0=xt[:, k, :n], in1=xt[:, k, :n], op=ALU.mult), reads=[xt_r], writes=[sq_r])
            S.op("pe", lambda e: e.matmul(ss[:, :n], lhsT=onesb[:], rhs=sq[:, :n], start=(k == 0), stop=(k == 15)), reads=[onesb_r, sq_r], writes=[ss_r])
        rs, rs_r = rsr.next()
        emit_rstd_bc(S, kb, rs, rs_r, ss, ss_r, D, n)
        for k in range(16):
            tm, tm_r = tmr.next()
            S.op("dve", lambda e: e.tensor_tensor(out=tm[:, :n], in0=xt[:, k, :n], in1=rs[:, :n], op=ALU.mult), reads=[xt_r, rs_r], writes=[tm_r])
            S.op("act", lambda e: e.activation(out=ht[:, k, :n], in_=tm[:, :n], func=AF.Identity, scale=c1[:, k, z:z + 1], bias=c0[:, k, z:z + 1]),
                 reads=[tm_r, c_r], writes=[ht_r])
        for k in range(16):
            dma(S, "sp", hT_own[k][:, t0:t0 + n], ht[:, k, :n], reads=[ht_r])
    S.barrier()
    st.close()


def build_fused():
    kb = KB()
    S = kb.S
    xrows = kb.din("xrows", [TROWS, D])
    cT2 = kb.din("cT2", [128, 16, 2])
    wmod = kb.din("wmod", [DEPTH, D, 3072])
    bmT = kb.din("bmT", [DEPTH, 128, 24])
    gvec = kb.din("gvec", [128, DEPTH, 16, 4])
    oneh = kb.din("oneh", [4])
    cosT = kb.din("cosT", [128, NLAT])
    sinT = kb.din("sinT", [128, NLAT])
    bconsts = kb.din("bconsts", [128, 6, 128])
    cmask = kb.din("cmask", [128, 2, 256], BF16)
    L = []
    for l in range(DEPTH):
        d = {}
        d["wfm"] = kb.din(f"wfm{l}", [D, NFM * 128])
        d["wtm"] = kb.din(f"wtm{l}", [D, NTM])
        d["lamv"] = kb.din(f"lamv{l}", [4, 64])
        d["gsub"] = kb.din(f"gsub{l}", [128])
        d["cw"] = kb.din(f"cw{l}", [128, 3, 5])
        d["cb"] = kb.din(f"cb{l}", [128, 3])
        d["pv"] = kb.din(f"pv{l}", [10])
        d["gssm"] = kb.din(f"gssm{l}", [128])
        d["sinkv"] = kb.din(f"sinkv{l}", [2])
        d["biasT"] = kb.din(f"biasT{l}", [128, 2, 8, 4, 64])
        d["wg"] = kb.din(f"wg{l}", [D, 4 * D])
        d["wb"] = kb.din(f"wb{l}", [4 * 512, D])
        d["wo"] = kb.din(f"wo{l}", [D, D])
        d["w1"] = kb.din(f"w1{l}", [D, D_FF])
        d["w2"] = kb.din(f"w2{l}", [D_FF, D])
        L.append(d)
    out = kb.dout("out", [2048, D])
    xT = kb.dscratch("xT", [128, 16, TROWS])
    hT_own = [kb.dscratch(f"hT_own{k}", [128, TROWS], BF16) for k in range(16)]
    hT_all = [kb.dscratch(f"hT_all{k}", [512, TROWS], BF16) for k in range(16)]
    ycat = {(i, q): kb.dscratch(f"ycat{i}{q}", [128, TROWS], BF16) for i in range(4) for q in range(4)}
    yall = {(i, q): kb.dscratch(f"yall{i}{q}", [512, TROWS], BF16) for i in range(4) for q in range(4)}
    ssq_own = kb.dscratch("ssq_own", [66, 128])
    msrc = kb.dscratch("msrc", [128, DEPTH * 48])
    mdst = kb.dscratch("mdst", [512, DEPTH * 48])
    ssq_all = kb.dscratch("ssq_all", [4 * 66, 128])
    ys_own = kb.dscratch("ys_own", [128, 16, TROWS], BF16)
    ssq_sel = kb.dscratch("ssq_sel", [4, TROWS])
    sc = make_xscratch(kb)
    WB = []
    for l in range(DEPTH):
        WB.append({"wg": kb.dscratch(f"wgb{l}", [D, 4 * D], BF16), "wb": kb.dscratch(f"wbb{l}", [4 * 512, D], BF16),
                   "wo": kb.dscratch(f"wob{l}", [D, D], BF16), "w1": kb.dscratch(f"w1b{l}", [D, D_FF], BF16),
                   "w2": kb.dscratch(f"w2b{l}", [D_FF, D], BF16)})

    def conv_weights(l):
        for nm in ("wg", "wb", "wo", "w1", "w2"):
            emit_wconv(kb, L[l][nm], WB[l][nm])
    cvm, cvm_r = kb.sb("cvm", [128, DEPTH, 6, 16, 2])
    gv, gv_r = kb.sb("gv", [128, DEPTH, 16, 4])
    oh, oh_r = kb.sb("oh", [128, 4])
    cst, cst_r = kb.sb("fcst", [128, 128])
    identb, identb_r = kb.sb("identb", [128, 128], BF16)
    onesb, onesb_r = kb.sb("fonesb", [128, 128], BF16)
    cc, cc_r = kb.sb("fcc", [128, 2, 16, 2])
    dma(S, "sp", gv[:], gvec, writes=[gv_r])
    dma(S, "sp", oh[:], oneh.partition_broadcast(128), writes=[oh_r])
    dma(S, "sp", cst[:], bconsts[:, 5, :], writes=[cst_r])
    S.op("dve", lambda e: e.tensor_copy(out=identb[:], in_=cst[:]), reads=[cst_r], writes=[identb_r])
    S.op("pool", lambda e: e.memset(onesb[:], 1.0), writes=[onesb_r])
    emit_mod(kb, cT2, wmod, bmT, cvm, cvm_r, cst, cst_r, msrc, mdst)
    emit_x2xT(kb, xrows, xT, cst, cst_r)
    for l in range(DEPTH):
        need_ctx = l < DEPTH - 1
        lam_init = 0.8 - 0.6 * math.exp(-0.3 * l)
        d = L[l]
        for z in range(2):
            S.op("dve", lambda e: e.tensor_copy(out=cc[:, 0, :, z], in_=cvm[:, l, 0, :, z]), reads=[cvm_r], writes=[cc_r])
            S.op("dve", lambda e: e.scalar_tensor_tensor(out=cc[:, 1, :, z], in0=cvm[:, l, 1, :, z], scalar=1.0, in1=gv[:, l, :, 0], op0=ALU.add, op1=ALU.mult),
                 reads=[cvm_r, gv_r], writes=[cc_r])
        emit_p1f(kb, xT, cc[:, 0], cc[:, 1], cc_r, hT_own, onesb, onesb_r)
        for k in range(16):
            allgather(S, hT_own[k], hT_all[k])
        S.barrier()
        xst = contextlib.ExitStack()
        ytp, ytp_r = kb.ps("ytp", [128, 1024], BF16, stack=xst)
        ytr = kb.ring("yts", 3, [128, 128], BF16, stack=xst)

        def load_hT(h, h_r, t0, n, isctx):
            if not isctx:
                r, off = divmod(t0, 2048)
                for k in range(16):
                    dma(S, "sp", h[:, k, :n], hT_all[k][r * 128:(r + 1) * 128, off:off + n], writes=[h_r])
            else:
                for r in range(4):
                    for k in range(16):
                        dma(S, "sp", h[:, k, r * 64:(r + 1) * 64], hT_all[k][r * 128:(r + 1) * 128, 2048:2112], writes=[h_r])

        def mk_ywrite(i):
            def yw(y, y_r, tok0, n):
                S.op("pe", lambda e: e.transpose(out=ytp[:, :n], in_=y[:n, :], identity=identb[:n, :n]), reads=[y_r, identb_r], writes=[ytp_r])
                yt, yt_r = ytr.next()
                S.op("dve", lambda e: e.tensor_copy(out=yt[:, :n], in_=ytp[:, :n]), reads=[ytp_r], writes=[yt_r])
                if tok0 < NLAT:
                    qq, off = divmod(tok0, 2048)
                    dma(S, "sp", ycat[(i, qq)][:, off:off + n], yt[:, :n], reads=[yt_r])
                else:
                    for c_ in range(0, n, 64):
                        qq, o2 = divmod(tok0 - NLAT + c_, 64)
                        dma(S, "sp", ycat[(i, qq)][:, 2048 + o2:2048 + o2 + 64], yt[:, c_:c_ + 64], reads=[yt_r])
            return yw

        pst, pst_r = kb.ps("ssqtp", [128, 512], F32, stack=xst)
        so, so_r = kb.sb("ssqo", [66, 128], F32, stack=xst)

        def ssq_write(sso, sso_r):
            S.op("pe", lambda e: e.transpose(out=pst[:66, :128], in_=sso[:, 0:66], identity=cst[:]), reads=[sso_r, cst_r], writes=[pst_r])
            S.op("act", lambda e: e.copy(out=so[:], in_=pst[:66, :128]), reads=[pst_r], writes=[so_r])
            dma(S, "sp", ssq_own, so[:], reads=[so_r])

        emit_p2(kb, load_hT, d["wfm"], d["wtm"], cosT, sinT, sc,
                after_wload=(lambda: [conv_weights(l_) for l_ in range(DEPTH)]) if l == 0 else None)
        def gather_branch(i):
            for qq in range(4):
                allgather(S, ycat[(i, qq)], yall[(i, qq)], bg=True)
        emit_mixA(kb, sc, d["lamv"], d["gsub"], lam_init, need_ctx, mk_ywrite(0))
        gather_branch(0)
        emit_mixC(kb, sc, d["sinkv"], cmask, need_ctx, mk_ywrite(2))
        gather_branch(2)
        emit_mixD(kb, sc, d["biasT"], need_ctx, mk_ywrite(3))
        gather_branch(3)
        emit_mixB(kb, sc, bconsts, d["cw"], d["cb"], d["pv"], d["gssm"], need_ctx, mk_ywrite(1), ssq_write)
        S.barrier(include_cc=False)
        xst.close()
        gather_branch(1)
        allgather(S, ssq_own, ssq_all, bg=True)
        S.barrier()
        T = TROWS if need_ctx else 2048
        sst = contextlib.ExitStack()
        lr = kb.ring("sl", 3, [128, TROWS], BF16, stack=sst)
        ar = kb.ring("sa", 2, [128, TROWS], BF16, stack=sst)
        for j in range(4):
            for i in range(4):
                acc, acc_r = ar.next()
                for qq in range(4):
                    t, t_r = lr.next()
                    dma(S, "sp", t[:, 0:T], yall[(i, qq)][j * 128:(j + 1) * 128, 0:T], writes=[t_r])
                    if qq == 0:
                        S.op("dve", lambda e: e.tensor_scalar(out=acc[:, :T], in0=t[:, :T], scalar1=oh[:, 0:1], scalar2=None, op0=ALU.mult), reads=[t_r, oh_r], writes=[acc_r])
                    else:
                        S.op("dve", lambda e: e.scalar_tensor_tensor(out=acc[:, :T], in0=t[:, :T], scalar=oh[:, qq:qq + 1], in1=acc[:, :T], op0=ALU.mult, op1=ALU.add),
                             reads=[t_r, oh_r, acc_r], writes=[acc_r])
                dma(S, "sp", ys_own[:, 4 * i + j, 0:T], acc[:, :T], reads=[acc_r])
        s4, s4_r = kb.sb("s4", [4, 4, TROWS], F32, stack=sst)
        s4a, s4a_r = kb.sb("s4a", [4, TROWS], F32, stack=sst)
        oh4, oh4_r = kb.sb("oh4", [4, 4], F32, stack=sst)
        dma(S, "sp", oh4[:], oneh.partition_broadcast(4), writes=[oh4_r])
        sflat = ssq_all.rearrange("(j c) p -> j (c p)", j=4)
        for qq in range(4):
            dma(S, "sp", s4[:, qq, 0:2048], sflat[:, qq * 2048:(qq + 1) * 2048], writes=[s4_r])
            dma(S, "sp", s4[:, qq, 2048:TROWS], sflat[:, NLAT + qq * 64:NLAT + (qq + 1) * 64], writes=[s4_r])
        S.op("dve", lambda e: e.tensor_scalar(out=s4a[:], in0=s4[:, 0, :], scalar1=oh4[:, 0:1], scalar2=None, op0=ALU.mult), reads=[s4_r, oh4_r], writes=[s4a_r])
        for qq in range(1, 4):
            S.op("dve", lambda e: e.scalar_tensor_tensor(out=s4a[:], in0=s4[:, qq, :], scalar=oh4[:, qq:qq + 1], in1=s4a[:], op0=ALU.mult, op1=ALU.add),
                 reads=[s4_r, oh4_r, s4a_r], writes=[s4a_r])
        dma(S, "sp", ssq_sel, s4a[:], reads=[s4a_r])
        S.barrier()
        sst.close()

        def fill_vt(vt, vt_r):
            S.op("dve", lambda e: e.tensor_copy(out=vt[:, :, 0:3], in_=gv[:, l, :, 1:4]), reads=[gv_r], writes=[vt_r])
            for z in range(2):
                for s_ in range(4):
                    S.op("dve", lambda e: e.tensor_copy(out=vt[:, :, 3 + 4 * z + s_], in_=cvm[:, l, 2 + s_, :, z]), reads=[cvm_r], writes=[vt_r])

        emit_g(kb, T, hT_own, ys_own, ssq_sel, xT, fill_vt, WB[l]["wg"], WB[l]["wb"], WB[l]["wo"], WB[l]["w1"], WB[l]["w2"], xT, False)
    emit_xT2out(kb, xT, out, cst, cst_r)
    return kb.finish()


def kernel(**inputs):
    inp = {k: np.ascontiguousarray(np.asarray(v)) for k, v in inputs.items()}
    nc = build_fused()
    cosT, sinT = rope_tables()
    consts = ssd_consts()
    cmask = swa_masks()
    gvec = np.stack([inp["g_pre_mix"], inp["g_post_mix"], inp["g_pre_mlp"], inp["g_post_mlp"]], -1)
    gvec = np.ascontiguousarray(gvec.reshape(DEPTH, 16, 128, 4).transpose(2, 0, 1, 3))
    bmT = np.ascontiguousarray(inp["b_mod"].reshape(DEPTH, 96, 128).transpose(0, 2, 1))
    per_layer = []
    for l in range(DEPTH):
        per_layer.append({
            "wg": np.ascontiguousarray(inp["w_in"][l][:, GATE_OFF:]),
            "wb": np.ascontiguousarray(inp["w_branch"][l].reshape(4 * 512, D)),
            "lamv": np.ascontiguousarray(np.stack([inp["lam_q1"][l], inp["lam_k1"][l], inp["lam_q2"][l], inp["lam_k2"][l]], 0)),
        })
    maps = []
    for core in range(8):
        b, q = divmod(core, 4)
        j = q
        cvec = np.stack([inp["c"][b], inp["c_ctx"]], 1)
        m = {
            "xrows": token_shard(inp["x"], inp["ctx"], b, q),
            "cT2": np.ascontiguousarray(cvec.reshape(16, 128, 2).transpose(1, 0, 2)),
            "wmod": np.ascontiguousarray(inp["w_mod"][:, :, q * 3072:(q + 1) * 3072]), "bmT": np.ascontiguousarray(bmT[:, :, q * 24:(q + 1) * 24]), "gvec": gvec,
            "oneh": np.eye(4, dtype=np.float32)[q].copy(),
            "cosT": cosT, "sinT": sinT, "bconsts": consts, "cmask": cmask,
        }
        fm, tm = x_weight_cols(j)
        for l in range(DEPTH):
            cw, cb, pv, gssm = ssd_params(inp, l, j)
            m.update({
                f"wfm{l}": np.ascontiguousarray(inp["w_in"][l][:, fm]), f"wtm{l}": np.ascontiguousarray(inp["w_in"][l][:, tm]),
                f"lamv{l}": per_layer[l]["lamv"], f"gsub{l}": np.ascontiguousarray(inp["g_subln"][l]),
                f"cw{l}": cw, f"cb{l}": cb, f"pv{l}": pv, f"gssm{l}": gssm,
                f"sinkv{l}": np.ascontiguousarray(inp["sink"][l][2 * j:2 * j + 2]),
                f"biasT{l}": na_bias(inp["rpb"][l], j),
                f"wg{l}": per_layer[l]["wg"], f"wb{l}": per_layer[l]["wb"], f"wo{l}": inp["w_out"][l],
                f"w1{l}": inp["w_ff1"][l], f"w2{l}": inp["w_ff2"][l],
            })
        maps.append(m)
    res = run_spmd(nc, maps)
    outp = np.zeros((2, NLAT, D), np.float32)
    for core in range(8):
        b, q = divmod(core, 4)
        outp[b, q * 2048:(q + 1) * 2048] = res[core]["out"]
    return outp
```

```python
import contextlib
import math
import numpy as np
import ml_dtypes
import concourse.bass as bass
import concourse.mybir as mybir
from concourse.bass_utils import run_bass_kernel_spmd

F32 = mybir.dt.float32
BF16 = mybir.dt.bfloat16
AF = mybir.ActivationFunctionType
ALU = mybir.AluOpType
NPBF = ml_dtypes.bfloat16

D = 2048
NLAT = 8192
NCTX = 256
NTOK = NLAT + NCTX
DEPTH = 2
GRID_W = 64
EPS = 1e-6
D_FF = 8192
IN_SPLITS = (512, 512, 512, 512, 1024, 16, 512, 128, 128, 512, 512, 512, 8192)
IN_OFF = np.concatenate([[0], np.cumsum(IN_SPLITS)]).astype(int)
GATE_OFF = int(IN_OFF[12])

SEM_EPOCH = 30000
DMA_POOL = 12


class Res:
    __slots__ = ("name", "lw", "rd", "ps")

    def __init__(self, name, ps=False):
        self.name = name
        self.lw = None
        self.rd = []
        self.ps = ps


class Sched:
    ENGS = ("pe", "act", "dve", "pool", "sp")

    def __init__(self, nc, stack):
        self.nc = nc
        self.stack = stack
        self.ops = {e: [] for e in self.ENGS}
        self.cnt = {e: 0 for e in self.ENGS}
        self.sem = {}
        self.semcnt = {e: 0 for e in self.ENGS}
        self.waited = {e: {} for e in self.ENGS}
        self.dma_pool = {}
        self.dma_i = {e: 0 for e in self.ENGS}
        self.nsem = 0
        self.finals = []
        self.allsems = {}
        self.eng = {"pe": nc.tensor, "act": nc.scalar, "dve": nc.vector, "pool": nc.gpsimd, "sp": nc.sync}
        for e in ("pe", "act", "dve", "pool"):
            self._new_sem(e)
        for q in ("sp", "pool"):
            self.dma_pool[q] = [[self._alloc_sem(f"d{q}{i}"), 0] for i in range(DMA_POOL)]
        self.dma_pool["cc"] = [[self._alloc_sem(f"dcc{i}"), 0] for i in range(DMA_POOL)]
        self.cc_ids = {id(sl[0]) for sl in self.dma_pool["cc"]}
        self.dma_i["cc"] = 0

    def _alloc_sem(self, name):
        self.nsem += 1
        s = self.stack.enter_context(self.nc.semaphore(f"{name}_{self.nsem}"))
        self.allsems[id(s)] = [s, 0]
        return s

    def _new_sem(self, e):
        self.sem[e] = self._alloc_sem(f"s{e}")
        self.semcnt[e] = 0

    def _need(self, eng, ev, waits, idx_now):
        if ev is None:
            return
        sem, val, peng, pidx, isdma = ev
        if not isdma and peng == eng:
            if eng == "pe":
                return
            if pidx < idx_now - 2:
                return
        w = self.waited[eng]
        k = id(sem)
        if w.get(k, -1) >= val:
            return
        w[k] = val
        waits.append((sem, val))

    def op(self, eng, fn, reads=(), writes=(), dma=False, final=False, dma_inc=16, pool=None):
        waits = []
        idx_now = self.cnt[eng]
        for r in reads:
            self._need(eng, r.lw, waits, idx_now)
            if r.ps:
                for ev in r.rd:
                    if ev[2] != eng:
                        self._need(eng, ev, waits, idx_now)
        for wr in writes:
            self._need(eng, wr.lw, waits, idx_now)
            for ev in wr.rd:
                self._need(eng, ev, waits, idx_now)
        if dma:
            pname = pool or eng
            slot = self.dma_pool[pname][self.dma_i[pname] % DMA_POOL]
            self.dma_i[pname] += 1
            sem = slot[0]
            if slot[1] > 0:
                self._need(eng, (sem, slot[1], eng, -1, True), waits, idx_now)
            slot[1] += dma_inc
            ev = (sem, slot[1], eng, -1, True)
            inc = dma_inc
        else:
            if self.semcnt[eng] >= SEM_EPOCH:
                self._new_sem(eng)
            self.semcnt[eng] += 1
            ev = (self.sem[eng], self.semcnt[eng], eng, idx_now, False)
            inc = 1
        self.allsems[id(ev[0])][1] = ev[1]
        self.cnt[eng] += 1
        e = self.eng[eng]
        for (s_, v_) in waits:
            e.wait_ge(s_, v_)
        fn(e).then_inc(ev[0], inc)
        for r in reads:
            if not dma:
                r.rd = [x for x in r.rd if x[4] or x[2] != eng]
            r.rd.append(ev)
        for wr in writes:
            wr.lw = ev
            wr.rd = []
        if final:
            self.finals.append(ev)
        return ev

    def barrier(self, include_cc=True):
        snap = [(s, v) for (s, v) in self.allsems.values() if v > 0 and (include_cc or id(s) not in self.cc_ids)]
        for e in self.ENGS:
            waits = []
            w = self.waited[e]
            for (s, v) in snap:
                if w.get(id(s), -1) < v:
                    w[id(s)] = v
                    waits.append((s, v))
            for (s_, v_) in waits:
                self.eng[e].wait_ge(s_, v_)

    def emit(self):
        seen = {}
        for ev in self.finals:
            k = id(ev[0])
            if k not in seen or seen[k][1] < ev[1]:
                seen[k] = (ev[0], ev[1])
        for (s_, v_) in seen.values():
            self.nc.sync.wait_ge(s_, v_)


class KB:
    def __init__(self):
        self.nc = bass.Bass("TRN2", target_bir_lowering=False)
        self.stack = contextlib.ExitStack()
        self.S = Sched(self.nc, self.stack)
        self.uid = 0

    def din(self, name, shape, dt=F32):
        return self.nc.dram_tensor(name, list(shape), dt, kind="ExternalInput").ap()

    def dout(self, name, shape, dt=F32):
        return self.nc.dram_tensor(name, list(shape), dt, kind="ExternalOutput").ap()

    def dscratch(self, name, shape, dt=F32):
        return self.nc.dram_tensor(name, list(shape), dt, kind="Internal").ap()

    def sb(self, name, shape, dt=F32, stack=None):
        self.uid += 1
        t = (stack or self.stack).enter_context(self.nc.sbuf_tensor(f"{name}_{self.uid}", list(shape), dt))
        return t, Res(name)

    def ps(self, name, shape, dt=F32, stack=None):
        self.uid += 1
        t = (stack or self.stack).enter_context(self.nc.psum_tensor(f"{name}_{self.uid}", list(shape), dt))
        return t, Res(name, ps=True)

    def ring(self, name, n, shape, dt=F32, psum=False, stack=None):
        f = self.ps if psum else self.sb
        return Ring([f(f"{name}{i}", shape, dt, stack=stack) for i in range(n)])

    def finish(self):
        self.S.emit()
        self.stack.close()
        return self.nc


class Ring:
    def __init__(self, items):
        self.items = items
        self.i = 0

    def next(self):
        it = self.items[self.i % len(self.items)]
        self.i += 1
        return it


def run_spmd(nc, in_maps):
    res = run_bass_kernel_spmd(nc, in_maps, core_ids=list(range(len(in_maps))))
    return res.results


def dma(S, q, out, in_, reads=(), writes=(), final=False):
    return S.op(q, lambda e: e.dma_start(out=out, in_=in_), reads=reads, writes=writes, dma=True, final=final)


def emit_rstd(S, kb, ss, ss_r, n, tmp=None):
    S.op("dve", lambda e: e.tensor_scalar(out=ss, in0=ss, scalar1=1.0 / n, scalar2=EPS, op0=ALU.mult, op1=ALU.add),
         reads=[ss_r], writes=[ss_r])
    S.op("act", lambda e: e.activation(out=ss, in_=ss, func=AF.Sqrt), reads=[ss_r], writes=[ss_r])
    S.op("dve", lambda e: e.reciprocal(out=ss, in_=ss), reads=[ss_r], writes=[ss_r])


def build_mod():
    kb = KB()
    S = kb.S
    NCOL = 12288 // 8
    cT = kb.din("cT", [128, 16, 3])
    wm = kb.din("wm", [DEPTH, D, NCOL])
    bm = kb.din("bm", [DEPTH, NCOL])
    out = kb.dout("mod", [DEPTH, 3, NCOL])
    ct, ct_r = kb.sb("ct", [128, 16, 3])
    st, st_r = kb.sb("st", [128, 16, 3])
    dma(S, "sp", ct[:], cT, writes=[ct_r])
    S.op("act", lambda e: e.activation(out=st[:], in_=ct[:], func=AF.Silu), reads=[ct_r], writes=[st_r])
    wring = kb.ring("w", 2, [128, 16, 512])
    pring = kb.ring("p", 2, [3, 512], psum=True)
    bt, bt_r = kb.sb("bt", [3, DEPTH, NCOL])
    ot, ot_r = kb.sb("ot", [3, DEPTH, NCOL])
    for l in range(DEPTH):
        dma(S, "sp", bt[:, l, :], bm[l, :].partition_broadcast(3), writes=[bt_r])
    for l in range(DEPTH):
        wv = wm[l].rearrange("(kc p) c -> p kc c", p=128)
        for nt in range(NCOL // 512):
            (w, w_r) = wring.next()
            (p, p_r) = pring.next()
            dma(S, "sp", w[:], wv[:, :, nt * 512:(nt + 1) * 512], writes=[w_r])
            for kc in range(16):
                S.op("pe", lambda e, w=w, p=p, kc=kc: e.matmul(p[:], lhsT=st[:, kc, :], rhs=w[:, kc, :], start=(kc == 0), stop=(kc == 15)),
                     reads=[st_r, w_r], writes=[p_r])
            S.op("dve", lambda e, p=p, l=l, nt=nt: e.tensor_tensor(out=ot[:, l, nt * 512:(nt + 1) * 512], in0=p[:], in1=bt[:, l, nt * 512:(nt + 1) * 512], op=ALU.add),
                 reads=[p_r, bt_r], writes=[ot_r])
    dma(S, "sp", out.rearrange("l j c -> j l c"), ot[:], reads=[ot_r], final=True)
    return kb.finish()


def run_mod(inp):
    cvec = np.stack([inp["c"][0], inp["c"][1], inp["c_ctx"]], 0)
    cT = np.ascontiguousarray(cvec.T.reshape(16, 128, 3).transpose(1, 0, 2))
    NCOL = 12288 // 8
    nc = build_mod()
    maps = []
    for core in range(8):
        sl = slice(core * NCOL, (core + 1) * NCOL)
        maps.append({"cT": cT, "wm": np.ascontiguousarray(inp["w_mod"][:, :, sl]), "bm": np.ascontiguousarray(inp["b_mod"][:, sl])})
    res = run_spmd(nc, maps)
    return np.concatenate([r["mod"] for r in res], axis=2)


TROWS = 2048 + 64


def build_p1():
    kb = KB()
    S = kb.S
    x = kb.din("x", [TROWS, D])
    vec = kb.din("vec", [5, D])
    h = kb.dout("h", [TROWS, D], BF16)
    vb, vb_r = kb.sb("vb", [128, 5, D])
    for i in range(5):
        dma(S, "sp", vb[:, i, :], vec[i, :].partition_broadcast(128), writes=[vb_r])
    gs, gs_r = kb.sb("gs", [128, 2, D])
    for i in range(2):
        S.op("dve", lambda e, i=i: e.scalar_tensor_tensor(out=gs[:, i, :], in0=vb[:, 2 + 2 * i, :], scalar=1.0, in1=vb[:, 0, :], op0=ALU.add, op1=ALU.mult),
             reads=[vb_r], writes=[gs_r])
    xr = kb.ring("x", 3, [128, D])
    sq, sq_r = kb.sb("sq", [128, D])
    tr = kb.ring("t", 2, [128, D])
    hr = kb.ring("h", 2, [128, D], BF16)
    ssr = kb.ring("ss", 3, [128, 1])
    tiles = [(i * 128, 128, 0) for i in range(16)] + [(2048, 64, 1)]
    for (r0, n, isc) in tiles:
        xt, xt_r = xr.next()
        ss, ss_r = ssr.next()
        t, t_r = tr.next()
        ht, ht_r = hr.next()
        dma(S, "sp", xt[:n, :], x[r0:r0 + n, :], writes=[xt_r])
        S.op("pool", lambda e, ss=ss: e.memset(ss[:], 0.0), writes=[ss_r])
        S.op("act", lambda e, xt=xt, ss=ss, n=n: e.activation(out=sq[:n, :], in_=xt[:n, :], func=AF.Square, accum_out=ss[:n, :]),
             reads=[xt_r], writes=[sq_r, ss_r])
        emit_rstd(S, kb, ss[:n, :], ss_r, D)
        S.op("dve", lambda e, t=t, xt=xt, ss=ss, n=n, isc=isc: e.scalar_tensor_tensor(out=t[:n, :], in0=xt[:n, :], scalar=ss[:n, 0:1], in1=gs[:n, isc, :], op0=ALU.mult, op1=ALU.mult),
             reads=[xt_r, ss_r, gs_r], writes=[t_r])
        S.op("pool", lambda e, t=t, ht=ht, n=n, isc=isc: e.tensor_tensor(out=ht[:n, :], in0=t[:n, :], in1=vb[:n, 1 + 2 * isc, :], op=ALU.add),
             reads=[t_r, vb_r], writes=[ht_r])
        dma(S, "sp", h[r0:r0 + n, :], ht[:n, :], reads=[ht_r], final=True)
    return kb.finish()


def token_shard(xfull, cfull, b, q):
    return np.concatenate([xfull[b, q * 2048:(q + 1) * 2048], cfull[b, q * 64:(q + 1) * 64]], 0)


def run_p1(xcur, cxcur, mod_l, gvec, shift_i, scale_i):
    nc = build_p1()
    maps = []
    for core in range(8):
        b, q = divmod(core, 4)
        vec = np.stack([gvec, mod_l[b, shift_i], mod_l[b, scale_i], mod_l[2, shift_i], mod_l[2, scale_i]], 0)
        maps.append({"x": token_shard(xcur, cxcur, b, q), "vec": np.ascontiguousarray(vec)})
    res = run_spmd(nc, maps)
    hl = np.zeros((2, NLAT, D), NPBF)
    hc = np.zeros((2, NCTX, D), NPBF)
    for core in range(8):
        b, q = divmod(core, 4)
        hh = res[core]["h"]
        hl[b, q * 2048:(q + 1) * 2048] = hh[:2048]
        hc[b, q * 64:(q + 1) * 64] = hh[2048:]
    return hl, hc


NFM = 13
NTM = 452


def x_weight_cols(j):
    o = IN_OFF
    kv = j // 2
    g = j // 2

    def swap64(cols):
        cols = np.asarray(cols).reshape(-1, 64)
        return np.concatenate([cols[:, 32:], cols[:, :32]], 1).reshape(-1)
    r = np.arange
    aq = o[0] + j * 128 + r(128)
    ak = o[1] + j * 128 + r(128)
    cq = o[6] + 2 * j * 64 + r(128)
    ck = np.concatenate([o[7] + kv * 64 + r(64)] * 2)
    dq = o[9] + 2 * j * 64 + r(128)
    dk = o[10] + 2 * j * 64 + r(128)
    bx = o[4] + 2 * j * 64 + r(128)
    bb = o[4] + 512 + g * 128 + r(128)
    bc = o[4] + 768 + g * 128 + r(128)
    fm = np.concatenate([aq, swap64(aq), ak, swap64(ak), cq, swap64(cq), ck, swap64(ck), dq, dk, bx, bb, bc])
    av = o[2] + j * 128 + r(128)
    cv = o[8] + kv * 64 + r(64)
    dv = o[11] + 2 * j * 64 + r(128)
    bz = o[3] + 2 * j * 64 + r(128)
    bdt = o[5] + np.array([2 * j, 2 * j + 1, 8 + 2 * j, 8 + 2 * j + 1])
    tm = np.concatenate([av, cv, dv, bz, bdt])
    return fm.astype(int), tm.astype(int)


def rope_tables():
    t = np.arange(NLAT)
    row = (t // GRID_W).astype(np.float32)
    col = (t % GRID_W).astype(np.float32)
    nf = 16
    inv = (10000.0 ** (-np.arange(nf, dtype=np.float32) / nf)).astype(np.float32)
    ang = np.concatenate([row[:, None] * inv, col[:, None] * inv], -1)
    cos = np.cos(ang).T.astype(np.float32)
    sin = np.sin(ang).T.astype(np.float32)
    cosT = np.concatenate([cos, cos, cos, cos], 0)
    sinT = np.concatenate([-sin, sin, -sin, sin], 0)
    return np.ascontiguousarray(cosT), np.ascontiguousarray(sinT)


class XScratch:
    pass


P2_DEBUG = {}


def emit_p2(kb, hT, wfm, wtm, cosT, sinT, sc, after_wload=None):
    S = kb.S
    st = contextlib.ExitStack()
    W, W_r = kb.sb("W", [128, 16, NFM * 128 + NTM], BF16, stack=st)
    wv = wfm.rearrange("(kc p) c -> p kc c", p=128)
    wv2 = wtm.rearrange("(kc p) c -> p kc c", p=128)
    for kc in range(16):
        dma(S, "pool", W[:, kc, 0:NFM * 128], wv[:, kc, :], writes=[W_r])
        dma(S, "pool", W[:, kc, NFM * 128:], wv2[:, kc, :], writes=[W_r])
    if after_wload is not None:
        after_wload()
    hr = kb.ring("hT", 2, [128, 16, 512], BF16, stack=st)
    cr = kb.ring("cs", 2, [128, 2, 512], F32, stack=st)
    pr = kb.ring("pp", 6, [128, 512], F32, psum=True, stack=st)
    t1r = kb.ring("t1", 2, [128, 512], F32, stack=st)
    t2r = kb.ring("t2", 2, [128, 512], F32, stack=st)
    obr = kb.ring("ob", 3, [128, 512], BF16, stack=st)
    ofr = kb.ring("of", 3, [128, 512], F32, stack=st)
    tvr = kb.ring("tv", 2, [128, 320], BF16, stack=st)
    tzr = kb.ring("tz", 2, [128, 132], F32, stack=st)
    tiles = [(i * 512, 512, False) for i in range(16)] + [(NLAT, 256, True)]
    roped = [(0, sc.A_qT, 128), (2, sc.A_kT, 128), (4, sc.C_qT, 128), (6, sc.C_kT, 64)]
    plain = [(8, sc.D_qT), (9, sc.D_kT)]
    def load(ti):
        (t0, n, isctx) = tiles[ti]
        h, h_r = hr.next()
        if callable(hT):
            hT(h, h_r, t0, n, isctx)
        else:
            dma(S, "sp", h[:, :, :n], hT[:, :, t0:t0 + n], writes=[h_r])
        cs, cs_r = cr.next()
        if not isctx:
            dma(S, "sp", cs[:, 0, :n], cosT[:, t0:t0 + n], writes=[cs_r])
            dma(S, "sp", cs[:, 1, :n], sinT[:, t0:t0 + n], writes=[cs_r])
        return h, h_r, cs, cs_r
    nxt = load(0)
    for ti, (t0, n, isctx) in enumerate(tiles):
        h, h_r, cs, cs_r = nxt
        if ti + 1 < len(tiles):
            nxt = load(ti + 1)

        def proj(g):
            p, p_r = pr.next()
            for k in range(16):
                S.op("pe", lambda e, p=p, k=k, g=g: e.matmul(p[:, :n], lhsT=W[:, k, g * 128:(g + 1) * 128], rhs=h[:, k, :n], start=(k == 0), stop=(k == 15)),
                     reads=[W_r, h_r], writes=[p_r])
            return p, p_r
        if P2_DEBUG.get("stop") == 1:
            continue
        for (g, dst, rows) in (roped if P2_DEBUG.get("stop") != 3 else []):
            pa, pa_r = proj(g)
            ob, ob_r = obr.next()
            if isctx:
                S.op("act", lambda e, pa=pa, ob=ob: e.copy(out=ob[:, :n], in_=pa[:, :n]), reads=[pa_r], writes=[ob_r])
            else:
                pb, pb_r = proj(g + 1)
                t1, t1_r = t1r.next()
                t2, t2_r = t2r.next()
                S.op("dve", lambda e, pa=pa, t1=t1: e.tensor_tensor(out=t1[:, :n], in0=pa[:, :n], in1=cs[:, 0, :n], op=ALU.mult), reads=[pa_r, cs_r], writes=[t1_r])
                S.op("dve", lambda e, pb=pb, t2=t2: e.tensor_tensor(out=t2[:, :n], in0=pb[:, :n], in1=cs[:, 1, :n], op=ALU.mult), reads=[pb_r, cs_r], writes=[t2_r])
                S.op("pool", lambda e, t1=t1, t2=t2, ob=ob: e.tensor_tensor(out=ob[:, :n], in0=t1[:, :n], in1=t2[:, :n], op=ALU.add), reads=[t1_r, t2_r], writes=[ob_r])
            dma(S, "sp", dst[:rows, t0:t0 + n], ob[:rows, :n], reads=[ob_r])
        if P2_DEBUG.get("stop") == 2:
            continue
        for (g, dst) in (plain if P2_DEBUG.get("stop") != 3 else []):
            pa, pa_r = proj(g)
            ob, ob_r = obr.next()
            S.op("act", lambda e, pa=pa, ob=ob: e.copy(out=ob[:, :n], in_=pa[:, :n]), reads=[pa_r], writes=[ob_r])
            dma(S, "sp", dst[:, t0:t0 + n], ob[:, :n], reads=[ob_r])
        for i in range(3 if P2_DEBUG.get("stop") != 3 else 0):
            pa, pa_r = proj(10 + i)
            of, of_r = ofr.next()
            S.op("act", lambda e, pa=pa, of=of: e.copy(out=of[:, :n], in_=pa[:, :n]), reads=[pa_r], writes=[of_r])
            dma(S, "sp", sc.B_xbcT[i, :, t0:t0 + n], of[:, :n], reads=[of_r])
        if P2_DEBUG.get("stop") == 4:
            continue
        for s in range(n // 128):
            p, p_r = pr.next()
            for k in range(16):
                S.op("pe", lambda e, p=p, k=k, s=s: e.matmul(p[:, :NTM], lhsT=h[:, k, s * 128:(s + 1) * 128], rhs=W[:, k, NFM * 128:], start=(k == 0), stop=(k == 15)),
                     reads=[W_r, h_r], writes=[p_r])
            tv, tv_r = tvr.next()
            tz, tz_r = tzr.next()
            S.op("act", lambda e, p=p, tv=tv: e.copy(out=tv[:, :], in_=p[:, 0:320]), reads=[p_r], writes=[tv_r])
            S.op("act", lambda e, p=p, tz=tz: e.copy(out=tz[:, :], in_=p[:, 320:452]), reads=[p_r], writes=[tz_r])
            r0 = t0 + s * 128
            dma(S, "sp", sc.TMV[r0:r0 + 128, :], tv[:, :], reads=[tv_r])
            dma(S, "sp", sc.TMZ[r0:r0 + 128, :], tz[:, :], reads=[tz_r])
    S.barrier()
    st.close()


def make_xscratch(kb, as_output=False):
    sc = XScratch()
    mk = kb.dout if as_output else kb.dscratch
    sc.A_qT = mk("A_qT", [128, NTOK], BF16)
    sc.A_kT = mk("A_kT", [128, NTOK], BF16)
    sc.C_qT = mk("C_qT", [128, NTOK], BF16)
    sc.C_kT = mk("C_kT", [64, NTOK], BF16)
    sc.D_qT = mk("D_qT", [128, NTOK], BF16)
    sc.D_kT = mk("D_kT", [128, NTOK], BF16)
    sc.TMV = mk("TMV", [NTOK, 320], BF16)
    sc.TMZ = mk("TMZ", [NTOK, 132], F32)
    sc.A_v = sc.TMV[:, 0:128]
    sc.C_v = sc.TMV[:, 128:192]
    sc.D_v = sc.TMV[:, 192:320]
    sc.B_z = sc.TMZ[:, 0:128]
    sc.B_dt = sc.TMZ[:, 128:132]
    sc.B_xbcT = mk("B_xbcT", [3, 128, NTOK], F32)
    sc.r_fm = Res("sc_fm")
    sc.r_tm = Res("sc_tm")
    return sc


def emit_mixA(kb, sc, lamv, gsub, lam_init, need_ctx, yA):
    S = kb.S
    st = contextlib.ExitStack()
    QT, QT_r = kb.sb("QT", [64, 2, NTOK], BF16, stack=st)
    KT, KT_r = kb.sb("KT", [64, 2, NTOK], BF16, stack=st)
    V, V_r = kb.sb("V", [128, 66, 129], BF16, stack=st)
    for m in range(2):
        dma(S, "sp", QT[:, m, :], sc.A_qT[m * 64:(m + 1) * 64, :], writes=[QT_r])
        dma(S, "sp", KT[:, m, :], sc.A_kT[m * 64:(m + 1) * 64, :], writes=[KT_r])
    S.op("pool", lambda e: e.memset(V[:, :, 128:129], 1.0), writes=[V_r])
    avv = sc.A_v.rearrange("(t p) c -> p t c", p=128)
    for t in range(0, 66, 6):
        dma(S, "sp", V[:, t:t + 6, 0:128], avv[:, t:t + 6, :], writes=[V_r])
    lv, lv_r = kb.sb("lv", [128, 4, 64], stack=st)
    for i in range(4):
        dma(S, "sp", lv[:, i, :], lamv[i, :].partition_broadcast(128), writes=[lv_r])
    lp, lp_r = kb.sb("lp", [128, 2, 64], stack=st)
    ls, ls_r = kb.sb("ls", [128, 2], stack=st)
    nlam, nlam_r = kb.sb("nlam", [128, 1], stack=st)
    for i in range(2):
        S.op("dve", lambda e: e.tensor_tensor(out=lp[:, i, :], in0=lv[:, 2 * i, :], in1=lv[:, 2 * i + 1, :], op=ALU.mult), reads=[lv_r], writes=[lp_r])
    S.op("dve", lambda e: e.reduce_sum(out=ls[:], in_=lp[:], axis=mybir.AxisListType.X), reads=[lp_r], writes=[ls_r])
    S.op("act", lambda e: e.activation(out=ls[:], in_=ls[:], func=AF.Exp), reads=[ls_r], writes=[ls_r])
    S.op("dve", lambda e: e.tensor_tensor(out=nlam[:], in0=ls[:, 1:2], in1=ls[:, 0:1], op=ALU.subtract), reads=[ls_r], writes=[nlam_r])
    S.op("dve", lambda e: e.tensor_scalar(out=nlam[:], in0=nlam[:], scalar1=-float(lam_init), scalar2=None, op0=ALU.add), reads=[nlam_r], writes=[nlam_r])
    gb, gb_r = kb.sb("gb", [128, 128], stack=st)
    dma(S, "sp", gb[:], gsub.partition_broadcast(128), writes=[gb_r])
    S.op("dve", lambda e: e.tensor_scalar(out=gb[:], in0=gb[:], scalar1=float(1.0 - lam_init), scalar2=None, op0=ALU.mult), reads=[gb_r], writes=[gb_r])

    Sr = kb.ring("S", 2, [128, 2, 256], F32, psum=True, stack=st)
    Or = [[kb.ps(f"O{m}{s}", [128, 512], F32, stack=st) for s in range(2)] for m in range(2)]
    Pr = kb.ring("P", 3, [128, 2, 256], BF16, stack=st)
    rcr = kb.ring("rc", 2, [128, 2], stack=st)
    o1r = kb.ring("o1", 2, [128, 128], stack=st)
    o2r = kb.ring("o2", 2, [128, 128], stack=st)
    sqr = kb.ring("sq", 2, [128, 128], stack=st)
    ssr = kb.ring("ss", 2, [128, 1], stack=st)
    yr = kb.ring("y", 3, [128, 128], BF16, stack=st)
    jobs = [(qt * 256, list(range(66))) for qt in range(32)]
    if need_ctx:
        jobs.append((NLAT, [64, 65]))
    for (q0, kts) in jobs:
        def pv(ki, kt, P, P_r):
            for m in range(2):
                for s in range(2):
                    O, O_r = Or[m][s]
                    S.op("pe", lambda e: e.matmul(O[:, 0:129], lhsT=P[:, m, s * 128:(s + 1) * 128], rhs=V[:, kt, :], start=(ki == 0), stop=(ki == len(kts) - 1)),
                         reads=[P_r, V_r], writes=[O_r])
        pend = None
        for ki, kt in enumerate(kts):
            Sp, Sp_r = Sr.next()
            for m in range(2):
                S.op("pe", lambda e: e.matmul(Sp[:, m, :], lhsT=KT[:, m, kt * 128:(kt + 1) * 128], rhs=QT[:, m, q0:q0 + 256], start=True, stop=True, skip_group_check=True),
                     reads=[KT_r, QT_r], writes=[Sp_r])
            P, P_r = Pr.next()
            S.op("act", lambda e: e.activation(out=P[:], in_=Sp[:], func=AF.Exp, scale=0.125), reads=[Sp_r], writes=[P_r])
            if pend is not None:
                pv(*pend)
            pend = (ki, kt, P, P_r)
        pv(*pend)
        for s in range(2):
            rc, rc_r = rcr.next()
            for m in range(2):
                S.op("dve", lambda e: e.reciprocal(out=rc[:, m:m + 1], in_=Or[m][s][0][:, 128:129]), reads=[Or[m][s][1]], writes=[rc_r])
            S.op("dve", lambda e: e.tensor_tensor(out=rc[:, 1:2], in0=rc[:, 1:2], in1=nlam[:], op=ALU.mult), reads=[rc_r, nlam_r], writes=[rc_r])
            o1, o1_r = o1r.next()
            o2, o2_r = o2r.next()
            S.op("act", lambda e: e.activation(out=o1[:], in_=Or[0][s][0][:, 0:128], func=AF.Copy, scale=rc[:, 0:1]), reads=[Or[0][s][1], rc_r], writes=[o1_r])
            S.op("dve", lambda e: e.scalar_tensor_tensor(out=o2[:], in0=Or[1][s][0][:, 0:128], scalar=rc[:, 1:2], in1=o1[:], op0=ALU.mult, op1=ALU.add),
                 reads=[Or[1][s][1], rc_r, o1_r], writes=[o2_r])
            sq, sq_r = sqr.next()
            ss, ss_r = ssr.next()
            S.op("pool", lambda e: e.memset(ss[:], 0.0), writes=[ss_r])
            S.op("act", lambda e: e.activation(out=sq[:], in_=o2[:], func=AF.Square, accum_out=ss[:]), reads=[o2_r], writes=[sq_r, ss_r])
            emit_rstd(S, kb, ss[:], ss_r, 128)
            y, y_r = yr.next()
            S.op("dve", lambda e: e.scalar_tensor_tensor(out=y[:], in0=o2[:], scalar=ss[:, 0:1], in1=gb[:], op0=ALU.mult, op1=ALU.mult),
                 reads=[o2_r, ss_r, gb_r], writes=[y_r])
            r0 = q0 + s * 128
            if callable(yA):
                yA(y, y_r, r0, 128)
            else:
                dma(S, "sp", yA[r0:r0 + 128, :], y[:], reads=[y_r], final=True)
    S.barrier(include_cc=False)
    st.close()


def build_x(lam_init, need_ctx, parts=("A", "B", "C", "D"), debug=False):
    kb = KB()
    hT = kb.din("hT", [128, 16, NTOK], BF16)
    wfm = kb.din("wfm", [D, NFM * 128])
    wtm = kb.din("wtm", [D, NTM])
    cosT = kb.din("cosT", [128, NLAT])
    sinT = kb.din("sinT", [128, NLAT])
    sc = make_xscratch(kb, as_output=debug)
    emit_p2(kb, hT, wfm, wtm, cosT, sinT, sc)
    if "A" in parts:
        lamv = kb.din("lamv", [4, 64])
        gsub = kb.din("gsub", [128])
        yA = kb.dout("yA", [NTOK, 128], BF16)
        emit_mixA(kb, sc, lamv, gsub, lam_init, need_ctx, yA)
    if "B" in parts:
        consts = kb.din("bconsts", [128, 6, 128])
        cw = kb.din("cw", [128, 3, 5])
        cb = kb.din("cb", [128, 3])
        pv = kb.din("pv", [10])
        gssm = kb.din("gssm", [128])
        yB = kb.dout("yB", [NTOK, 128], BF16)
        ssqB = kb.dout("ssqB", [128, 66])
        emit_mixB(kb, sc, consts, cw, cb, pv, gssm, need_ctx, yB, ssqB)
    if "C" in parts:
        sinkv = kb.din("sinkv", [2])
        cmask = kb.din("cmask", [128, 2, 256], BF16)
        yC = kb.dout("yC", [NTOK, 128], BF16)
        emit_mixC(kb, sc, sinkv, cmask, need_ctx, yC)
    if "D" in parts:
        biasT = kb.din("biasT", [128, 2, 8, 4, 64])
        yD = kb.dout("yD", [NTOK, 128], BF16)
        emit_mixD(kb, sc, biasT, need_ctx, yD)
    return kb.finish()


def make_hT(hl, hc, b):
    hh = np.concatenate([hl[b], hc[b]], 0)
    return np.ascontiguousarray(hh.T.reshape(16, 128, NTOK).transpose(1, 0, 2))


def emit_mixC(kb, sc, sinkv, cmask, need_ctx, yC):
    S = kb.S
    st = contextlib.ExitStack()
    QT, QT_r = kb.sb("cQT", [64, 2, NTOK], BF16, stack=st)
    KT, KT_r = kb.sb("cKT", [64, NTOK], BF16, stack=st)
    V, V_r = kb.sb("cV", [128, 66, 65], BF16, stack=st)
    for h in range(2):
        dma(S, "sp", QT[:, h, :], sc.C_qT[h * 64:(h + 1) * 64, :], writes=[QT_r])
    dma(S, "sp", KT[:], sc.C_kT[:, :], writes=[KT_r])
    S.op("pool", lambda e: e.memset(V[:, :, 64:65], 1.0), writes=[V_r])
    cvv = sc.C_v.rearrange("(t p) c -> p t c", p=128)
    for t in range(0, 66, 6):
        dma(S, "sp", V[:, t:t + 6, 0:64], cvv[:, t:t + 6, :], writes=[V_r])
    mk, mk_r = kb.sb("cmk", [128, 2, 256], BF16, stack=st)
    dma(S, "sp", mk[:], cmask, writes=[mk_r])
    es, es_r = kb.sb("ces", [128, 2], stack=st)
    dma(S, "sp", es[:], sinkv.partition_broadcast(128), writes=[es_r])
    S.op("act", lambda e: e.activation(out=es[:], in_=es[:], func=AF.Exp), reads=[es_r], writes=[es_r])
    Sr = kb.ring("cS", 2, [128, 512], F32, psum=True, stack=st)
    Or = kb.ring("cO", 4, [128, 512], F32, psum=True, stack=st)
    Pr = kb.ring("cP", 3, [128, 256], BF16, stack=st)
    Pmr = kb.ring("cPm", 3, [128, 256], BF16, stack=st)
    lr = kb.ring("cl", 4, [128, 1], stack=st)
    yr = kb.ring("cy", 3, [128, 128], BF16, stack=st)
    blocks = list(range(64)) + ([64, 65] if need_ctx else [])
    for n in blocks:
        q0 = n * 128
        if n < 64:
            kts = ([(n - 1, 0)] if n > 0 else []) + [(n, None)] + ([(n + 1, 1)] if n < 63 else []) + [(64, None), (65, None)]
        else:
            kts = [(64, None), (65, None)]
        Os = [Or.next(), Or.next()]

        def pvc(ki, kt, P, P_r):
            for h in range(2):
                O, O_r = Os[h]
                S.op("pe", lambda e: e.matmul(O[:, 0:65], lhsT=P[:, h * 128:(h + 1) * 128], rhs=V[:, kt, :], start=(ki == 0), stop=(ki == len(kts) - 1)),
                     reads=[P_r, V_r], writes=[O_r])
        pend = None
        for ki, (kt, msk) in enumerate(kts):
            Sp, Sp_r = Sr.next()
            for h in range(2):
                S.op("pe", lambda e: e.matmul(Sp[:, h * 128:(h + 1) * 128], lhsT=KT[:, kt * 128:(kt + 1) * 128], rhs=QT[:, h, q0:q0 + 128], start=True, stop=True, skip_group_check=True),
                     reads=[KT_r, QT_r], writes=[Sp_r])
            P, P_r = Pr.next()
            S.op("act", lambda e: e.activation(out=P[:], in_=Sp[:, 0:256], func=AF.Exp, scale=0.125), reads=[Sp_r], writes=[P_r])
            if msk is not None:
                Pm, Pm_r = Pmr.next()
                S.op("pool", lambda e: e.tensor_tensor(out=Pm[:], in0=P[:], in1=mk[:, msk, :], op=ALU.mult), reads=[P_r, mk_r], writes=[Pm_r])
                P, P_r = Pm, Pm_r
            if pend is not None:
                pvc(*pend)
            pend = (ki, kt, P, P_r)
        pvc(*pend)
        y, y_r = yr.next()
        for h in range(2):
            O, O_r = Os[h]
            l_, l_r = lr.next()
            S.op("dve", lambda e: e.tensor_tensor(out=l_[:], in0=O[:, 64:65], in1=es[:, h:h + 1], op=ALU.add), reads=[O_r, es_r], writes=[l_r])
            S.op("dve", lambda e: e.reciprocal(out=l_[:], in_=l_[:]), reads=[l_r], writes=[l_r])
            S.op("act", lambda e: e.activation(out=y[:, h * 64:(h + 1) * 64], in_=O[:, 0:64], func=AF.Copy, scale=l_[:, 0:1]), reads=[O_r, l_r], writes=[y_r])
        if callable(yC):
            yC(y, y_r, q0, 128)
        else:
            dma(S, "sp", yC[q0:q0 + 128, :], y[:], reads=[y_r], final=True)
    S.barrier(include_cc=False)
    st.close()


def swa_masks():
    j = np.arange(128)[:, None]
    i = np.arange(128)[None, :]
    lo = (i <= j).astype(np.float32)
    hi = (j <= i).astype(np.float32)
    m = np.stack([np.concatenate([lo, lo], 1), np.concatenate([hi, hi], 1)], 1)
    return np.ascontiguousarray(m.astype(NPBF))


NA_PAT_ROWS = [64, 0, 1, 2, 3, 125, 126, 127]


def na_pattern(r):
    if 4 <= r <= 124:
        return 0
    return 1 + r if r < 4 else 5 + (r - 125)


def na_bias(rpb_l, j):
    out = np.full((128, 2, 8, 4, 64), -30000.0, np.float32)
    qc = np.arange(64)
    cs = np.clip(qc - 8, 0, 48)
    kc = np.arange(64)
    valid = (kc[:, None] >= cs[None, :]) & (kc[:, None] < cs[None, :] + 16)
    co = np.clip(kc[:, None] - qc[None, :] + 15, 0, 30)
    for h2 in range(2):
        head = 2 * j + h2
        for p, r in enumerate(NA_PAT_ROWS):
            rs = int(np.clip(r - 4, 0, 120))
            for t in range(4):
                for j2 in range(2):
                    ro = rs + 2 * t + j2 - r + 7
                    vals = rpb_l[head, ro][co]
                    blk = out[j2 * 64:(j2 + 1) * 64, h2, p, t, :]
                    blk[valid] = vals[valid]
    return out


def emit_mixD(kb, sc, biasT, need_ctx, yD):
    S = kb.S
    st = contextlib.ExitStack()
    QT, QT_r = kb.sb("dQT", [64, 2, NTOK], BF16, stack=st)
    KT, KT_r = kb.sb("dKT", [64, 2, NTOK], BF16, stack=st)
    Ve, Ve_r = kb.sb("dVe", [128, 66, 2, 65], BF16, stack=st)
    Vo, Vo_r = kb.sb("dVo", [128, 63, 2, 65], BF16, stack=st)
    bt, bt_r = kb.sb("dbt", [128, 2, 8, 4, 64], F32, stack=st)
    for h in range(2):
        dma(S, "sp", QT[:, h, :], sc.D_qT[h * 64:(h + 1) * 64, :], writes=[QT_r])
        dma(S, "sp", KT[:, h, :], sc.D_kT[h * 64:(h + 1) * 64, :], writes=[KT_r])
    S.op("pool", lambda e: e.memset(Ve[:], 1.0), writes=[Ve_r])
    S.op("pool", lambda e: e.memset(Vo[:], 1.0), writes=[Vo_r])
    for h in range(2):
        dve_ = sc.D_v[:, h * 64:(h + 1) * 64].rearrange("(t p) c -> p t c", p=128)
        dvo_ = sc.D_v[64:64 + 63 * 128, h * 64:(h + 1) * 64].rearrange("(t p) c -> p t c", p=128)
        for t in range(0, 66, 6):
            dma(S, "sp", Ve[:, t:t + 6, h, 0:64], dve_[:, t:t + 6, :], writes=[Ve_r])
        for t in range(0, 63, 7):
            dma(S, "sp", Vo[:, t:t + 7, h, 0:64], dvo_[:, t:t + 7, :], writes=[Vo_r])
    for h in range(2):
        for p_ in range(8):
            dma(S, "sp", bt[:, h, p_], biasT[:, h, p_], writes=[bt_r])
    Sr = kb.ring("dS", 3, [128, 512], F32, psum=True, stack=st)
    Or = kb.ring("dO", 3, [128, 512], F32, psum=True, stack=st)
    tr = kb.ring("dt", 3, [128, 4, 64], F32, stack=st)
    Pr = kb.ring("dP", 3, [128, 6, 64], BF16, stack=st)
    rr = kb.ring("dr", 4, [64, 1], stack=st)
    yr = kb.ring("dy", 3, [64, 128], BF16, stack=st)
    rows = [(r, False) for r in range(128)] + ([(c, True) for c in range(4)] if need_ctx else [])

    def stage2(P, P_r, ktl, nk, h, y, y_r, q0):
        O, O_r = Or.next()
        for ti, (_, (Vt, Vt_r, vi)) in enumerate(ktl):
            S.op("pe", lambda e: e.matmul(O[0:64, 0:65], lhsT=P[:, ti, :], rhs=Vt[:, vi, h, :], start=(ti == 0), stop=(ti == nk - 1)),
                 reads=[P_r, Vt_r], writes=[O_r])
        rc, rc_r = rr.next()
        S.op("dve", lambda e: e.reciprocal(out=rc[:], in_=O[0:64, 64:65]), reads=[O_r], writes=[rc_r])
        S.op("act", lambda e: e.activation(out=y[:, h * 64:(h + 1) * 64], in_=O[0:64, 0:64], func=AF.Copy, scale=rc[:, 0:1]), reads=[O_r, rc_r], writes=[y_r])
        if h == 1:
            if callable(yD):
                yD(y, y_r, q0, 64)
            else:
                dma(S, "sp", yD[q0:q0 + 64, :], y[:], reads=[y_r], final=True)
    pend = None
    for (r, isctx) in rows:
        y, y_r = yr.next()
        if isctx:
            q0 = NLAT + r * 64
            ktl = [(NLAT, (Ve, Ve_r, 64)), (NLAT + 128, (Ve, Ve_r, 65))]
            pat = None
        else:
            q0 = r * 64
            rs = int(np.clip(r - 4, 0, 120))
            pat = na_pattern(r)
            ktl = []
            for t in range(4):
                if rs % 2 == 0:
                    ktl.append((rs * 64 + t * 128, (Ve, Ve_r, rs // 2 + t)))
                else:
                    ktl.append((rs * 64 + t * 128, (Vo, Vo_r, (rs - 1) // 2 + t)))
            ktl += [(NLAT, (Ve, Ve_r, 64)), (NLAT + 128, (Ve, Ve_r, 65))]
        nk = len(ktl)
        for h in range(2):
            Sp, Sp_r = Sr.next()
            for ti, (k0, _) in enumerate(ktl):
                S.op("pe", lambda e: e.matmul(Sp[:, ti * 64:(ti + 1) * 64], lhsT=KT[:, h, k0:k0 + 128], rhs=QT[:, h, q0:q0 + 64], start=True, stop=True, skip_group_check=True),
                     reads=[KT_r, QT_r], writes=[Sp_r])
            P, P_r = Pr.next()
            if not isctx:
                tmp, tmp_r = tr.next()
                S.op("dve", lambda e: e.scalar_tensor_tensor(out=tmp[:], in0=Sp[:, 0:256].rearrange("p (t q) -> p t q", t=4), scalar=0.125, in1=bt[:, h, pat], op0=ALU.mult, op1=ALU.add),
                     reads=[Sp_r, bt_r], writes=[tmp_r])
                S.op("act", lambda e: e.activation(out=P[:, 0:4, :], in_=tmp[:], func=AF.Exp), reads=[tmp_r], writes=[P_r])
                S.op("act", lambda e: e.activation(out=P[:, 4:6, :], in_=Sp[:, 256:384].rearrange("p (t q) -> p t q", t=2), func=AF.Exp, scale=0.125), reads=[Sp_r], writes=[P_r])
            else:
                S.op("act", lambda e: e.activation(out=P[:, 0:2, :], in_=Sp[:, 0:128].rearrange("p (t q) -> p t q", t=2), func=AF.Exp, scale=0.125), reads=[Sp_r], writes=[P_r])
            if pend is not None:
                stage2(*pend)
            pend = (P, P_r, ktl, nk, h, y, y_r, q0)
    if pend is not None:
        stage2(*pend)
    S.barrier(include_cc=False)
    st.close()


def ssd_consts():
    s = np.arange(128)[:, None]
    l_ = np.arange(128)[None, :]
    triF = (s <= l_).astype(np.float32)
    triB = (s >= l_).astype(np.float32)
    c = np.stack([triF, triB, (triF - 1.0) * 30000.0, (triB - 1.0) * 30000.0, np.ones((128, 128), np.float32), np.eye(128, dtype=np.float32)], 1)
    return np.ascontiguousarray(c.astype(np.float32))


B_DEBUG = {}


def emit_mixB(kb, sc, consts, cw, cb, pv, gssm, need_ctx, yB, ssqB):
    S = kb.S
    AXX = mybir.AxisListType.X
    st = contextlib.ExitStack()
    NCH = 66
    cst, cst_r = kb.sb("bcst", [128, 6, 128], stack=st)
    dma(S, "sp", cst[:], consts, writes=[cst_r])
    triF, triB, mskF, mskB, ones, ident = (cst[:, i, :] for i in range(6))
    cwt, cwt_r = kb.sb("bcw", [128, 3, 5], stack=st)
    cbt, cbt_r = kb.sb("bcb", [128, 3], stack=st)
    dma(S, "sp", cwt[:], cw, writes=[cwt_r])
    dma(S, "sp", cbt[:], cb, writes=[cbt_r])
    pvt, pvt_r = kb.sb("bpv", [128, 10], stack=st)
    dma(S, "sp", pvt[:], pv.partition_broadcast(128), writes=[pvt_r])
    gsb, gsb_r = kb.sb("bgs", [128, 128], stack=st)
    dma(S, "sp", gsb[:], gssm.partition_broadcast(128), writes=[gsb_r])
    xs, xs_r = kb.sb("bxs", [128, NCH, 128], F32, stack=st)
    Btm, Btm_r = kb.sb("bBtm", [128, NCH, 128], BF16, stack=st)
    BT, BT_r = kb.sb("bBT", [128, NTOK], BF16, stack=st)
    CT, CT_r = kb.sb("bCT", [128, NTOK], BF16, stack=st)
    dt, dt_r = kb.sb("bdt", [128, NCH, 4], F32, stack=st)
    da, da_r = kb.sb("bda", [128, NCH, 4], F32, stack=st)
    zt, zt_r = kb.sb("bzt", [128, NCH, 132], F32, stack=st)
    ztv = sc.TMZ.rearrange("(c p) f -> p c f", p=128)
    for c0 in range(0, NCH, 6):
        dma(S, "sp", zt[:, c0:c0 + 6, :], ztv[:, c0:c0 + 6, :], writes=[zt_r])
    S.op("dve", lambda e: e.tensor_copy(out=dt[:], in_=zt[:, :, 128:132]), reads=[zt_r], writes=[dt_r])
    An, An_r = kb.sb("bAn", [128, 4], stack=st)
    S.op("act", lambda e: e.activation(out=An[:], in_=pvt[:, 4:8], func=AF.Exp), reads=[pvt_r], writes=[An_r])
    S.op("dve", lambda e: e.tensor_scalar(out=An[:], in0=An[:], scalar1=-1.0, scalar2=None, op0=ALU.mult), reads=[An_r], writes=[An_r])
    for f in range(4):
        S.op("dve", lambda e: e.tensor_scalar(out=dt[:, :, f], in0=dt[:, :, f], scalar1=pvt[:, f:f + 1], scalar2=None, op0=ALU.add), reads=[dt_r, pvt_r], writes=[dt_r])
    S.op("act", lambda e: e.activation(out=dt[:], in_=dt[:], func=AF.Exp), reads=[dt_r], writes=[dt_r])
    S.op("act", lambda e: e.activation(out=dt[:], in_=dt[:], func=AF.Ln, bias=1.0, scale=1.0), reads=[dt_r], writes=[dt_r])
    for f in range(4):
        S.op("dve", lambda e: e.tensor_scalar(out=da[:, :, f], in0=dt[:, :, f], scalar1=An[:, f:f + 1], scalar2=None, op0=ALU.mult), reads=[dt_r, An_r], writes=[da_r])
    if B_DEBUG.get("stop") == 1:
        S.barrier(include_cc=False)
        st.close()
        return
    st1 = contextlib.ExitStack()
    TC = 1024
    X, X_r = kb.sb("bX", [128, 3, TC + 4], F32, stack=st1)
    acc, acc_r = kb.sb("bacc", [128, 3, TC], F32, stack=st1)
    tpr = kb.ring("btp", 2, [128, 512], F32, psum=True, stack=st1)
    segs = [(i * TC, TC, 0, NLAT) for i in range(NLAT // TC)] + [(NLAT, NCTX, NLAT, NTOK)]
    for (t0, n, s0, s1) in segs:
        lo = max(t0 - 2, s0)
        hi = min(t0 + n + 2, s1)
        if lo > t0 - 2:
            S.op("pool", lambda e: e.memset(X[:, :, 0:2], 0.0), writes=[X_r])
        if hi < t0 + n + 2:
            S.op("pool", lambda e: e.memset(X[:, :, n + 2:n + 4], 0.0), writes=[X_r])
        for i in range(3):
            dma(S, "sp", X[:, i, lo - (t0 - 2):hi - (t0 - 2)], sc.B_xbcT[i, :, lo:hi], writes=[X_r])
        for i in range(3):
            S.op("dve", lambda e: e.tensor_scalar(out=acc[:, i, :n], in0=X[:, i, 0:n], scalar1=cwt[:, i, 0:1], scalar2=cbt[:, i:i + 1], op0=ALU.mult, op1=ALU.add),
                 reads=[X_r, cwt_r, cbt_r], writes=[acc_r])
            for k in range(1, 5):
                S.op("dve", lambda e: e.scalar_tensor_tensor(out=acc[:, i, :n], in0=X[:, i, k:k + n], scalar=cwt[:, i, k:k + 1], in1=acc[:, i, :n], op0=ALU.mult, op1=ALU.add),
                     reads=[X_r, cwt_r, acc_r], writes=[acc_r])
        S.op("act", lambda e: e.activation(out=acc[:, :, :n], in_=acc[:, :, :n], func=AF.Silu), reads=[acc_r], writes=[acc_r])
        S.op("pool", lambda e: e.tensor_copy(out=BT[:, t0:t0 + n], in_=acc[:, 1, :n]), reads=[acc_r], writes=[BT_r])
        S.op("pool", lambda e: e.tensor_copy(out=CT[:, t0:t0 + n], in_=acc[:, 2, :n]), reads=[acc_r], writes=[CT_r])
        for sub in range(n // 128):
            c = (t0 + sub * 128) // 128
            tp, tp_r = tpr.next()
            S.op("pe", lambda e: e.transpose(out=tp[:, 0:128], in_=acc[:, 0, sub * 128:(sub + 1) * 128], identity=ident), reads=[acc_r, cst_r], writes=[tp_r])
            S.op("pe", lambda e: e.transpose(out=tp[:, 128:256], in_=acc[:, 1, sub * 128:(sub + 1) * 128], identity=ident), reads=[acc_r, cst_r], writes=[tp_r])
            S.op("act", lambda e: e.copy(out=xs[:, c, :], in_=tp[:, 0:128]), reads=[tp_r], writes=[xs_r])
            S.op("dve", lambda e: e.tensor_copy(out=Btm[:, c, :], in_=tp[:, 128:256]), reads=[tp_r], writes=[Btm_r])
    S.barrier(include_cc=False)
    st1.close()
    if B_DEBUG.get("stop") == 2:
        st.close()
        return
    st2 = contextlib.ExitStack()
    yF, yF_r = kb.sb("byF", [128, NCH, 128], F32, stack=st2)
    S.op("pool", lambda e: e.memset(yF[:], 0.0), writes=[yF_r])

    class DirState:
        pass
    DS = []
    for d in range(2):
        o = DirState()
        o.hst, o.hst_r = kb.sb(f"bh{d}", [128, 2, 64], F32, stack=st2)
        o.hb, o.hb_r = kb.sb(f"bhb{d}", [128, 2, 64], BF16, stack=st2)
        o.pA, o.pA_r = kb.ps(f"bpA{d}", [128, 512], F32, stack=st2)
        o.pB, o.pB_r = kb.ps(f"bpB{d}", [128, 512], F32, stack=st2)
        o.ac, o.ac_r = kb.sb(f"bac{d}", [128, 2], stack=st2)
        o.wc, o.wc_r = kb.sb(f"bwc{d}", [128, 2], stack=st2)
        o.dtw, o.dtw_r = kb.sb(f"bdtw{d}", [128, 2], stack=st2)
        o.ea, o.ea_r = kb.sb(f"bea{d}", [128, 2], stack=st2)
        o.cd, o.cd_r = kb.sb(f"bcd{d}", [128, 2], stack=st2)
        o.seg, o.seg_r = kb.sb(f"bseg{d}", [128, 2, 128], F32, stack=st2)
        o.Ld, o.Ld_r = kb.sb(f"bLd{d}", [128, 2, 128], F32, stack=st2)
        o.M, o.M_r = kb.sb(f"bM{d}", [128, 2, 128], BF16, stack=st2)
        o.xg, o.xg_r = kb.sb(f"bxg{d}", [128, 2, 64], BF16, stack=st2)
        o.xgw, o.xgw_r = kb.sb(f"bxgw{d}", [128, 2, 64], BF16, stack=st2)
        o.xgf, o.xgf_r = kb.sb(f"bxgf{d}", [128, 2, 64], F32, stack=st2)
        o.yd, o.yd_r = kb.sb(f"byd{d}", [128, 2, 64], F32, stack=st2)
        S.op("pool", lambda e: e.memset(o.hst[:], 0.0), writes=[o.hst_r])
        S.op("pool", lambda e: e.memset(o.hb[:], 0.0), writes=[o.hb_r])
        DS.append(o)

    def process(d, c):
        o = DS[d]
        tri = triF if d == 0 else triB
        msk = mskF if d == 0 else mskB
        pA, pA_r, pB, pB_r = o.pA, o.pA_r, o.pB, o.pB_r
        csl = slice(c * 128, (c + 1) * 128)
        dac = da[:, c, 2 * d:2 * d + 2]
        S.op("pe", lambda e: e.matmul(pA[:, 0:2], lhsT=tri, rhs=dac, start=True, stop=True, skip_group_check=True), reads=[cst_r, da_r], writes=[pA_r])
        S.op("pe", lambda e: e.matmul(pA[:, 2:4], lhsT=ones, rhs=dac, start=True, stop=True, skip_group_check=True), reads=[cst_r, da_r], writes=[pA_r])
        for h in range(2):
            S.op("pe", lambda e: e.matmul(pA[:, 256 + h * 128:256 + (h + 1) * 128], lhsT=da[:, c, 2 * d + h:2 * d + h + 1].to_broadcast([128, 128]), rhs=tri, start=True, stop=True, skip_group_check=True),
                 reads=[cst_r, da_r], writes=[pA_r])
        S.op("pe", lambda e: e.matmul(pA[:, 128:256], lhsT=BT[:, csl], rhs=CT[:, csl], start=True, stop=True, skip_group_check=True), reads=[BT_r, CT_r], writes=[pA_r])
        yield
        S.op("act", lambda e: e.copy(out=o.ac[:], in_=pA[:, 0:2]), reads=[pA_r], writes=[o.ac_r])
        S.op("act", lambda e: e.activation(out=o.ea[:], in_=pA[:, 0:2], func=AF.Exp), reads=[pA_r], writes=[o.ea_r])
        S.op("act", lambda e: e.activation(out=o.cd[:], in_=pA[:, 2:4], func=AF.Exp), reads=[pA_r], writes=[o.cd_r])
        yield
        S.op("dve", lambda e: e.tensor_tensor(out=o.wc[:], in0=pA[:, 2:4], in1=o.ac[:], op=ALU.subtract), reads=[pA_r, o.ac_r], writes=[o.wc_r])
        for h in range(2):
            S.op("dve", lambda e: e.scalar_tensor_tensor(out=o.seg[:, h, :], in0=pA[:, 256 + h * 128:256 + (h + 1) * 128], scalar=o.ac[:, h:h + 1], in1=msk, op0=ALU.subtract, op1=ALU.add),
                 reads=[pA_r, o.ac_r, cst_r], writes=[o.seg_r])
        yield
        S.op("act", lambda e: e.activation(out=o.wc[:], in_=o.wc[:], func=AF.Exp), reads=[o.wc_r], writes=[o.wc_r])
        S.op("act", lambda e: e.activation(out=o.Ld[:], in_=o.seg[:], func=AF.Exp), reads=[o.seg_r], writes=[o.Ld_r])
        yield
        S.op("dve", lambda e: e.tensor_tensor(out=o.dtw[:], in0=dt[:, c, 2 * d:2 * d + 2], in1=o.wc[:], op=ALU.mult), reads=[dt_r, o.wc_r], writes=[o.dtw_r])
        for h in range(2):
            S.op("dve", lambda e: e.tensor_tensor(out=o.M[:, h, :], in0=pA[:, 128:256], in1=o.Ld[:, h, :], op=ALU.mult), reads=[pA_r, o.Ld_r], writes=[o.M_r])
            S.op("act", lambda e: e.activation(out=o.xg[:, h, :], in_=xs[:, c, h * 64:(h + 1) * 64], func=AF.Copy, scale=dt[:, c, 2 * d + h:2 * d + h + 1]),
                 reads=[xs_r, dt_r], writes=[o.xg_r])
            S.op("act", lambda e: e.activation(out=o.xgw[:, h, :], in_=xs[:, c, h * 64:(h + 1) * 64], func=AF.Copy, scale=o.dtw[:, h:h + 1]),
                 reads=[xs_r, o.dtw_r], writes=[o.xgw_r])
        yield
        for h in range(2):
            S.op("pe", lambda e: e.matmul(pB[:, (2 * h) * 64:(2 * h + 1) * 64], lhsT=o.M[:, h, :], rhs=o.xg[:, h, :], start=True, stop=True, skip_group_check=True),
                 reads=[o.M_r, o.xg_r], writes=[pB_r])
            S.op("pe", lambda e: e.matmul(pB[:, (2 * h + 1) * 64:(2 * h + 2) * 64], lhsT=CT[:, csl], rhs=o.hb[:, h, :], start=True, stop=True, skip_group_check=True),
                 reads=[CT_r, o.hb_r], writes=[pB_r])
            S.op("pe", lambda e: e.matmul(pB[:, 256 + h * 64:256 + (h + 1) * 64], lhsT=Btm[:, c, :], rhs=o.xgw[:, h, :], start=True, stop=True, skip_group_check=True),
                 reads=[Btm_r, o.xgw_r], writes=[pB_r])
        yield
        S.op("act", lambda e: e.copy(out=o.yd[:], in_=pB[:, 0:256].rearrange("p (h t q) -> p h t q", h=2, t=2)[:, :, 0, :]), reads=[pB_r], writes=[o.yd_r])
        yield
        for h in range(2):
            S.op("dve", lambda e: e.scalar_tensor_tensor(out=o.yd[:, h, :], in0=pB[:, (2 * h + 1) * 64:(2 * h + 2) * 64], scalar=o.ea[:, h:h + 1], in1=o.yd[:, h, :], op0=ALU.mult, op1=ALU.add),
                 reads=[pB_r, o.ea_r, o.yd_r], writes=[o.yd_r])
            S.op("dve", lambda e: e.scalar_tensor_tensor(out=o.hst[:, h, :], in0=o.hst[:, h, :], scalar=o.cd[:, h:h + 1], in1=pB[:, 256 + h * 64:256 + (h + 1) * 64], op0=ALU.mult, op1=ALU.add),
                 reads=[o.hst_r, o.cd_r, pB_r], writes=[o.hst_r])
        yield
        S.op("act", lambda e: e.copy(out=o.hb[:], in_=o.hst[:]), reads=[o.hst_r], writes=[o.hb_r])
        S.op("pool", lambda e: e.tensor_tensor(out=yF[:, c, :], in0=yF[:, c, :], in1=o.yd[:].rearrange("p h q -> p (h q)"), op=ALU.add),
             reads=[yF_r, o.yd_r], writes=[yF_r])

    orders = [[64, 65] + list(range(64)), [65, 64] + list(range(63, -1, -1))]
    for step in range(NCH):
        gens = [process(0, orders[0][step]), process(1, orders[1][step])]
        alive = list(gens)
        while alive:
            for g in list(alive):
                try:
                    next(g)
                except StopIteration:
                    alive.remove(g)
    zr = kb.ring("bz", 2, [128, 128], F32, stack=st2)
    yr = kb.ring("byo", 2, [128, 128], F32, stack=st2)
    sqr = kb.ring("bsq", 2, [128, 128], F32, stack=st2)
    yor = kb.ring("byb", 2, [128, 128], BF16, stack=st2)
    sso, sso_r = kb.sb("bsso", [128, NCH], F32, stack=st2)
    S.op("pool", lambda e: e.memset(sso[:], 0.0), writes=[sso_r])
    nch_out = NCH if need_ctx else 64
    for c in range(nch_out):
        z, z_r = zr.next()
        y, y_r = yr.next()
        sq, sq_r = sqr.next()
        S.op("act", lambda e: e.activation(out=z[:], in_=zt[:, c, 0:128], func=AF.Silu), reads=[zt_r], writes=[z_r])
        for h in range(2):
            S.op("dve", lambda e: e.scalar_tensor_tensor(out=y[:, h * 64:(h + 1) * 64], in0=xs[:, c, h * 64:(h + 1) * 64], scalar=pvt[:, 8 + h:9 + h], in1=yF[:, c, h * 64:(h + 1) * 64], op0=ALU.mult, op1=ALU.add),
                 reads=[xs_r, pvt_r, yF_r], writes=[y_r])
        S.op("dve", lambda e: e.tensor_tensor(out=y[:], in0=y[:], in1=z[:], op=ALU.mult), reads=[y_r, z_r], writes=[y_r])
        S.op("act", lambda e: e.activation(out=sq[:], in_=y[:], func=AF.Square, accum_out=sso[:, c:c + 1]), reads=[y_r], writes=[sq_r, sso_r])
        yo, yo_r = yor.next()
        S.op("pool", lambda e: e.tensor_tensor(out=yo[:], in0=y[:], in1=gsb[:], op=ALU.mult), reads=[y_r, gsb_r], writes=[yo_r])
        if callable(yB):
            yB(yo, yo_r, c * 128, 128)
        else:
            dma(S, "sp", yB[c * 128:(c + 1) * 128, :], yo[:], reads=[yo_r], final=True)
    if callable(ssqB):
        ssqB(sso, sso_r)
    else:
        dma(S, "sp", ssqB[:, 0:nch_out], sso[:, 0:nch_out], reads=[sso_r], final=True)
    S.barrier(include_cc=False)
    st2.close()
    st.close()


def ssd_params(inp, l, j):
    g = j // 2
    chans = [np.arange(2 * j * 64, (2 * j + 2) * 64), 512 + g * 128 + np.arange(128), 768 + g * 128 + np.arange(128)]
    cw = np.stack([inp["conv_w"][l][:, ch].T for ch in chans], 1)
    cb = np.stack([inp["conv_b"][l][ch] for ch in chans], 1)
    hh = [2 * j, 2 * j + 1]
    pv = np.concatenate([inp["dt_bias"][l][0, hh], inp["dt_bias"][l][1, hh], inp["a_log"][l][0, hh], inp["a_log"][l][1, hh], inp["d_skip"][l][hh]])
    gssm = inp["g_ssm"][l][2 * j * 64:(2 * j + 2) * 64]
    return (np.ascontiguousarray(cw, np.float32), np.ascontiguousarray(cb, np.float32), np.ascontiguousarray(pv, np.float32), np.ascontiguousarray(gssm, np.float32))


def g_chunks(T):
    chunks = [[(i * 512, 512, False)] for i in range(4)]
    if T > 2048:
        chunks[3].append((2048, T - 2048, True))
    return chunks


def wload(S, dst, src, k0, nk, c0, ncols, writes):
    if src.dtype == BF16:
        dma(S, "sp", dst[:, 0:nk, :], src[k0:k0 + nk * 128, c0:c0 + ncols].rearrange("(kc p) c -> p kc c", p=128), writes=writes)
    else:
        for k in range(nk):
            dma(S, "pool", dst[:, k, :], src[k0 + k * 128:k0 + (k + 1) * 128, c0:c0 + ncols], writes=writes)


def emit_wconv(kb, src, dst):
    rows = src.shape[0]
    for r0 in range(0, rows, 512):
        dma(kb.S, "pool", dst[r0:r0 + 512, :], src[r0:r0 + 512, :])


def emit_rstd_bc(S, kb, dst, dst_r, ps, ps_r, n, width):
    S.op("dve", lambda e: e.tensor_scalar(out=dst[:, :width], in0=ps[:, :width], scalar1=1.0 / n, scalar2=EPS, op0=ALU.mult, op1=ALU.add),
         reads=[ps_r], writes=[dst_r])
    S.op("act", lambda e: e.activation(out=dst[:, :width], in_=dst[:, :width], func=AF.Sqrt), reads=[dst_r], writes=[dst_r])
    S.op("dve", lambda e: e.reciprocal(out=dst[:, :width], in_=dst[:, :width]), reads=[dst_r], writes=[dst_r])


def build_g(T):
    kb = KB()
    hT = kb.din("hT", [128, 16, T], BF16)
    ysT = kb.din("ysT", [128, 16, T], BF16)
    ssq4 = kb.din("ssq4", [4, T])
    xT = kb.din("xT", [128, 16, T])
    vecs = kb.din("vecs", [128, 16, 11])
    wg = kb.din("wg", [D, 4 * D])
    wb = kb.din("wb", [4 * 512, D])
    wo = kb.din("wo", [D, D])
    w1 = kb.din("w1", [D, D_FF])
    w2 = kb.din("w2", [D_FF, D])
    x2T = kb.dout("x2T", [128, 16, T])
    emit_g(kb, T, hT, ysT, ssq4, xT, vecs, wg, wb, wo, w1, w2, x2T, True)
    return kb.finish()


def emit_g(kb, T, hT, ysT, ssq4, xT, vecs, wg, wb, wo, w1, w2, x2T, final):
    S = kb.S
    top = contextlib.ExitStack()
    _sb, _ring = kb.sb, kb.ring
    kb_sb = lambda name, shape, dt=F32, stack=None: _sb(name, shape, dt, stack=stack or top)
    kb_ring = lambda name, n, shape, dt=F32, psum=False, stack=None: _ring(name, n, shape, dt, psum=psum, stack=stack or top)
    TCM = 576
    vt, vt_r = kb_sb("vt", [128, 16, 11])
    if callable(vecs):
        vecs(vt, vt_r)
    else:
        dma(S, "sp", vt[:], vecs, writes=[vt_r])
    cv, cv_r = kb_sb("cv", [128, 16, 8])
    for z in range(2):
        o = 3 + 4 * z
        S.op("dve", lambda e: e.tensor_tensor(out=cv[:, :, 4 * z + 0], in0=vt[:, :, o + 0], in1=vt[:, :, 0], op=ALU.mult), reads=[vt_r], writes=[cv_r])
        S.op("dve", lambda e: e.tensor_copy(out=cv[:, :, 4 * z + 1], in_=vt[:, :, o + 1]), reads=[vt_r], writes=[cv_r])
        S.op("dve", lambda e: e.scalar_tensor_tensor(out=cv[:, :, 4 * z + 2], in0=vt[:, :, o + 2], scalar=1.0, in1=vt[:, :, 1], op0=ALU.add, op1=ALU.mult), reads=[vt_r], writes=[cv_r])
        S.op("dve", lambda e: e.tensor_tensor(out=cv[:, :, 4 * z + 3], in0=vt[:, :, o + 3], in1=vt[:, :, 2], op=ALU.mult), reads=[vt_r], writes=[cv_r])
    onesb, onesb_r = kb_sb("onesb", [128, 128], BF16)
    S.op("pool", lambda e: e.memset(onesb[:], 1.0), writes=[onesb_r])
    pr = kb_ring("gp", 4, [128, 512], F32, psum=True)
    sr = kb_ring("gs", 2, [128, 512], F32, psum=True)
    yT, yT_r = kb_sb("yT", [128, 16, TCM], F32)
    sgr = kb_ring("sg", 2, [128, 512], F32)
    tmr = kb_ring("tm", 2, [128, 512], F32)
    sqr = kb_ring("gsq", 2, [128, 512], BF16)
    rs, rs_r = kb_sb("rs", [128, TCM], F32)
    for chunk in g_chunks(T):
        c0 = chunk[0][0]
        tc_ = sum(n for (_, n, _) in chunk)
        nts = [(t0 - c0, n, isc) for (t0, n, isc) in chunk]
        sa = contextlib.ExitStack()
        h, h_r = kb_sb("gh", [128, 16, TCM], BF16, stack=sa)
        ys, ys_r = kb_sb("gys", [128, 16, TCM], BF16, stack=sa)
        mT, mT_r = kb_sb("gm", [128, 16, TCM], BF16, stack=sa)
        macc, macc_r = kb_sb("gmacc", [128, 4, TCM], F32, stack=sa)
        sq4, sq4_r = kb_sb("gsq4", [128, 4, TCM], F32, stack=sa)
        Wr = kb_ring("gW", 2, [128, 16, 512], BF16, stack=sa)
        Wbr = kb_ring("gWb", 2, [128, 4, 512], BF16, stack=sa)
        def load_g1(cg, i):
            W, W_r = Wr.next()
            Wb, Wb_r = Wbr.next()
            wload(S, W, wg, 0, 16, i * D + cg * 512, 512, [W_r])
            wload(S, Wb, wb, i * 512, 4, cg * 512, 512, [Wb_r])
            return W, W_r, Wb, Wb_r

        def load_g2(cg):
            W, W_r = Wr.next()
            wload(S, W, wo, 0, 16, cg * 512, 512, [W_r])
            return W, W_r
        g1blocks = [(cg, i) for cg in range(4) for i in range(4)]
        nxt = load_g1(*g1blocks[0])
        for k in range(16):
            if isinstance(hT, list):
                dma(S, "sp", h[:, k, :tc_], hT[k][:, c0:c0 + tc_], writes=[h_r])
            else:
                dma(S, "sp", h[:, k, :tc_], hT[:, k, c0:c0 + tc_], writes=[h_r])
            dma(S, "sp", ys[:, k, :tc_], ysT[:, k, c0:c0 + tc_], writes=[ys_r])
        for j in range(4):
            dma(S, "sp", sq4[:, j, :tc_], ssq4[j, c0:c0 + tc_].partition_broadcast(128), writes=[sq4_r])
        for j in range(1, 4):
            S.op("dve", lambda e: e.tensor_tensor(out=sq4[:, 0, :tc_], in0=sq4[:, 0, :tc_], in1=sq4[:, j, :tc_], op=ALU.add), reads=[sq4_r], writes=[sq4_r])
        S.op("dve", lambda e: e.tensor_scalar(out=sq4[:, 0, :tc_], in0=sq4[:, 0, :tc_], scalar1=1.0 / 512, scalar2=EPS, op0=ALU.mult, op1=ALU.add), reads=[sq4_r], writes=[sq4_r])
        S.op("act", lambda e: e.activation(out=sq4[:, 0, :tc_], in_=sq4[:, 0, :tc_], func=AF.Sqrt), reads=[sq4_r], writes=[sq4_r])
        S.op("dve", lambda e: e.reciprocal(out=sq4[:, 0, :tc_], in_=sq4[:, 0, :tc_]), reads=[sq4_r], writes=[sq4_r])
        for k in range(4, 8):
            S.op("dve", lambda e: e.tensor_tensor(out=ys[:, k, :tc_], in0=ys[:, k, :tc_], in1=sq4[:, 0, :tc_], op=ALU.mult), reads=[ys_r, sq4_r], writes=[ys_r])
        nxt2 = None
        for bi, (cg, i) in enumerate(g1blocks):
            if True:
                W, W_r, Wb, Wb_r = nxt
                if bi + 1 < len(g1blocks):
                    nxt = load_g1(*g1blocks[bi + 1])
                else:
                    nxt2 = load_g2(0)
                for c4 in range(4):
                    for (o, n, isc) in nts:
                        pg, pg_r = pr.next()
                        pb, pb_r = pr.next()
                        for k in range(16):
                            S.op("pe", lambda e: e.matmul(pg[:, :n], lhsT=W[:, k, c4 * 128:(c4 + 1) * 128], rhs=h[:, k, o:o + n], start=(k == 0), stop=(k == 15)),
                                 reads=[W_r, h_r], writes=[pg_r])
                        for k in range(4):
                            S.op("pe", lambda e: e.matmul(pb[:, :n], lhsT=Wb[:, k, c4 * 128:(c4 + 1) * 128], rhs=ys[:, 4 * i + k, o:o + n], start=(k == 0), stop=(k == 3)),
                                 reads=[Wb_r, ys_r], writes=[pb_r])
                        sg, sg_r = sgr.next()
                        S.op("act", lambda e: e.activation(out=sg[:, :n], in_=pg[:, :n], func=AF.Sigmoid), reads=[pg_r], writes=[sg_r])
                        if i == 0:
                            S.op("dve", lambda e: e.tensor_tensor(out=macc[:, c4, o:o + n], in0=pb[:, :n], in1=sg[:, :n], op=ALU.mult), reads=[pb_r, sg_r], writes=[macc_r])
                        else:
                            tm, tm_r = tmr.next()
                            S.op("dve", lambda e: e.tensor_tensor(out=tm[:, :n], in0=pb[:, :n], in1=sg[:, :n], op=ALU.mult), reads=[pb_r, sg_r], writes=[tm_r])
                            if i < 3:
                                S.op("pool", lambda e: e.tensor_tensor(out=macc[:, c4, o:o + n], in0=macc[:, c4, o:o + n], in1=tm[:, :n], op=ALU.add), reads=[macc_r, tm_r], writes=[macc_r])
                            else:
                                S.op("pool", lambda e: e.tensor_tensor(out=mT[:, cg * 4 + c4, o:o + n], in0=macc[:, c4, o:o + n], in1=tm[:, :n], op=ALU.add), reads=[macc_r, tm_r], writes=[mT_r])
        ssp = [sr.next() for _ in nts]
        for cg in range(4):
            W, W_r = nxt2
            if cg + 1 < 4:
                nxt2 = load_g2(cg + 1)
            for c4 in range(4):
                ct = cg * 4 + c4
                for ni, (o, n, isc) in enumerate(nts):
                    py, py_r = pr.next()
                    for k in range(16):
                        S.op("pe", lambda e: e.matmul(py[:, :n], lhsT=W[:, k, c4 * 128:(c4 + 1) * 128], rhs=mT[:, k, o:o + n], start=(k == 0), stop=(k == 15)),
                             reads=[W_r, mT_r], writes=[py_r])
                    S.op("act", lambda e: e.copy(out=yT[:, ct, o:o + n], in_=py[:, :n]), reads=[py_r], writes=[yT_r])
                    sq, sq_r = sqr.next()
                    S.op("dve", lambda e: e.tensor_tensor(out=sq[:, :n], in0=yT[:, ct, o:o + n], in1=yT[:, ct, o:o + n], op=ALU.mult), reads=[yT_r], writes=[sq_r])
                    S.op("pe", lambda e: e.matmul(ssp[ni][0][:, :n], lhsT=onesb[:], rhs=sq[:, :n], start=(ct == 0), stop=(ct == 15)),
                         reads=[onesb_r, sq_r], writes=[ssp[ni][1]])
        for ni, (o, n, isc) in enumerate(nts):
            emit_rstd_bc(S, kb, rs[:, o:o + n], rs_r, ssp[ni][0], ssp[ni][1], D, n)
        S.barrier()
        sa.close()
        sb_ = contextlib.ExitStack()
        h2, h2_r = kb_sb("gh2", [128, 16, TCM], BF16, stack=sb_)
        facc, facc_r = kb_sb("gfacc", [128, 16, TCM], F32, stack=sb_)
        xin = kb_ring("gxin", 2, [128, TCM], F32, stack=sb_)
        W1r = kb_ring("gW1", 2, [128, 16, 256], BF16, stack=sb_)
        W2r = kb_ring("gW2", 3, [128, 2, D], BF16, stack=sb_)
        aTr = kb_ring("gaT", 3, [128, 2, TCM], BF16, stack=sb_)
        rlr = kb_ring("grl", 2, [128, 512], F32, stack=sb_)

        def load_ffn(hc):
            W1, W1_r = W1r.next()
            W2, W2_r = W2r.next()
            wload(S, W1, w1, 0, 16, hc * 256, 256, [W1_r])
            wload(S, W2, w2, hc * 256, 2, 0, D, [W2_r])
            return W1, W1_r, W2, W2_r
        nxtf = load_ffn(0)
        ssp = [sr.next() for _ in nts]
        for ft in range(16):
            xt, xt_r = xin.next()
            dma(S, "sp", xt[:, :tc_], xT[:, ft, c0:c0 + tc_], writes=[xt_r])
            S.op("dve", lambda e: e.tensor_tensor(out=yT[:, ft, :tc_], in0=yT[:, ft, :tc_], in1=rs[:, :tc_], op=ALU.mult), reads=[yT_r, rs_r], writes=[yT_r])
            for ni, (o, n, isc) in enumerate(nts):
                z = 4 * int(isc)
                S.op("dve", lambda e: e.scalar_tensor_tensor(out=yT[:, ft, o:o + n], in0=yT[:, ft, o:o + n], scalar=cv[:, ft, z:z + 1], in1=xt[:, o:o + n], op0=ALU.mult, op1=ALU.add),
                     reads=[yT_r, cv_r, xt_r], writes=[yT_r])
                sq, sq_r = sqr.next()
                S.op("pool", lambda e: e.tensor_tensor(out=sq[:, :n], in0=yT[:, ft, o:o + n], in1=yT[:, ft, o:o + n], op=ALU.mult), reads=[yT_r], writes=[sq_r])
                S.op("pe", lambda e: e.matmul(ssp[ni][0][:, :n], lhsT=onesb[:], rhs=sq[:, :n], start=(ft == 0), stop=(ft == 15)),
                     reads=[onesb_r, sq_r], writes=[ssp[ni][1]])
        for ni, (o, n, isc) in enumerate(nts):
            emit_rstd_bc(S, kb, rs[:, o:o + n], rs_r, ssp[ni][0], ssp[ni][1], D, n)
        for ft in range(16):
            for ni, (o, n, isc) in enumerate(nts):
                z = 4 * int(isc)
                tm, tm_r = tmr.next()
                S.op("dve", lambda e: e.tensor_tensor(out=tm[:, :n], in0=yT[:, ft, o:o + n], in1=rs[:, o:o + n], op=ALU.mult), reads=[yT_r, rs_r], writes=[tm_r])
                S.op("act", lambda e: e.activation(out=h2[:, ft, o:o + n], in_=tm[:, :n], func=AF.Identity, scale=cv[:, ft, z + 2:z + 3], bias=cv[:, ft, z + 1:z + 2]),
                     reads=[tm_r, cv_r], writes=[h2_r])
        NHC = D_FF // 256

        def ffn_a(hc):
            nonlocal nxtf
            W1, W1_r, W2, W2_r = nxtf
            if hc + 1 < NHC:
                nxtf = load_ffn(hc + 1)
            aT, aT_r = aTr.next()
            for kt in range(2):
                for (o, n, isc) in nts:
                    pa, pa_r = pr.next()
                    for k in range(16):
                        S.op("pe", lambda e: e.matmul(pa[:, :n], lhsT=W1[:, k, kt * 128:(kt + 1) * 128], rhs=h2[:, k, o:o + n], start=(k == 0), stop=(k == 15)),
                             reads=[W1_r, h2_r], writes=[pa_r])
                    rl, rl_r = rlr.next()
                    S.op("act", lambda e: e.activation(out=rl[:, :n], in_=pa[:, :n], func=AF.Relu), reads=[pa_r], writes=[rl_r])
                    S.op("pool", lambda e: e.tensor_tensor(out=aT[:, kt, o:o + n], in0=rl[:, :n], in1=rl[:, :n], op=ALU.mult), reads=[rl_r], writes=[aT_r])
            return hc, W2, W2_r, aT, aT_r

        def ffn_f(hc, W2, W2_r, aT, aT_r):
            for ct in range(16):
                for (o, n, isc) in nts:
                    pf, pf_r = pr.next()
                    for kt in range(2):
                        S.op("pe", lambda e: e.matmul(pf[:, :n], lhsT=W2[:, kt, ct * 128:(ct + 1) * 128], rhs=aT[:, kt, o:o + n], start=(kt == 0), stop=(kt == 1)),
                             reads=[W2_r, aT_r], writes=[pf_r])
                    if hc == 0:
                        S.op("act", lambda e: e.copy(out=facc[:, ct, o:o + n], in_=pf[:, :n]), reads=[pf_r], writes=[facc_r])
                    else:
                        S.op("dve", lambda e: e.tensor_tensor(out=facc[:, ct, o:o + n], in0=pf[:, :n], in1=facc[:, ct, o:o + n], op=ALU.add), reads=[pf_r, facc_r], writes=[facc_r])
        pendf = ffn_a(0)
        for hc in range(1, NHC):
            cur = ffn_a(hc)
            ffn_f(*pendf)
            pendf = cur
        ffn_f(*pendf)
        ssp = [sr.next() for _ in nts]
        for ft in range(16):
            for ni, (o, n, isc) in enumerate(nts):
                sq, sq_r = sqr.next()
                S.op("pool", lambda e: e.tensor_tensor(out=sq[:, :n], in0=facc[:, ft, o:o + n], in1=facc[:, ft, o:o + n], op=ALU.mult), reads=[facc_r], writes=[sq_r])
                S.op("pe", lambda e: e.matmul(ssp[ni][0][:, :n], lhsT=onesb[:], rhs=sq[:, :n], start=(ft == 0), stop=(ft == 15)),
                     reads=[onesb_r, sq_r], writes=[ssp[ni][1]])
        for ni, (o, n, isc) in enumerate(nts):
            emit_rstd_bc(S, kb, rs[:, o:o + n], rs_r, ssp[ni][0], ssp[ni][1], D, n)
        for ft in range(16):
            S.op("dve", lambda e: e.tensor_tensor(out=facc[:, ft, :tc_], in0=facc[:, ft, :tc_], in1=rs[:, :tc_], op=ALU.mult), reads=[facc_r, rs_r], writes=[facc_r])
            for ni, (o, n, isc) in enumerate(nts):
                z = 4 * int(isc)
                S.op("dve", lambda e: e.scalar_tensor_tensor(out=facc[:, ft, o:o + n], in0=facc[:, ft, o:o + n], scalar=cv[:, ft, z + 3:z + 4], in1=yT[:, ft, o:o + n], op0=ALU.mult, op1=ALU.add),
                     reads=[facc_r, cv_r, yT_r], writes=[facc_r])
            dma(S, "sp", x2T[:, ft, c0:c0 + tc_], facc[:, ft, :tc_], reads=[facc_r], final=final)
        S.barrier()
        sb_.close()
    S.barrier()
    top.close()


def fm_layout(a):
    T, Dd = a.shape
    return np.ascontiguousarray(a.T.reshape(Dd // 128, 128, T).transpose(1, 0, 2))


def run_x(inp, l, hl, hc, need_ctx):
    lam_init = 0.8 - 0.6 * math.exp(-0.3 * l)
    nc = build_x(lam_init, need_ctx)
    cosT, sinT = rope_tables()
    w_in = inp["w_in"][l]
    hTs = [make_hT(hl, hc, b) for b in range(2)]
    lamv = np.ascontiguousarray(np.stack([inp["lam_q1"][l], inp["lam_k1"][l], inp["lam_q2"][l], inp["lam_k2"][l]], 0))
    consts = ssd_consts()
    cmask = swa_masks()
    maps = []
    for core in range(8):
        b, j = divmod(core, 4)
        fm, tm = x_weight_cols(j)
        cw, cb, pv, gssm = ssd_params(inp, l, j)
        maps.append({
            "hT": hTs[b], "wfm": np.ascontiguousarray(w_in[:, fm]), "wtm": np.ascontiguousarray(w_in[:, tm]),
            "cosT": cosT, "sinT": sinT, "lamv": lamv, "gsub": np.ascontiguousarray(inp["g_subln"][l]),
            "bconsts": consts, "cw": cw, "cb": cb, "pv": pv, "gssm": gssm,
            "sinkv": np.ascontiguousarray(inp["sink"][l][2 * j:2 * j + 2]), "cmask": cmask,
            "biasT": na_bias(inp["rpb"][l], j),
        })
    return run_spmd(nc, maps)


def run_g(inp, l, xcur, cxcur, hl, hc, xres, mod_l, need_ctx):
    T = TROWS if need_ctx else 2048
    nc = build_g(T)
    wg = np.ascontiguousarray(inp["w_in"][l][:, GATE_OFF:])
    wb = np.ascontiguousarray(inp["w_branch"][l].reshape(4 * 512, D))
    maps = []
    for core in range(8):
        b, q = divmod(core, 4)
        rows = np.arange(q * 2048, (q + 1) * 2048)
        if need_ctx:
            rows = np.concatenate([rows, NLAT + np.arange(q * 64, (q + 1) * 64)])
        hh = np.concatenate([hl[b], hc[b]], 0)[rows]
        xx = np.concatenate([xcur[b], cxcur[b]], 0)[rows]
        ysT = np.zeros((128, 16, T), NPBF)
        ssq4 = np.zeros((4, T), np.float32)
        for j in range(4):
            r = xres[b * 4 + j]
            for i, nm in enumerate(("yA", "yB", "yC", "yD")):
                ysT[:, 4 * i + j, :] = r[nm][rows].T
            ssq4[j] = r["ssqB"].T.reshape(-1)[rows]
        vec = np.stack([inp["g_post_mix"][l], inp["g_pre_mlp"][l], inp["g_post_mlp"][l],
                        mod_l[b, 2], mod_l[b, 3], mod_l[b, 4], mod_l[b, 5],
                        mod_l[2, 2], mod_l[2, 3], mod_l[2, 4], mod_l[2, 5]], 1)
        vecs = np.ascontiguousarray(vec.reshape(16, 128, 11).transpose(1, 0, 2))
        maps.append({"hT": fm_layout(hh), "ysT": ysT, "ssq4": ssq4, "xT": fm_layout(xx), "vecs": vecs,
                     "wg": wg, "wb": wb, "wo": inp["w_out"][l], "w1": inp["w_ff1"][l], "w2": inp["w_ff2"][l]})
    res = run_spmd(nc, maps)
    xn = np.zeros_like(xcur)
    cxn = np.array(cxcur, copy=True)
    for core in range(8):
        b, q = divmod(core, 4)
        o = res[core]["x2T"].transpose(1, 0, 2).reshape(D, T).T
        xn[b, q * 2048:(q + 1) * 2048] = o[:2048]
        if need_ctx:
            cxn[b, q * 64:(q + 1) * 64] = o[2048:]
    return xn, cxn


RG = [[0, 1, 2, 3], [4, 5, 6, 7]]


def allgather(S, src, dst, bg=False):
    return S.op("pool", lambda e: e.collective_compute("AllGather", ALU.bypass, replica_groups=RG, ins=[src], outs=[dst]),
                dma=True, dma_inc=1, pool="cc")


def emit_mod(kb, cT2, wmod, bmT, cvm, cvm_r, identf, identf_r, msrc, mdst):
    S = kb.S
    st = contextlib.ExitStack()
    ct, ct_r = kb.sb("mct", [128, 16, 2], stack=st)
    sT, sT_r = kb.sb("msT", [128, 16, 2], stack=st)
    bt, bt_r = kb.sb("mbt", [128, DEPTH, 24], stack=st)
    mrow, mrow_r = kb.sb("mrow", [2, 3072], F32, stack=st)
    part, part_r = kb.sb("mpart", [128, DEPTH, 24, 2], F32, stack=st)
    dma(S, "sp", ct[:], cT2, writes=[ct_r])
    for l in range(DEPTH):
        dma(S, "sp", bt[:, l, :], bmT[l], writes=[bt_r])
    S.op("act", lambda e: e.activation(out=sT[:], in_=ct[:], func=AF.Silu), reads=[ct_r], writes=[sT_r])
    wr = kb.ring("mw", 2, [128, 16, 512], F32, stack=st)
    pr = kb.ring("mp", 2, [128, 512], F32, psum=True, stack=st)
    pt, pt_r = kb.ps("mpt", [128, 512], F32, stack=st)
    for l in range(DEPTH):
        wv = wmod[l].rearrange("(kc p) c -> p kc c", p=128)
        for cg in range(6):
            w, w_r = wr.next()
            dma(S, "sp", w[:], wv[:, :, cg * 512:(cg + 1) * 512], writes=[w_r])
            p, p_r = pr.next()
            for k in range(16):
                S.op("pe", lambda e: e.matmul(p[0:2, :], lhsT=sT[:, k, :], rhs=w[:, k, :], start=(k == 0), stop=(k == 15)), reads=[w_r, sT_r], writes=[p_r])
            S.op("act", lambda e: e.copy(out=mrow[:, cg * 512:(cg + 1) * 512], in_=p[0:2, :]), reads=[p_r], writes=[mrow_r])
        for c_ in range(24):
            S.op("pe", lambda e: e.transpose(out=pt[:, c_ * 2:c_ * 2 + 2], in_=mrow[0:2, c_ * 128:(c_ + 1) * 128], identity=identf[0:2, 0:2]),
                 reads=[mrow_r, identf_r], writes=[pt_r])
        ptv = pt[:, 0:48].rearrange("p (c z) -> p c z", z=2)
        for z in range(2):
            S.op("dve", lambda e: e.tensor_tensor(out=part[:, l, :, z], in0=ptv[:, :, z], in1=bt[:, l, :], op=ALU.add), reads=[pt_r, bt_r], writes=[part_r])
    dma(S, "sp", msrc, part[:].rearrange("p l c z -> p (l c z)"), reads=[part_r])
    S.barrier()
    allgather(S, msrc, mdst)
    S.barrier()
    for l in range(DEPTH):
        cvl = cvm[:, l].rearrange("p s k z -> p (s k z)")
        for r in range(4):
            dma(S, "sp", cvl[:, r * 48:(r + 1) * 48], mdst[r * 128:(r + 1) * 128, l * 48:(l + 1) * 48], writes=[cvm_r])
    S.barrier()
    st.close()


def emit_x2xT(kb, xrows, xT, identf, identf_r):
    S = kb.S
    st = contextlib.ExitStack()
    xr = kb.ring("tx", 2, [128, D], F32, stack=st)
    pr = kb.ring("tp", 2, [128, 512], F32, psum=True, stack=st)
    orr = kb.ring("to", 2, [128, 16, 128], F32, stack=st)
    tiles = [(i * 128, 128) for i in range(16)] + [(2048, 64)]
    for (r0, n) in tiles:
        xt, xt_r = xr.next()
        ot, ot_r = orr.next()
        dma(S, "sp", xt[:n, :], xrows[r0:r0 + n, :], writes=[xt_r])
        for g in range(4):
            p, p_r = pr.next()
            for k4 in range(4):
                k = g * 4 + k4
                S.op("pe", lambda e: e.transpose(out=p[:, k4 * 128:k4 * 128 + n], in_=xt[:n, k * 128:(k + 1) * 128], identity=identf[:n, :n]),
                     reads=[xt_r, identf_r], writes=[p_r])
            for k4 in range(4):
                eng = "act" if k4 % 2 == 0 else "dve"
                if eng == "act":
                    S.op("act", lambda e: e.copy(out=ot[:, g * 4 + k4, :n], in_=p[:, k4 * 128:k4 * 128 + n]), reads=[p_r], writes=[ot_r])
                else:
                    S.op("dve", lambda e: e.tensor_copy(out=ot[:, g * 4 + k4, :n], in_=p[:, k4 * 128:k4 * 128 + n]), reads=[p_r], writes=[ot_r])
        dma(S, "sp", xT[:, :, r0:r0 + n], ot[:, :, :n], reads=[ot_r])
    S.barrier()
    st.close()


def emit_xT2out(kb, xT, out, identf, identf_r):
    S = kb.S
    st = contextlib.ExitStack()
    xr = kb.ring("ux", 2, [128, 16, 128], F32, stack=st)
    pr = kb.ring("up", 2, [128, 512], F32, psum=True, stack=st)
    orr = kb.ring("uo", 2, [128, D], F32, stack=st)
    for t in range(16):
        xt, xt_r = xr.next()
        ot, ot_r = orr.next()
        dma(S, "sp", xt[:], xT[:, :, t * 128:(t + 1) * 128], writes=[xt_r])
        for g in range(4):
            p, p_r = pr.next()
            for k4 in range(4):
                k = g * 4 + k4
                S.op("pe", lambda e: e.transpose(out=p[:, k4 * 128:(k4 + 1) * 128], in_=xt[:, k, :], identity=identf[:]), reads=[xt_r, identf_r], writes=[p_r])
            S.op("act", lambda e: e.copy(out=ot[:, g * 512:(g + 1) * 512], in_=p[:]), reads=[p_r], writes=[ot_r])
        dma(S, "sp", out[t * 128:(t + 1) * 128, :], ot[:], reads=[ot_r], final=True)
    S.barrier()
    st.close()


def emit_p1f(kb, xT, c0, c1, c_r, hT_own, onesb, onesb_r):
    S = kb.S
    st = contextlib.ExitStack()
    xr = kb.ring("px", 2, [128, 16, 512], F32, stack=st)
    hr = kb.ring("ph", 2, [128, 16, 512], BF16, stack=st)
    sqr = kb.ring("psq", 2, [128, 512], BF16, stack=st)
    tmr = kb.ring("ptm", 2, [128, 512], F32, stack=st)
    sr = kb.ring("pss", 2, [128, 512], F32, psum=True, stack=st)
    rsr = kb.ring("prs", 2, [128, 512], F32, stack=st)
    tiles = [(i * 512, 512, 0) for i in range(4)] + [(2048, 64, 1)]
    for (t0, n, z) in tiles:
        xt, xt_r = xr.next()
        ht, ht_r = hr.next()
        for k in range(0, 16, 4):
            dma(S, "sp", xt[:, k:k + 4, :n], xT[:, k:k + 4, t0:t0 + n], writes=[xt_r])
        ss, ss_r = sr.next()
        for k in range(16):
            sq, sq_r = sqr.next()
            S.op("pool" if k % 2 else "dve", lambda e: e.tensor_tensor(out=sq[:, :n], in0=xt[:, k, :n], in1=xt[:, k, :n], op=ALU.mult), reads=[xt_r], writes=[sq_r])
            S.op("pe", lambda e: e.matmul(ss[:, :n], lhsT=onesb[:], rhs=sq[:, :n], start=(k == 0), stop=(k == 15)), reads=[onesb_r, sq_r], writes=[ss_r])
        rs, rs_r = rsr.next()
        emit_rstd_bc(S, kb, rs, rs_r, ss, ss_r, D, n)
        for k in range(16):
            tm, tm_r = tmr.next()
            S.op("dve", lambda e: e.tensor_tensor(out=tm[:, :n], in0=xt[:, k, :n], in1=rs[:, :n], op=ALU.mult), reads=[xt_r, rs_r], writes=[tm_r])
            S.op("act", lambda e: e.activation(out=ht[:, k, :n], in_=tm[:, :n], func=AF.Identity, scale=c1[:, k, z:z + 1], bias=c0[:, k, z:z + 1]),
                 reads=[tm_r, c_r], writes=[ht_r])
        for k in range(16):
            dma(S, "sp", hT_own[k][:, t0:t0 + n], ht[:, k, :n], reads=[ht_r])
    S.barrier()
    st.close()


def build_fused():
    kb = KB()
    S = kb.S
    xrows = kb.din("xrows", [TROWS, D])
    cT2 = kb.din("cT2", [128, 16, 2])
    wmod = kb.din("wmod", [DEPTH, D, 3072])
    bmT = kb.din("bmT", [DEPTH, 128, 24])
    gvec = kb.din("gvec", [128, DEPTH, 16, 4])
    oneh = kb.din("oneh", [4])
    cosT = kb.din("cosT", [128, NLAT])
    sinT = kb.din("sinT", [128, NLAT])
    bconsts = kb.din("bconsts", [128, 6, 128])
    cmask = kb.din("cmask", [128, 2, 256], BF16)
    L = []
    for l in range(DEPTH):
        d = {}
        d["wfm"] = kb.din(f"wfm{l}", [D, NFM * 128])
        d["wtm"] = kb.din(f"wtm{l}", [D, NTM])
        d["lamv"] = kb.din(f"lamv{l}", [4, 64])
        d["gsub"] = kb.din(f"gsub{l}", [128])
        d["cw"] = kb.din(f"cw{l}", [128, 3, 5])
        d["cb"] = kb.din(f"cb{l}", [128, 3])
        d["pv"] = kb.din(f"pv{l}", [10])
        d["gssm"] = kb.din(f"gssm{l}", [128])
        d["sinkv"] = kb.din(f"sinkv{l}", [2])
        d["biasT"] = kb.din(f"biasT{l}", [128, 2, 8, 4, 64])
        d["wg"] = kb.din(f"wg{l}", [D, 4 * D])
        d["wb"] = kb.din(f"wb{l}", [4 * 512, D])
        d["wo"] = kb.din(f"wo{l}", [D, D])
        d["w1"] = kb.din(f"w1{l}", [D, D_FF])
        d["w2"] = kb.din(f"w2{l}", [D_FF, D])
        L.append(d)
    out = kb.dout("out", [2048, D])
    xT = kb.dscratch("xT", [128, 16, TROWS])
    hT_own = [kb.dscratch(f"hT_own{k}", [128, TROWS], BF16) for k in range(16)]
    hT_all = [kb.dscratch(f"hT_all{k}", [512, TROWS], BF16) for k in range(16)]
    ycat = {(i, q): kb.dscratch(f"ycat{i}{q}", [128, TROWS], BF16) for i in range(4) for q in range(4)}
    yall = {(i, q): kb.dscratch(f"yall{i}{q}", [512, TROWS], BF16) for i in range(4) for q in range(4)}
    ssq_own = kb.dscratch("ssq_own", [66, 128])
    msrc = kb.dscratch("msrc", [128, DEPTH * 48])
    mdst = kb.dscratch("mdst", [512, DEPTH * 48])
    ssq_all = kb.dscratch("ssq_all", [4 * 66, 128])
    ys_own = kb.dscratch("ys_own", [128, 16, TROWS], BF16)
    ssq_sel = kb.dscratch("ssq_sel", [4, TROWS])
    sc = make_xscratch(kb)
    WB = []
    for l in range(DEPTH):
        WB.append({"wg": kb.dscratch(f"wgb{l}", [D, 4 * D], BF16), "wb": kb.dscratch(f"wbb{l}", [4 * 512, D], BF16),
                   "wo": kb.dscratch(f"wob{l}", [D, D], BF16), "w1": kb.dscratch(f"w1b{l}", [D, D_FF], BF16),
                   "w2": kb.dscratch(f"w2b{l}", [D_FF, D], BF16)})

    def conv_weights(l):
        for nm in ("wg", "wb", "wo", "w1", "w2"):
            emit_wconv(kb, L[l][nm], WB[l][nm])
    cvm, cvm_r = kb.sb("cvm", [128, DEPTH, 6, 16, 2])
    gv, gv_r = kb.sb("gv", [128, DEPTH, 16, 4])
    oh, oh_r = kb.sb("oh", [128, 4])
    cst, cst_r = kb.sb("fcst", [128, 128])
    identb, identb_r = kb.sb("identb", [128, 128], BF16)
    onesb, onesb_r = kb.sb("fonesb", [128, 128], BF16)
    cc, cc_r = kb.sb("fcc", [128, 2, 16, 2])
    dma(S, "sp", gv[:], gvec, writes=[gv_r])
    dma(S, "sp", oh[:], oneh.partition_broadcast(128), writes=[oh_r])
    dma(S, "sp", cst[:], bconsts[:, 5, :], writes=[cst_r])
    S.op("dve", lambda e: e.tensor_copy(out=identb[:], in_=cst[:]), reads=[cst_r], writes=[identb_r])
    S.op("pool", lambda e: e.memset(onesb[:], 1.0), writes=[onesb_r])
    emit_mod(kb, cT2, wmod, bmT, cvm, cvm_r, cst, cst_r, msrc, mdst)
    emit_x2xT(kb, xrows, xT, cst, cst_r)
    for l in range(DEPTH):
        need_ctx = l < DEPTH - 1
        lam_init = 0.8 - 0.6 * math.exp(-0.3 * l)
        d = L[l]
        for z in range(2):
            S.op("dve", lambda e: e.tensor_copy(out=cc[:, 0, :, z], in_=cvm[:, l, 0, :, z]), reads=[cvm_r], writes=[cc_r])
            S.op("dve", lambda e: e.scalar_tensor_tensor(out=cc[:, 1, :, z], in0=cvm[:, l, 1, :, z], scalar=1.0, in1=gv[:, l, :, 0], op0=ALU.add, op1=ALU.mult),
                 reads=[cvm_r, gv_r], writes=[cc_r])
        emit_p1f(kb, xT, cc[:, 0], cc[:, 1], cc_r, hT_own, onesb, onesb_r)
        for k in range(16):
            allgather(S, hT_own[k], hT_all[k])
        S.barrier()
        xst = contextlib.ExitStack()
        ytp, ytp_r = kb.ps("ytp", [128, 1024], BF16, stack=xst)
        ytr = kb.ring("yts", 3, [128, 128], BF16, stack=xst)

        def load_hT(h, h_r, t0, n, isctx):
            if not isctx:
                r, off = divmod(t0, 2048)
                for k in range(16):
                    dma(S, "sp", h[:, k, :n], hT_all[k][r * 128:(r + 1) * 128, off:off + n], writes=[h_r])
            else:
                for r in range(4):
                    for k in range(16):
                        dma(S, "sp", h[:, k, r * 64:(r + 1) * 64], hT_all[k][r * 128:(r + 1) * 128, 2048:2112], writes=[h_r])

        def mk_ywrite(i):
            def yw(y, y_r, tok0, n):
                S.op("pe", lambda e: e.transpose(out=ytp[:, :n], in_=y[:n, :], identity=identb[:n, :n]), reads=[y_r, identb_r], writes=[ytp_r])
                yt, yt_r = ytr.next()
                S.op("dve", lambda e: e.tensor_copy(out=yt[:, :n], in_=ytp[:, :n]), reads=[ytp_r], writes=[yt_r])
                if tok0 < NLAT:
                    qq, off = divmod(tok0, 2048)
                    dma(S, "sp", ycat[(i, qq)][:, off:off + n], yt[:, :n], reads=[yt_r])
                else:
                    for c_ in range(0, n, 64):
                        qq, o2 = divmod(tok0 - NLAT + c_, 64)
                        dma(S, "sp", ycat[(i, qq)][:, 2048 + o2:2048 + o2 + 64], yt[:, c_:c_ + 64], reads=[yt_r])
            return yw

        pst, pst_r = kb.ps("ssqtp", [128, 512], F32, stack=xst)
        so, so_r = kb.sb("ssqo", [66, 128], F32, stack=xst)

        def ssq_write(sso, sso_r):
            S.op("pe", lambda e: e.transpose(out=pst[:66, :128], in_=sso[:, 0:66], identity=cst[:]), reads=[sso_r, cst_r], writes=[pst_r])
            S.op("act", lambda e: e.copy(out=so[:], in_=pst[:66, :128]), reads=[pst_r], writes=[so_r])
            dma(S, "sp", ssq_own, so[:], reads=[so_r])

        emit_p2(kb, load_hT, d["wfm"], d["wtm"], cosT, sinT, sc,
                after_wload=(lambda: [conv_weights(l_) for l_ in range(DEPTH)]) if l == 0 else None)
        def gather_branch(i):
            for qq in range(4):
                allgather(S, ycat[(i, qq)], yall[(i, qq)], bg=True)
        emit_mixA(kb, sc, d["lamv"], d["gsub"], lam_init, need_ctx, mk_ywrite(0))
        gather_branch(0)
        emit_mixC(kb, sc, d["sinkv"], cmask, need_ctx, mk_ywrite(2))
        gather_branch(2)
        emit_mixD(kb, sc, d["biasT"], need_ctx, mk_ywrite(3))
        gather_branch(3)
        emit_mixB(kb, sc, bconsts, d["cw"], d["cb"], d["pv"], d["gssm"], need_ctx, mk_ywrite(1), ssq_write)
        S.barrier(include_cc=False)
        xst.close()
        gather_branch(1)
        allgather(S, ssq_own, ssq_all, bg=True)
        S.barrier()
        T = TROWS if need_ctx else 2048
        sst = contextlib.ExitStack()
        lr = kb.ring("sl", 3, [128, TROWS], BF16, stack=sst)
        ar = kb.ring("sa", 2, [128, TROWS], BF16, stack=sst)
        for j in range(4):
            for i in range(4):
                acc, acc_r = ar.next()
                for qq in range(4):
                    t, t_r = lr.next()
                    dma(S, "sp", t[:, 0:T], yall[(i, qq)][j * 128:(j + 1) * 128, 0:T], writes=[t_r])
                    if qq == 0:
                        S.op("dve", lambda e: e.tensor_scalar(out=acc[:, :T], in0=t[:, :T], scalar1=oh[:, 0:1], scalar2=None, op0=ALU.mult), reads=[t_r, oh_r], writes=[acc_r])
                    else:
                        S.op("dve", lambda e: e.scalar_tensor_tensor(out=acc[:, :T], in0=t[:, :T], scalar=oh[:, qq:qq + 1], in1=acc[:, :T], op0=ALU.mult, op1=ALU.add),
                             reads=[t_r, oh_r, acc_r], writes=[acc_r])
                dma(S, "sp", ys_own[:, 4 * i + j, 0:T], acc[:, :T], reads=[acc_r])
        s4, s4_r = kb.sb("s4", [4, 4, TROWS], F32, stack=sst)
        s4a, s4a_r = kb.sb("s4a", [4, TROWS], F32, stack=sst)
        oh4, oh4_r = kb.sb("oh4", [4, 4], F32, stack=sst)
        dma(S, "sp", oh4[:], oneh.partition_broadcast(4), writes=[oh4_r])
        sflat = ssq_all.rearrange("(j c) p -> j (c p)", j=4)
        for qq in range(4):
            dma(S, "sp", s4[:, qq, 0:2048], sflat[:, qq * 2048:(qq + 1) * 2048], writes=[s4_r])
            dma(S, "sp", s4[:, qq, 2048:TROWS], sflat[:, NLAT + qq * 64:NLAT + (qq + 1) * 64], writes=[s4_r])
        S.op("dve", lambda e: e.tensor_scalar(out=s4a[:], in0=s4[:, 0, :], scalar1=oh4[:, 0:1], scalar2=None, op0=ALU.mult), reads=[s4_r, oh4_r], writes=[s4a_r])
        for qq in range(1, 4):
            S.op("dve", lambda e: e.scalar_tensor_tensor(out=s4a[:], in0=s4[:, qq, :], scalar=oh4[:, qq:qq + 1], in1=s4a[:], op0=ALU.mult, op1=ALU.add),
                 reads=[s4_r, oh4_r, s4a_r], writes=[s4a_r])
        dma(S, "sp", ssq_sel, s4a[:], reads=[s4a_r])
        S.barrier()
        sst.close()

        def fill_vt(vt, vt_r):
            S.op("dve", lambda e: e.tensor_copy(out=vt[:, :, 0:3], in_=gv[:, l, :, 1:4]), reads=[gv_r], writes=[vt_r])
            for z in range(2):
                for s_ in range(4):
                    S.op("dve", lambda e: e.tensor_copy(out=vt[:, :, 3 + 4 * z + s_], in_=cvm[:, l, 2 + s_, :, z]), reads=[cvm_r], writes=[vt_r])

        emit_g(kb, T, hT_own, ys_own, ssq_sel, xT, fill_vt, WB[l]["wg"], WB[l]["wb"], WB[l]["wo"], WB[l]["w1"], WB[l]["w2"], xT, False)
    emit_xT2out(kb, xT, out, cst, cst_r)
    return kb.finish()


def kernel(**inputs):
    inp = {k: np.ascontiguousarray(np.asarray(v)) for k, v in inputs.items()}
    nc = build_fused()
    cosT, sinT = rope_tables()
    consts = ssd_consts()
    cmask = swa_masks()
    gvec = np.stack([inp["g_pre_mix"], inp["g_post_mix"], inp["g_pre_mlp"], inp["g_post_mlp"]], -1)
    gvec = np.ascontiguousarray(gvec.reshape(DEPTH, 16, 128, 4).transpose(2, 0, 1, 3))
    bmT = np.ascontiguousarray(inp["b_mod"].reshape(DEPTH, 96, 128).transpose(0, 2, 1))
    per_layer = []
    for l in range(DEPTH):
        per_layer.append({
            "wg": np.ascontiguousarray(inp["w_in"][l][:, GATE_OFF:]),
            "wb": np.ascontiguousarray(inp["w_branch"][l].reshape(4 * 512, D)),
            "lamv": np.ascontiguousarray(np.stack([inp["lam_q1"][l], inp["lam_k1"][l], inp["lam_q2"][l], inp["lam_k2"][l]], 0)),
        })
    maps = []
    for core in range(8):
        b, q = divmod(core, 4)
        j = q
        cvec = np.stack([inp["c"][b], inp["c_ctx"]], 1)
        m = {
            "xrows": token_shard(inp["x"], inp["ctx"], b, q),
            "cT2": np.ascontiguousarray(cvec.reshape(16, 128, 2).transpose(1, 0, 2)),
            "wmod": np.ascontiguousarray(inp["w_mod"][:, :, q * 3072:(q + 1) * 3072]), "bmT": np.ascontiguousarray(bmT[:, :, q * 24:(q + 1) * 24]), "gvec": gvec,
            "oneh": np.eye(4, dtype=np.float32)[q].copy(),
            "cosT": cosT, "sinT": sinT, "bconsts": consts, "cmask": cmask,
        }
        fm, tm = x_weight_cols(j)
        for l in range(DEPTH):
            cw, cb, pv, gssm = ssd_params(inp, l, j)
            m.update({
                f"wfm{l}": np.ascontiguousarray(inp["w_in"][l][:, fm]), f"wtm{l}": np.ascontiguousarray(inp["w_in"][l][:, tm]),
                f"lamv{l}": per_layer[l]["lamv"], f"gsub{l}": np.ascontiguousarray(inp["g_subln"][l]),
                f"cw{l}": cw, f"cb{l}": cb, f"pv{l}": pv, f"gssm{l}": gssm,
                f"sinkv{l}": np.ascontiguousarray(inp["sink"][l][2 * j:2 * j + 2]),
                f"biasT{l}": na_bias(inp["rpb"][l], j),
                f"wg{l}": per_layer[l]["wg"], f"wb{l}": per_layer[l]["wb"], f"wo{l}": inp["w_out"][l],
                f"w1{l}": inp["w_ff1"][l], f"w2{l}": inp["w_ff2"][l],
            })
        maps.append(m)
    res = run_spmd(nc, maps)
    outp = np.zeros((2, NLAT, D), np.float32)
    for core in range(8):
        b, q = divmod(core, 4)
        outp[b, q * 2048:(q + 1) * 2048] = res[core]["out"]
    return outp
```
